# Optimizing a Trainium2 kernel written in Bass

```python
import math
import jax, jax.numpy as jnp
from jax import lax
import numpy as np

D_MODEL = 2048
BATCH = 8
SEQ = 4096
DEPTH = 4
DEC_BATCH = 16
DEC_SEQ = 32
PAST_LEN = 1024

CHUNK = 64
N_MIXERS = 3
N_HEADS = 16
HEAD_DIM = D_MODEL // N_HEADS
BAND_CHUNKS = 8
BAND_PAST = BAND_CHUNKS * CHUNK
REL_CLIP_A = 256
DIFF_QK_DIM = HEAD_DIM // 2
DIFF_V_DIM = HEAD_DIM
T5_BUCKETS = 32
T5_MAX_DIST = 128
Q_BLOCK = 128
D_FF = 256 * math.ceil(8 * D_MODEL / (3 * 256))
ALPHA = (2 * DEPTH) ** 0.25
DEEPNORM_BETA = (8 * DEPTH) ** -0.25
LN_EPS = 1e-5
RMS_EPS = 1e-5
NEG_INF = -1e30
N_LAYERS_A = len(range(0, DEPTH, N_MIXERS))
N_LAYERS_B = len(range(1, DEPTH, N_MIXERS))
N_LAYERS_C = len(range(2, DEPTH, N_MIXERS))

kernel_name = 'hybrid_band_diff_stickbreak_stream_step'


def _layer_norm(x, g, b):
    xf = x.astype(jnp.float32)
    xc = xf - jnp.mean(xf, axis=-1, keepdims=True)
    var = jnp.mean(xc * xc, axis=-1, keepdims=True)
    return (xc * lax.rsqrt(var + LN_EPS) * g.astype(jnp.float32) + b.astype(jnp.float32)).astype(x.dtype)


def _swiglu(x, w_gate, w_up, w_down):
    h = jax.nn.silu(jnp.einsum('btd,df->btf', x, w_gate)) * jnp.einsum('btd,df->btf', x, w_up)
    return jnp.einsum('btf,fd->btd', h, w_down)


def _qkv(x, w_in):
    b, t, _ = x.shape
    h = jnp.einsum('btd,de->bte', x, w_in).reshape(b, t, 3, N_HEADS, HEAD_DIM)
    return h[:, :, 0], h[:, :, 1], h[:, :, 2]


def _merge_heads(o, w_out):
    b, t = o.shape[:2]
    return jnp.einsum('bte,ed->btd', o.reshape(b, t, -1), w_out)


def _masked_softmax(s, valid):
    return jax.nn.softmax(jnp.where(valid, s, NEG_INF), axis=-1)


def _chunk_causal(qpos, kpos):
    return (kpos[None, :] // CHUNK) <= (qpos[:, None] // CHUNK)


def _clipped_rel_bias(table, qpos, kpos):
    idx = jnp.clip(kpos[None, :] - qpos[:, None], -REL_CLIP_A, REL_CLIP_A) + REL_CLIP_A
    return jnp.transpose(table[idx], (2, 0, 1)).astype(jnp.float32)


def _softmax_attend(q, k, v, bias, valid):
    s = jnp.einsum('bqhd,bkhd->bhqk', q, k, preferred_element_type=jnp.float32) * (HEAD_DIM ** -0.5) + bias
    p = _masked_softmax(s, valid)
    return jnp.einsum('bhqk,bkhd->bqhd', p.astype(v.dtype), v)


def band_attn_prompt(q, k, v, table):
    b, s = q.shape[:2]
    band = BAND_PAST + CHUNK
    pad = ((0, 0), (BAND_PAST, 0), (0, 0), (0, 0))
    kp = jnp.pad(k, pad)
    vp = jnp.pad(v, pad)
    offs = jnp.arange(band) - BAND_PAST
    bias = _clipped_rel_bias(table, jnp.arange(CHUNK), offs)

    def one_chunk(c):
        start = c * CHUNK
        qc = lax.dynamic_slice_in_dim(q, start, CHUNK, axis=1)
        kb = lax.dynamic_slice_in_dim(kp, start, band, axis=1)
        vb = lax.dynamic_slice_in_dim(vp, start, band, axis=1)
        valid = jnp.broadcast_to((start + offs >= 0)[None, :], (CHUNK, band))
        return _softmax_attend(qc, kb, vb, bias, valid)

    out = lax.map(one_chunk, jnp.arange(s // CHUNK))
    return jnp.moveaxis(out, 0, 1).reshape(b, s, N_HEADS, HEAD_DIM)


def band_attn_sample(q, k_new, v_new, k_cache, v_cache, table):
    r, t = k_cache.shape[1], q.shape[1]
    k = jnp.concatenate([k_cache, k_new], axis=1)
    v = jnp.concatenate([v_cache, v_new], axis=1)
    qpos = PAST_LEN + jnp.arange(t)
    kpos = PAST_LEN - r + jnp.arange(r + t)
    qc = qpos[:, None] // CHUNK
    kc = kpos[None, :] // CHUNK
    valid = (kc <= qc) & (kc >= qc - BAND_CHUNKS)
    return _softmax_attend(q, k, v, _clipped_rel_bias(table, qpos, kpos), valid)


def _t5_bias(table, qpos, kpos):
    rel = kpos[None, :] - qpos[:, None]
    half = T5_BUCKETS // 2
    max_exact = half // 2
    n = jnp.abs(rel)
    nf = jnp.maximum(n, 1).astype(jnp.float32)
    large = max_exact + (jnp.log(nf / max_exact) / math.log(T5_MAX_DIST / max_exact) * (half - max_exact)).astype(jnp.int32)
    bucket = jnp.where(rel > 0, half, 0) + jnp.where(n < max_exact, n, jnp.minimum(large, half - 1))
    return jnp.transpose(table[bucket], (2, 0, 1)).astype(jnp.float32)


def _diff_lambda(lq1, lk1, lq2, lk2, lam_init):
    f = jnp.float32
    return (jnp.exp(jnp.sum(lq1.astype(f) * lk1.astype(f)))
            - jnp.exp(jnp.sum(lq2.astype(f) * lk2.astype(f))) + lam_init)


def _diff_attend(q, k, v, bias, valid, lam):
    b, tq = q.shape[:2]
    tk = k.shape[1]
    q2 = q.reshape(b, tq, N_HEADS, 2, DIFF_QK_DIM)
    k2 = k.reshape(b, tk, N_HEADS, 2, DIFF_QK_DIM)
    s = jnp.einsum('bqhcd,bkhcd->bchqk', q2, k2, preferred_element_type=jnp.float32) * (DIFF_QK_DIM ** -0.5) + bias
    p = _masked_softmax(s, valid)
    w = p[:, 0] - lam * p[:, 1]
    return jnp.einsum('bhqk,bkhd->bqhd', w.astype(v.dtype), v)


def _head_rmsnorm(o, g, lam_init):
    of = o.astype(jnp.float32)
    of = of * lax.rsqrt(jnp.mean(of * of, axis=-1, keepdims=True) + RMS_EPS) * g.astype(jnp.float32)
    return (of * (1.0 - lam_init)).astype(o.dtype)


def diff_attn_prompt(q, k, v, t5_table, lam):
    b, s = q.shape[:2]
    kpos = jnp.arange(s)

    def one_block(blk):
        q0 = blk * Q_BLOCK
        qb = lax.dynamic_slice_in_dim(q, q0, Q_BLOCK, axis=1)
        qpos = q0 + jnp.arange(Q_BLOCK)
        return _diff_attend(qb, k, v, _t5_bias(t5_table, qpos, kpos), _chunk_causal(qpos, kpos), lam)

    out = lax.map(one_block, jnp.arange(s // Q_BLOCK))
    return jnp.moveaxis(out, 0, 1).reshape(b, s, N_HEADS, DIFF_V_DIM)


def diff_attn_sample(q, k_new, v_new, k_cache, v_cache, t5_table, lam):
    r, t = k_cache.shape[1], q.shape[1]
    k = jnp.concatenate([k_cache, k_new], axis=1)
    v = jnp.concatenate([v_cache, v_new], axis=1)
    qpos = r + jnp.arange(t)
    kpos = jnp.arange(r + t)
    return _diff_attend(q, k, v, _t5_bias(t5_table, qpos, kpos), _chunk_causal(qpos, kpos), lam)


def _stick_breaking(q, k, v, valid):
    z = jnp.einsum('bqhd,bkhd->bhqk', q, k, preferred_element_type=jnp.float32) * (HEAD_DIM ** -0.5)
    log_beta = jax.nn.log_sigmoid(z)
    log_1m = jnp.where(valid, jax.nn.log_sigmoid(-z), 0.0)
    log_a = log_beta + lax.cumsum(log_1m, axis=3, reverse=True) - log_1m
    a = jnp.where(valid, jnp.exp(log_a), 0.0)
    return jnp.einsum('bhqk,bkhd->bqhd', a.astype(v.dtype), v)


def stick_prompt(q, k, v):
    b, s = q.shape[:2]
    kpos = jnp.arange(s)

    def one_block(blk):
        q0 = blk * Q_BLOCK
        qb = lax.dynamic_slice_in_dim(q, q0, Q_BLOCK, axis=1)
        qpos = q0 + jnp.arange(Q_BLOCK)
        return _stick_breaking(qb, k, v, kpos[None, :] < qpos[:, None])

    out = lax.map(one_block, jnp.arange(s // Q_BLOCK))
    return jnp.moveaxis(out, 0, 1).reshape(b, s, N_HEADS, HEAD_DIM)


def stick_sample(q, k_new, v_new, k_cache, v_cache):
    r, t = k_cache.shape[1], q.shape[1]
    k = jnp.concatenate([k_cache, k_new], axis=1)
    v = jnp.concatenate([v_cache, v_new], axis=1)
    qpos = r + jnp.arange(t)
    kpos = jnp.arange(r + t)
    return _stick_breaking(q, k, v, kpos[None, :] < qpos[:, None])


def setup_inputs(seed: int = 0) -> dict:
    key = jax.random.key(seed)
    k = jax.random.split(key, 28)
    f32 = jnp.float32

    def nrm(i, shape, scale):
        return jax.random.normal(k[i], shape, f32) * scale

    band_rows = min(BAND_PAST, PAST_LEN)
    d_sc = D_MODEL ** -0.5
    return {
        'x_prompt': nrm(0, (BATCH, SEQ, D_MODEL), 1.0),
        'x_sample': nrm(1, (DEC_BATCH, DEC_SEQ, D_MODEL), 1.0),
        'cache_a_k': nrm(2, (N_LAYERS_A, DEC_BATCH, band_rows, N_HEADS, HEAD_DIM), 1.0),
        'cache_a_v': nrm(3, (N_LAYERS_A, DEC_BATCH, band_rows, N_HEADS, HEAD_DIM), 1.0),
        'cache_b_k': nrm(4, (N_LAYERS_B, DEC_BATCH, PAST_LEN, N_HEADS, 2 * DIFF_QK_DIM), 1.0),
        'cache_b_v': nrm(5, (N_LAYERS_B, DEC_BATCH, PAST_LEN, N_HEADS, DIFF_V_DIM), 1.0),
        'cache_c_k': nrm(6, (N_LAYERS_C, DEC_BATCH, PAST_LEN, N_HEADS, HEAD_DIM), 1.0),
        'cache_c_v': nrm(7, (N_LAYERS_C, DEC_BATCH, PAST_LEN, N_HEADS, HEAD_DIM), 1.0),
        'w_in_a': nrm(8, (N_LAYERS_A, D_MODEL, 3 * D_MODEL), d_sc),
        'w_out_a': nrm(9, (N_LAYERS_A, D_MODEL, D_MODEL), d_sc * DEEPNORM_BETA),
        'rel_bias_a': nrm(10, (N_LAYERS_A, 2 * REL_CLIP_A + 1, N_HEADS), 0.1),
        'w_in_b': nrm(11, (N_LAYERS_B, D_MODEL, 3 * D_MODEL), d_sc),
        'w_out_b': nrm(12, (N_LAYERS_B, D_MODEL, D_MODEL), d_sc * DEEPNORM_BETA),
        'lambda_q1': nrm(13, (N_LAYERS_B, DIFF_QK_DIM), 0.1),
        'lambda_k1': nrm(14, (N_LAYERS_B, DIFF_QK_DIM), 0.1),
        'lambda_q2': nrm(15, (N_LAYERS_B, DIFF_QK_DIM), 0.1),
        'lambda_k2': nrm(16, (N_LAYERS_B, DIFF_QK_DIM), 0.1),
        'diff_norm_g': 1.0 + nrm(17, (N_LAYERS_B, DIFF_V_DIM), 0.02),
        't5_bias': nrm(18, (T5_BUCKETS, N_HEADS), 0.1),
        'w_in_c': nrm(19, (N_LAYERS_C, D_MODEL, 3 * D_MODEL), d_sc),
        'w_out_c': nrm(20, (N_LAYERS_C, D_MODEL, D_MODEL), d_sc * DEEPNORM_BETA),
        'ln1_g': 1.0 + nrm(21, (DEPTH, D_MODEL), 0.02),
        'ln1_b': nrm(22, (DEPTH, D_MODEL), 0.02),
        'ln2_g': 1.0 + nrm(23, (DEPTH, D_MODEL), 0.02),
        'ln2_b': nrm(24, (DEPTH, D_MODEL), 0.02),
        'w_gate': nrm(25, (DEPTH, D_MODEL, D_FF), d_sc),
        'w_up': nrm(26, (DEPTH, D_MODEL, D_FF), d_sc),
        'w_down': nrm(27, (DEPTH, D_FF, D_MODEL), (D_FF ** -0.5) * DEEPNORM_BETA),
    }


def reference(x_prompt, x_sample, cache_a_k, cache_a_v, cache_b_k, cache_b_v, cache_c_k, cache_c_v,
              w_in_a, w_out_a, rel_bias_a, w_in_b, w_out_b, lambda_q1, lambda_k1, lambda_q2, lambda_k2,
              diff_norm_g, t5_bias, w_in_c, w_out_c, ln1_g, ln1_b, ln2_g, ln2_b, w_gate, w_up, w_down):
    yp, ys = x_prompt, x_sample
    pa_k, pa_v, sa_k, sa_v = [], [], [], []
    pb_k, pb_v, sb_k, sb_v = [], [], [], []
    pc_k, pc_v, sc_k, sc_v = [], [], [], []
    for i in range(DEPTH):
        kind = i % N_MIXERS
        j = i // N_MIXERS
        if kind == 0:
            qp, kp, vp = _qkv(yp, w_in_a[j])
            qs, ks, vs = _qkv(ys, w_in_a[j])
            op = band_attn_prompt(qp, kp, vp, rel_bias_a[j])
            os_ = band_attn_sample(qs, ks, vs, cache_a_k[j], cache_a_v[j], rel_bias_a[j])
            keep = min(BAND_PAST, kp.shape[1])
            pa_k.append(kp[:, -keep:])
            pa_v.append(vp[:, -keep:])
            sa_k.append(ks)
            sa_v.append(vs)
            w_out = w_out_a[j]
        elif kind == 1:
            qp, kp, vp = _qkv(yp, w_in_b[j])
            qs, ks, vs = _qkv(ys, w_in_b[j])
            lam_init = 0.8 - 0.6 * math.exp(-0.3 * i)
            lam = _diff_lambda(lambda_q1[j], lambda_k1[j], lambda_q2[j], lambda_k2[j], lam_init)
            op = _head_rmsnorm(diff_attn_prompt(qp, kp, vp, t5_bias, lam), diff_norm_g[j], lam_init)
            os_ = _head_rmsnorm(diff_attn_sample(qs, ks, vs, cache_b_k[j], cache_b_v[j], t5_bias, lam),
                                diff_norm_g[j], lam_init)
            pb_k.append(kp)
            pb_v.append(vp)
            sb_k.append(ks)
            sb_v.append(vs)
            w_out = w_out_b[j]
        else:
            qp, kp, vp = _qkv(yp, w_in_c[j])
            qs, ks, vs = _qkv(ys, w_in_c[j])
            op = stick_prompt(qp, kp, vp)
            os_ = stick_sample(qs, ks, vs, cache_c_k[j], cache_c_v[j])
            pc_k.append(kp)
            pc_v.append(vp)
            sc_k.append(ks)
            sc_v.append(vs)
            w_out = w_out_c[j]
        yp = _layer_norm(ALPHA * yp + _merge_heads(op, w_out), ln1_g[i], ln1_b[i])
        ys = _layer_norm(ALPHA * ys + _merge_heads(os_, w_out), ln1_g[i], ln1_b[i])
        yp = _layer_norm(ALPHA * yp + _swiglu(yp, w_gate[i], w_up[i], w_down[i]), ln2_g[i], ln2_b[i])
        ys = _layer_norm(ALPHA * ys + _swiglu(ys, w_gate[i], w_up[i], w_down[i]), ln2_g[i], ln2_b[i])
    new_a_k_prompt = jnp.stack(pa_k)
    new_a_v_prompt = jnp.stack(pa_v)
    new_b_k_prompt = jnp.stack(pb_k)
    new_b_v_prompt = jnp.stack(pb_v)
    new_c_k_prompt = jnp.stack(pc_k)
    new_c_v_prompt = jnp.stack(pc_v)
    new_a_k_sample = jnp.stack(sa_k)
    new_a_v_sample = jnp.stack(sa_v)
    new_b_k_sample = jnp.stack(sb_k)
    new_b_v_sample = jnp.stack(sb_v)
    new_c_k_sample = jnp.stack(sc_k)
    new_c_v_sample = jnp.stack(sc_v)
    return (yp, ys, new_a_k_prompt, new_a_v_prompt, new_b_k_prompt, new_b_v_prompt, new_c_k_prompt, new_c_v_prompt,
            new_a_k_sample, new_a_v_sample, new_b_k_sample, new_b_v_sample, new_c_k_sample, new_c_v_sample)
```

```python
import contextlib
import math
import numpy as np
import concourse.bass as bass
import concourse.mybir as mybir
from concourse.bass_utils import run_bass_kernel_spmd

F32 = mybir.dt.float32
BF16 = mybir.dt.bfloat16
AF = mybir.ActivationFunctionType
ALU = mybir.AluOpType

D = 2048
NH = 16
HD = 128
SEQ = 4096
NS = 64
NTOK = SEQ + NS
DFF = 5632
NFC = DFF // 128
DEPTH = 4
ALPHA = (2 * DEPTH) ** 0.25
LN_EPS = 1e-5
RMS_EPS = 1e-5
PAST = 1024
NEG = -30000.0
KINDS = [0, 1, 2, 0]
LJ = [0, 0, 0, 1]

LAYER_SEL = (0, 1, 2, 3)
N_CORES = 8
WCH = 2
DBG_NO_F32OUT = False
TILE_SEL = None
PHASES = (0, 1, 2, 3)


class Tok:
    __slots__ = ("w", "r", "sem", "cnt")

    def __init__(self):
        self.w = None
        self.r = {}
        self.sem = None
        self.cnt = 0


class Sched:
    def __init__(self, nc, stack):
        self.nc = nc
        self.stack = stack
        self.eng = {"pe": nc.tensor, "act": nc.scalar, "dve": nc.vector, "pool": nc.gpsimd, "sp": nc.sync}
        self.semobj = {}
        self.cnt = {}
        self.known = {}
        for e in self.eng:
            self.semobj[e] = stack.enter_context(nc.semaphore("sem_" + e))
            self.cnt[e] = 0
            self.known[e] = {}
        self.ndsem = 0
        self.dma_marks = {}

    def _wait(self, e, deps):
        kn = self.known[e]
        for sid, val in deps.items():
            if kn.get(sid, 0) < val:
                self.eng[e].wait_ge(self.semobj[sid], val)
                kn[sid] = val

    def _deps(self, e, reads, writes):
        deps = {}

        def add(mark, raw):
            sid, val = mark
            if sid == e and (e == "pe" or not raw):
                return
            if deps.get(sid, 0) < val:
                deps[sid] = val
        for t in reads:
            if t.w is not None:
                add(t.w, True)
        for t in writes:
            if t.w is not None:
                add(t.w, False)
            for k, v in t.r.items():
                add((k, v), False)
        return deps

    def op(self, e, fn, reads=(), writes=()):
        self._wait(e, self._deps(e, reads, writes))
        inst = fn(self.eng[e])
        self.cnt[e] += 1
        inst.then_inc(self.semobj[e], 1)
        c = self.cnt[e]
        for t in reads:
            t.r[e] = c
        for t in writes:
            t.w = (e, c)
            t.r = {}
        return inst

    def dma(self, q, out, in_, tok, is_write, extra_reads=(), extra_writes=()):
        if tok.sem is None:
            sid = "d%d" % self.ndsem
            self.ndsem += 1
            assert self.ndsem < 90, "too many dma sems"
            self.semobj[sid] = self.stack.enter_context(self.nc.semaphore("sem_" + sid))
            tok.sem = sid
        reads = list(extra_reads) + ([] if is_write else [tok])
        writes = list(extra_writes) + ([tok] if is_write else [])
        self._wait(q, self._deps(q, reads, writes))
        inst = self.eng[q].dma_start(out=out, in_=in_)
        tok.cnt += 16
        inst.then_inc(self.semobj[tok.sem], 16)
        mark = (tok.sem, tok.cnt)
        self.dma_marks[tok.sem] = tok.cnt
        for t in reads:
            t.r[tok.sem] = tok.cnt
        for t in writes:
            t.w = mark
            t.r = {}
        return inst

    def barrier(self):
        deps = {e: self.cnt[e] for e in self.eng if self.cnt[e] > 0}
        deps.update(self.dma_marks)
        for e in self.eng:
            d = {k: v for k, v in deps.items() if k != e}
            self._wait(e, d)

    def final_wait(self):
        self.barrier()


def build_program():
    nc = bass.Bass("TRN2", target_bir_lowering=False)
    stack = contextlib.ExitStack()
    with stack:
        _emit(nc, stack)
    return nc


def _emit(nc, stack):
    S = Sched(nc, stack)

    def din(name, shape, dt=F32):
        return nc.dram_tensor(name, list(shape), dt, kind="ExternalInput").ap()

    def dout(name, shape, dt=F32):
        return nc.dram_tensor(name, list(shape), dt, kind="ExternalOutput").ap()

    def dscr(name, shape, dt):
        return nc.dram_tensor(name, list(shape), dt, kind="Internal").ap()

    def sb(name, shape, dt):
        return stack.enter_context(nc.sbuf_tensor("s_" + name, list(shape), dt))

    def ps(name, shape, dt):
        return stack.enter_context(nc.psum_tensor("p_" + name, list(shape), dt))

    xp = din("xp", [SEQ, D])
    xs = din("xs", [NS, D])
    ca_k = din("ca_k", [2, 2, 512, D])
    ca_v = din("ca_v", [2, 2, 512, D])
    cb_k = din("cb_k", [1, 2, PAST, D])
    cb_v = din("cb_v", [1, 2, PAST, D])
    cc_k = din("cc_k", [1, 2, PAST, D])
    cc_v = din("cc_v", [1, 2, PAST, D])
    w_in = [din("w_in_L%d" % l, [D, 3 * D]) if l in LAYER_SEL else None for l in range(DEPTH)]
    w_out = [din("w_out_L%d" % l, [D, D]) if l in LAYER_SEL else None for l in range(DEPTH)]
    w_gate = [din("w_gate_L%d" % l, [D, DFF]) if l in LAYER_SEL else None for l in range(DEPTH)]
    w_up = [din("w_up_L%d" % l, [D, DFF]) if l in LAYER_SEL else None for l in range(DEPTH)]
    w_down = [din("w_down_L%d" % l, [DFF, D]) if l in LAYER_SEL else None for l in range(DEPTH)]
    ln1_g = din("ln1_g", [DEPTH, D])
    ln1_b = din("ln1_b", [DEPTH, D])
    ln2_g = din("ln2_g", [DEPTH, D])
    ln2_b = din("ln2_b", [DEPTH, D])
    ident_d = din("ident", [128, 128])
    umat_d = din("umat", [128, 128])
    ones_d = din("ones", [128, 128])
    biasA_d = din("biasA", [2, NH, 128, 8, 512])
    biasAs_d = din("biasAs", [2, NH, 128, 5, 32])
    biasB_d = din("biasB", [NH, 128, 6, 512])
    biasBs_d = din("biasBs", [NH, 128, 9, 32])
    cmask_d = din("cmask", [128, 8, 512])
    cmasks_d = din("cmasks", [32, 2, 32])
    lam_d = din("lam", [4, 64])
    dng_d = din("dng", [128, 1])

    yp = dout("yp", [SEQ, D])
    ys = dout("ys", [NS, D])
    pa_k = dout("pa_k", [2, 512, D])
    pa_v = dout("pa_v", [2, 512, D])
    pb_k = dout("pb_k", [1, SEQ, D])
    pb_v = dout("pb_v", [1, SEQ, D])
    pc_k = dout("pc_k", [1, SEQ, D])
    pc_v = dout("pc_v", [1, SEQ, D])
    sa_k = dout("sa_k", [2, NS, D])
    sa_v = dout("sa_v", [2, NS, D])
    sb_k = dout("sb_k", [1, NS, D])
    sb_v = dout("sb_v", [1, NS, D])
    sc_k = dout("sc_k", [1, NS, D])
    sc_v = dout("sc_v", [1, NS, D])
    pko = [pa_k, pb_k, pc_k]
    pvo = [pa_v, pb_v, pc_v]
    sko = [sa_k, sb_k, sc_k]
    svo = [sa_v, sb_v, sc_v]

    XT = dscr("XT", [16, 128, NTOK], BF16)
    QT = dscr("QT", [NH, 128, NTOK], BF16)
    KT = dscr("KT", [NH, 128, NTOK], BF16)
    VB = dscr("VB", [NTOK, D], BF16)
    OT = dscr("OT", [NH, 128, NTOK], BF16)

    ident = sb("ident", [128, 128], BF16)
    t_ident = Tok()
    NW = 3
    wslot = [sb("wslot%d" % i, [128, 16, 512], BF16) for i in range(NW)]
    t_w = [Tok() for _ in range(NW)]
    actT = sb("actT", [128, 16, 512], BF16)
    t_actT = Tok()
    xres = sb("xres", [128, 4, D], F32)
    t_xres = [Tok() for _ in range(4)]
    t_hT = [Tok() for _ in range(NFC)]
    t_gb = Tok()
    H = {}
    NST = 4
    st_f = [sb("stf%d" % i, [128, 512], F32) for i in range(NST)]
    t_stf = [Tok() for _ in range(NST)]
    st_b = [sb("stb%d" % i, [128, 512], BF16) for i in range(NST)]
    t_stb = [Tok() for _ in range(NST)]
    xb16 = sb("xb16", [128, D], BF16)
    t_xb16 = Tok()
    stats = sb("stats", [128, 4, 6], F32)
    mv = sb("mv", [128, 2], F32)
    rstd = sb("rstd", [128, 1], F32)
    t_stats = Tok()

    psf = ps("psf", [128, 6, 512], F32)
    t_psf = [Tok() for _ in range(6)]
    pst = ps("pst", [128, 2, 1024], BF16)
    t_pst = [Tok() for _ in range(2)]

    rr = {"ps": 0, "pst": 0, "stf": 0, "stb": 0, "w": 0, "ev": 0}

    def nxt(key, n):
        v = rr[key]
        rr[key] = (v + 1) % n
        return v

    def evac_engine():
        return "act" if nxt("ev", 2) == 0 else "dve"

    def copy_op(e, out, in_, reads, writes):
        if e == "act":
            S.op("act", lambda g: g.activation(out=out, in_=in_, func=AF.Identity), reads, writes)
        else:
            S.op(e, lambda g: g.tensor_copy(out=out, in_=in_), reads, writes)

    S.dma("pool", ident[:], ident_d[:, :], t_ident, True)

    TILES = [(i * 512, 512) for i in range(SEQ // 512)] + [(SEQ, NS)]
    if TILE_SEL is not None:
        TILES = [TILES[i] for i in TILE_SEL]

    def blocks_of(T):
        return [(b * 128, 128) for b in range(T // 128)] if T >= 128 else [(0, T)]

    def tokmajor_dram(tok0, n, prompt_ap, sample_ap):
        if tok0 < SEQ:
            return prompt_ap[tok0:tok0 + n, :]
        return sample_ap[tok0 - SEQ:tok0 - SEQ + n, :]

    def load_wgroup(src_ap_rows_cols, nk):
        i = nxt("w", NW)
        src = src_ap_rows_cols.rearrange("(k p) c -> p k c", p=128)
        k0 = 0
        while k0 < nk:
            k1 = min(nk, k0 + WCH)
            S.dma("pool", wslot[i][:, k0:k1, :], src[:, k0:k1, :], t_w[i], True)
            k0 = k1
        return i

    def transpose_to_actT(src_bf16, t_src, bsz, boff):
        for kc4 in range(4):
            pb = nxt("pst", 2)
            for j in range(4):
                kc = kc4 * 4 + j
                S.op("pe", lambda g, kc=kc, j=j, pb=pb: g.transpose(
                    out=pst[:, pb, j * 128:j * 128 + bsz], in_=src_bf16[0:bsz, kc * 128:(kc + 1) * 128],
                    identity=ident[0:bsz, 0:bsz]),
                    reads=[t_src, t_ident], writes=[t_pst[pb]])
            e = evac_engine()
            copy_op(e, actT[:, kc4 * 4:kc4 * 4 + 4, boff:boff + bsz],
                    pst[:, pb, 0:512].rearrange("p (j t) -> p j t", j=4)[:, :, 0:bsz],
                    [t_pst[pb]], [t_actT])

    def store_actT_to_XT(tok0, T):
        S.dma("sp", XT[:, :, tok0:tok0 + T].rearrange("k p t -> p k t"), actT[:, :, 0:T], t_actT, False)

    def phase0():
        for (tok0, T) in TILES:
            for bi, (boff, bsz) in enumerate(blocks_of(T)):
                src = tokmajor_dram(tok0 + boff, bsz, xp, xs)
                S.dma("sp", xres[0:bsz, bi, :], src, t_xres[bi], True)
                e = evac_engine()
                copy_op(e, xb16[0:bsz, :], xres[0:bsz, bi, :], [t_xres[bi]], [t_xb16])
                transpose_to_actT(xb16, t_xb16, bsz, boff)
            store_actT_to_XT(tok0, T)

    def phase1(li):
        kind, j = KINDS[li], LJ[li]
        win = w_in[li]
        for (tok0, T) in TILES:
            is_sample = tok0 >= SEQ
            S.dma("sp", actT[:, :, 0:T], XT[:, :, tok0:tok0 + T].rearrange("k p t -> p k t"), t_actT, True)
            if is_sample:
                kdst = sko[kind][j]
                vdst = svo[kind][j]
                row0 = 0
                want_k_tm = True
            elif kind == 0:
                want_k_tm = (tok0 == SEQ - 512)
                kdst = pko[0][j]
                vdst = pvo[0][j]
                row0 = 0
            else:
                want_k_tm = True
                kdst = pko[kind][0]
                vdst = pvo[kind][0]
                row0 = tok0
            want_vo = want_k_tm
            for g in range(12):
                wi = load_wgroup(win[:, g * 512:(g + 1) * 512], 16)
                sect = g // 4
                if sect < 2:
                    dstT = QT if sect == 0 else KT
                    for hh in range(4):
                        head = (g % 4) * 4 + hh
                        pb = nxt("ps", 6)
                        for kc in range(16):
                            S.op("pe", lambda gg, kc=kc, hh=hh, pb=pb, wi=wi: gg.matmul(
                                psf[:, pb, 0:T], wslot[wi][:, kc, hh * 128:(hh + 1) * 128], actT[:, kc, 0:T],
                                start=(kc == 0), stop=(kc == 15)),
                                reads=[t_w[wi], t_actT], writes=[t_psf[pb]])
                        si = nxt("stb", NST)
                        copy_op(evac_engine(), st_b[si][:, 0:T], psf[:, pb, 0:T], [t_psf[pb]], [t_stb[si]])
                        S.dma("sp", dstT[head, :, tok0:tok0 + T], st_b[si][:, 0:T], t_stb[si], False)
                if (sect == 1 and want_k_tm) or sect == 2:
                    for (boff, bsz) in blocks_of(T):
                        pb = nxt("ps", 6)
                        for kc in range(16):
                            S.op("pe", lambda gg, kc=kc, pb=pb, wi=wi, boff=boff, bsz=bsz: gg.matmul(
                                psf[0:bsz, pb, :], actT[:, kc, boff:boff + bsz], wslot[wi][:, kc, :],
                                start=(kc == 0), stop=(kc == 15)),
                                reads=[t_w[wi], t_actT], writes=[t_psf[pb]])
                        c0 = (g % 4) * 512
                        sf = nxt("stf", NST)
                        copy_op(evac_engine(), st_f[sf][0:bsz, :], psf[0:bsz, pb, :], [t_psf[pb]], [t_stf[sf]])
                        if sect == 2:
                            si = nxt("stb", NST)
                            copy_op(evac_engine(), st_b[si][0:bsz, :], st_f[sf][0:bsz, :], [t_stf[sf]], [t_stb[si]])
                            S.dma("sp", VB[tok0 + boff:tok0 + boff + bsz, c0:c0 + 512], st_b[si][0:bsz, :],
                                  t_stb[si], False)
                        if (sect == 1) or want_vo:
                            dst = kdst if sect == 1 else vdst
                            S.dma("sp", dst[row0 + boff:row0 + boff + bsz, c0:c0 + 512], st_f[sf][0:bsz, :],
                                  t_stf[sf], False)

    def layer_norm_block(bi, bsz, gsel):
        xr = xres[0:bsz, bi, :]
        for c in range(4):
            S.op("dve", lambda g, c=c: g.bn_stats(out=stats[0:bsz, c, :], in_=xres[0:bsz, bi, c * 512:(c + 1) * 512]),
                 reads=[t_xres[bi]], writes=[t_stats])
        S.op("dve", lambda g: g.bn_aggr(out=mv[0:bsz, :], in_=stats[0:bsz, :, :].rearrange("p a b -> p (a b)")),
             reads=[t_stats], writes=[t_stats])
        S.op("dve", lambda g: g.tensor_scalar(out=rstd[0:bsz, :], in0=mv[0:bsz, 1:2], scalar1=LN_EPS, scalar2=None,
                                              op0=ALU.add),
             reads=[t_stats], writes=[t_stats])
        S.op("act", lambda g: g.activation(out=rstd[0:bsz, :], in_=rstd[0:bsz, :], func=AF.Sqrt),
             reads=[t_stats], writes=[t_stats])
        S.op("dve", lambda g: g.reciprocal(out=rstd[0:bsz, :], in_=rstd[0:bsz, :]),
             reads=[t_stats], writes=[t_stats])
        S.op("dve", lambda g: g.tensor_scalar(out=xr, in0=xr, scalar1=mv[0:bsz, 0:1], scalar2=rstd[0:bsz, 0:1],
                                              op0=ALU.subtract, op1=ALU.mult),
             reads=[t_xres[bi], t_stats], writes=[t_xres[bi]])
        S.op("dve", lambda g: g.tensor_tensor(out=xr, in0=xr, in1=H['gb'][0:bsz, 0, :], op=ALU.mult),
             reads=[t_xres[bi], t_gb], writes=[t_xres[bi]])
        S.op("dve", lambda g: g.tensor_tensor(out=xr, in0=xr, in1=H['gb'][0:bsz, 1, :], op=ALU.add),
             reads=[t_xres[bi], t_gb], writes=[t_xres[bi]])
        S.op("act", lambda g: g.copy(out=xb16[0:bsz, :], in_=xr), reads=[t_xres[bi]], writes=[t_xb16])

    def load_gb(gvec, bvec):
        S.dma("sp", H["gb"][:, 0, :], gvec.partition_broadcast(128), t_gb, True)
        S.dma("sp", H["gb"][:, 1, :], bvec.partition_broadcast(128), t_gb, True)

    def phase3(li, last):
        kind, j = KINDS[li], LJ[li]
        wo = w_out[li]
        for (tok0, T) in TILES:
            blks = blocks_of(T)
            rsrc_p, rsrc_s = (xp, xs) if li == 0 else (yp, ys)
            for bi, (boff, bsz) in enumerate(blks):
                S.dma("sp", xres[0:bsz, bi, :], tokmajor_dram(tok0 + boff, bsz, rsrc_p, rsrc_s), t_xres[bi], True)
            S.dma("sp", actT[:, :, 0:T], OT[:, :, tok0:tok0 + T].rearrange("k p t -> p k t"), t_actT, True)
            load_gb(ln1_g[li], ln1_b[li])
            for g in range(4):
                wi = load_wgroup(wo[:, g * 512:(g + 1) * 512], 16)
                for bi, (boff, bsz) in enumerate(blks):
                    pb = nxt("ps", 6)
                    for kc in range(16):
                        S.op("pe", lambda gg, kc=kc, pb=pb, wi=wi, boff=boff, bsz=bsz: gg.matmul(
                            psf[0:bsz, pb, :], actT[:, kc, boff:boff + bsz], wslot[wi][:, kc, :],
                            start=(kc == 0), stop=(kc == 15)),
                            reads=[t_w[wi], t_actT], writes=[t_psf[pb]])
                    xr = xres[0:bsz, bi, g * 512:(g + 1) * 512]
                    S.op("dve", lambda gg, xr=xr, pb=pb, bsz=bsz: gg.scalar_tensor_tensor(
                        out=xr, in0=xr, scalar=ALPHA, in1=psf[0:bsz, pb, :], op0=ALU.mult, op1=ALU.add),
                        reads=[t_psf[pb], t_xres[bi]], writes=[t_xres[bi]])
            for bi, (boff, bsz) in enumerate(blks):
                layer_norm_block(bi, bsz, 0)
                transpose_to_actT(xb16, t_xb16, bsz, boff)
            load_gb(ln2_g[li], ln2_b[li])
            for g in range(DFF // 512):
                wg = load_wgroup(w_gate[li][:, g * 512:(g + 1) * 512], 16)
                wu = load_wgroup(w_up[li][:, g * 512:(g + 1) * 512], 16)
                for cc in range(4):
                    fc = g * 4 + cc
                    pg = nxt("ps", 6)
                    for kc in range(16):
                        S.op("pe", lambda gg, kc=kc, pg=pg, wg=wg, cc=cc: gg.matmul(
                            psf[:, pg, 0:T], wslot[wg][:, kc, cc * 128:(cc + 1) * 128], actT[:, kc, 0:T],
                            start=(kc == 0), stop=(kc == 15)),
                            reads=[t_w[wg], t_actT], writes=[t_psf[pg]])
                    pu = nxt("ps", 6)
                    for kc in range(16):
                        S.op("pe", lambda gg, kc=kc, pu=pu, wu=wu, cc=cc: gg.matmul(
                            psf[:, pu, 0:T], wslot[wu][:, kc, cc * 128:(cc + 1) * 128], actT[:, kc, 0:T],
                            start=(kc == 0), stop=(kc == 15)),
                            reads=[t_w[wu], t_actT], writes=[t_psf[pu]])
                    si = nxt("stf", NST)
                    S.op("act", lambda gg, si=si, pg=pg: gg.activation(out=st_f[si][:, 0:T], in_=psf[:, pg, 0:T],
                                                                      func=AF.Silu),
                         reads=[t_psf[pg]], writes=[t_stf[si]])
                    S.op("dve", lambda gg, si=si, pu=pu, fc=fc: gg.tensor_tensor(
                        out=H['hT'][:, fc, 0:T], in0=st_f[si][:, 0:T], in1=psf[:, pu, 0:T], op=ALU.mult),
                        reads=[t_stf[si], t_psf[pu]], writes=[t_hT[fc]])
            for g in range(4):
                pbs = [nxt("ps", 6) for _ in blks]
                for pc in range(4):
                    wi = load_wgroup(w_down[li][pc * 1408:(pc + 1) * 1408, g * 512:(g + 1) * 512], 11)
                    for bi, (boff, bsz) in enumerate(blks):
                        pb = pbs[bi]
                        for k in range(11):
                            fc = pc * 11 + k
                            S.op("pe", lambda gg, k=k, fc=fc, pb=pb, wi=wi, boff=boff, bsz=bsz: gg.matmul(
                                psf[0:bsz, pb, :], H['hT'][:, fc, boff:boff + bsz], wslot[wi][:, k, :],
                                start=(fc == 0), stop=(fc == NFC - 1)),
                                reads=[t_w[wi], t_hT[fc]], writes=[t_psf[pb]])
                for bi, (boff, bsz) in enumerate(blks):
                    pb = pbs[bi]
                    xr = xres[0:bsz, bi, g * 512:(g + 1) * 512]
                    S.op("dve", lambda gg, xr=xr, pb=pb, bsz=bsz: gg.scalar_tensor_tensor(
                        out=xr, in0=xr, scalar=ALPHA, in1=psf[0:bsz, pb, :], op0=ALU.mult, op1=ALU.add),
                        reads=[t_psf[pb], t_xres[bi]], writes=[t_xres[bi]])
            for bi, (boff, bsz) in enumerate(blks):
                layer_norm_block(bi, bsz, 1)
                S.dma("sp", tokmajor_dram(tok0 + boff, bsz, yp, ys), xres[0:bsz, bi, :], t_xres[bi], False)
                if not last:
                    transpose_to_actT(xb16, t_xb16, bsz, boff)
            if not last:
                store_actT_to_XT(tok0, T)

    t_kT, t_qT, t_vh, t_vs, t_bias, t_biass = Tok(), Tok(), Tok(), Tok(), Tok(), Tok()
    t_kcr, t_kcT, t_vc, t_cb, t_lam, t_cst = Tok(), Tok(), Tok(), Tok(), Tok(), Tok()
    t_wf = [Tok() for _ in range(6)]
    t_wb = [Tok() for _ in range(8)]
    t_ob = [Tok() for _ in range(2)]
    t_o = [Tok() for _ in range(3)]
    rr.update({"wf": 0, "wb": 0, "pss": 0, "ob": 0})
    LAM_INIT = [0.8 - 0.6 * math.exp(-0.3 * i) for i in range(DEPTH)]

    def phase2(li, st):
        kind, j = KINDS[li], LJ[li]

        def a_sb(name, shape, dt):
            return st.enter_context(nc.sbuf_tensor("s_%s_%d" % (name, li), list(shape), dt))
        kT = a_sb("kT", [128, NTOK], BF16)
        qT = a_sb("qT", [128, NTOK], BF16)
        vh = a_sb("vh", [128, 32, 128], BF16)
        vs = a_sb("vs", [32, 2, 128], BF16)
        bias = a_sb("bias", [128, 8, 512], F32)
        biass = a_sb("biass", [128, 9, 32], F32)
        kcr = a_sb("kcr", [128, 8, 128], BF16)
        kcT = a_sb("kcT", [128, 1024], BF16)
        vc = a_sb("vc", [128, 8, 128], BF16)
        cb = a_sb("cb", [128, 512], F32)
        wf = [st_f[i] for i in range(4)]
        wb = [st_b[i] for i in range(4)] + [a_sb("wb%d" % i, [128, 512], BF16) for i in range(2)]
        twf = t_stf
        twb = t_stb + t_wb[0:2]
        ob = [a_sb("ob%d" % i, [128, 512], BF16) for i in range(2)]
        osum = [a_sb("osum%d" % i, [128, 512], F32) for i in range(3)]
        umat = a_sb("umat", [128, 128], BF16)
        ones = a_sb("ones", [128, 128], BF16)
        onesf = a_sb("onesf", [128, 128], F32)
        lam = a_sb("lam", [128, 8], F32)
        lvec = a_sb("lvec", [128, 4, 64], F32)
        gcol = a_sb("gcol", [128, 1], F32)
        S.dma("pool", umat[:], umat_d[:, :], t_cst, True)
        S.dma("pool", ones[:], ones_d[:, :], t_cst, True)
        S.dma("sp", onesf[:], ones_d[:, :], t_cst, True)
        NCB = 4 if kind == 0 else 8
        ck, cv = [(ca_k, ca_v), (cb_k, cb_v), (cc_k, cc_v)][kind]
        scale = (HD ** -0.5) if kind != 1 else (64 ** -0.5)

        if kind == 1:
            for i in range(4):
                S.dma("sp", lvec[:, i, :], lam_d[i].partition_broadcast(128), t_lam, True)
            S.dma("sp", gcol[:, :], dng_d[:, :], t_lam, True)
            S.op("dve", lambda g: g.tensor_tensor(out=lvec[:, 0, :], in0=lvec[:, 0, :], in1=lvec[:, 1, :], op=ALU.mult),
                 [t_lam], [t_lam])
            S.op("dve", lambda g: g.tensor_tensor(out=lvec[:, 2, :], in0=lvec[:, 2, :], in1=lvec[:, 3, :], op=ALU.mult),
                 [t_lam], [t_lam])
            S.op("dve", lambda g: g.reduce_sum(out=lam[:, 0:1], in_=lvec[:, 0, :], axis=mybir.AxisListType.X),
                 [t_lam], [t_lam])
            S.op("dve", lambda g: g.reduce_sum(out=lam[:, 1:2], in_=lvec[:, 2, :], axis=mybir.AxisListType.X),
                 [t_lam], [t_lam])
            S.op("act", lambda g: g.activation(out=lam[:, 0:2], in_=lam[:, 0:2], func=AF.Exp), [t_lam], [t_lam])
            S.op("dve", lambda g: g.tensor_tensor(out=lam[:, 2:3], in0=lam[:, 1:2], in1=lam[:, 0:1], op=ALU.subtract),
                 [t_lam], [t_lam])
            S.op("dve", lambda g: g.tensor_scalar(out=lam[:, 2:3], in0=lam[:, 2:3], scalar1=-LAM_INIT[li], scalar2=None,
                                                  op0=ALU.add), [t_lam], [t_lam])
            S.op("dve", lambda g: g.tensor_scalar(out=gcol[:, :], in0=gcol[:, :], scalar1=1.0 - LAM_INIT[li],
                                                  scalar2=None, op0=ALU.mult), [t_lam], [t_lam])

        def softmax_group(qap, N, blocks, dsl):
            po, pr = 3, 4
            nb = len(blocks)
            for bi, (kap, vap, nk, bap) in enumerate(blocks):
                pss = nxt("pss", 3)
                S.op("pe", lambda g: g.matmul(psf[0:nk, pss, 0:N], kap[dsl, :], qap[dsl, :], start=True, stop=True),
                     [t_kT, t_qT, t_kcT], [t_psf[pss]])
                fi = nxt("wf", 4)
                S.op("dve", lambda g: g.scalar_tensor_tensor(out=wf[fi][0:nk, 0:N], in0=psf[0:nk, pss, 0:N], scalar=scale,
                                                             in1=bap, op0=ALU.mult, op1=ALU.add),
                     [t_psf[pss], t_bias, t_biass], [twf[fi]])
                pi = nxt("wb", 6)
                S.op("act", lambda g: g.activation(out=wb[pi][0:nk, 0:N], in_=wf[fi][0:nk, 0:N], func=AF.Exp),
                     [twf[fi]], [twb[pi]])
                S.op("pe", lambda g: g.matmul(psf[:, po, 0:N], vap, wb[pi][0:nk, 0:N], start=(bi == 0), stop=(bi == nb - 1)),
                     [twb[pi], t_vh, t_vs, t_vc], [t_psf[po]])
                S.op("pe", lambda g: g.matmul(psf[:, pr, 0:N], ones[0:nk, :], wb[pi][0:nk, 0:N], start=(bi == 0),
                                              stop=(bi == nb - 1)),
                     [twb[pi], t_cst], [t_psf[pr]])
            oi = nxt("ob", 3)
            fi = nxt("wf", 4)
            S.op("dve", lambda g: g.reciprocal(out=wf[fi][:, 0:N], in_=psf[:, pr, 0:N]), [t_psf[pr]], [twf[fi]])
            S.op("dve", lambda g: g.tensor_tensor(out=osum[oi][:, 0:N], in0=psf[:, po, 0:N], in1=wf[fi][:, 0:N], op=ALU.mult),
                 [t_psf[po], twf[fi]], [t_o[oi]])
            return oi

        def finish_A(oi, N, dst_tok0, h):
            bi = nxt("ob", 2) if False else (rr.__setitem__("obb", (rr.get("obb", 0) + 1) % 2) or rr["obb"])
            S.op("act", lambda g: g.activation(out=ob[bi][:, 0:N], in_=osum[oi][:, 0:N], func=AF.Identity),
                 [t_o[oi]], [t_ob[bi]])
            S.dma("sp", OT[h, :, dst_tok0:dst_tok0 + N], ob[bi][:, 0:N], t_ob[bi], False)

        def finish_B(o0, o1, N, dst_tok0, h):
            S.op("dve", lambda g: g.scalar_tensor_tensor(out=osum[o0][:, 0:N], in0=osum[o1][:, 0:N], scalar=lam[:, 2:3],
                                                         in1=osum[o0][:, 0:N], op0=ALU.mult, op1=ALU.add),
                 [t_o[o0], t_o[o1], t_lam], [t_o[o0]])
            S.op("dve", lambda g: g.tensor_tensor(out=osum[o1][:, 0:N], in0=osum[o0][:, 0:N], in1=osum[o0][:, 0:N],
                                                  op=ALU.mult), [t_o[o0]], [t_o[o1]])
            S.op("pe", lambda g: g.matmul(psf[:, 5, 0:N], onesf[:, :], osum[o1][:, 0:N], start=True, stop=True),
                 [t_o[o1], t_cst], [t_psf[5]])
            S.op("dve", lambda g: g.tensor_scalar(out=osum[o1][:, 0:N], in0=psf[:, 5, 0:N], scalar1=1.0 / HD,
                                                  scalar2=RMS_EPS, op0=ALU.mult, op1=ALU.add), [t_psf[5]], [t_o[o1]])
            S.op("act", lambda g: g.activation(out=osum[o1][:, 0:N], in_=osum[o1][:, 0:N], func=AF.Sqrt),
                 [t_o[o1]], [t_o[o1]])
            S.op("dve", lambda g: g.reciprocal(out=osum[o1][:, 0:N], in_=osum[o1][:, 0:N]), [t_o[o1]], [t_o[o1]])
            S.op("dve", lambda g: g.tensor_tensor(out=osum[o0][:, 0:N], in0=osum[o0][:, 0:N], in1=osum[o1][:, 0:N],
                                                  op=ALU.mult), [t_o[o0], t_o[o1]], [t_o[o0]])
            bi = (rr.__setitem__("obb", (rr.get("obb", 0) + 1) % 2) or rr["obb"])
            S.op("dve", lambda g: g.tensor_scalar(out=ob[bi][:, 0:N], in0=osum[o0][:, 0:N], scalar1=gcol[:, 0:1],
                                                  scalar2=None, op0=ALU.mult), [t_o[o0], t_lam], [t_ob[bi]])
            S.dma("sp", OT[h, :, dst_tok0:dst_tok0 + N], ob[bi][:, 0:N], t_ob[bi], False)

        def stick_group(qap, N, blocks, dst_tok0, h):
            po, pu, pc = 3, 4, 5
            S.op("dve", lambda g: g.memset(cb[:, 0:N], 0.0), [], [t_cb])
            nb = len(blocks)
            for bi, (kap, vap, nk, m01, mneg) in enumerate(reversed(blocks)):
                pss = nxt("pss", 3)
                S.op("pe", lambda g: g.matmul(psf[0:nk, pss, 0:N], kap, qap, start=True, stop=True),
                     [t_kT, t_qT, t_kcT], [t_psf[pss]])
                fs = nxt("wf", 4)
                fl = nxt("wf", 4)
                S.op("act", lambda g: g.activation(out=wf[fs][0:nk, 0:N], in_=psf[0:nk, pss, 0:N], func=AF.Exp,
                                                   scale=-scale), [t_psf[pss]], [twf[fs]])
                S.op("act", lambda g: g.activation(out=wf[fs][0:nk, 0:N], in_=wf[fs][0:nk, 0:N], func=AF.Ln, bias=1.0),
                     [twf[fs]], [twf[fs]])
                S.op("dve", lambda g: g.scalar_tensor_tensor(out=wf[fl][0:nk, 0:N], in0=psf[0:nk, pss, 0:N], scalar=-scale,
                                                             in1=wf[fs][0:nk, 0:N], op0=ALU.mult, op1=ALU.subtract),
                     [t_psf[pss], twf[fs]], [twf[fl]])
                if m01 is not None:
                    S.op("dve", lambda g: g.tensor_tensor(out=wf[fl][0:nk, 0:N], in0=wf[fl][0:nk, 0:N], in1=m01,
                                                          op=ALU.mult), [twf[fl], t_bias, t_biass], [twf[fl]])
                hi = nxt("wb", 6)
                lo = nxt("wb", 6)
                S.op("dve", lambda g: g.tensor_copy(out=wb[hi][0:nk, 0:N], in_=wf[fl][0:nk, 0:N]), [twf[fl]], [twb[hi]])
                S.op("pool", lambda g: g.tensor_tensor(out=wb[lo][0:nk, 0:N], in0=wf[fl][0:nk, 0:N], in1=wb[hi][0:nk, 0:N],
                                                       op=ALU.subtract), [twf[fl], twb[hi]], [twb[lo]])
                S.op("pe", lambda g: g.matmul(psf[0:nk, pu, 0:N], umat[0:nk, 0:nk], wb[hi][0:nk, 0:N], start=True, stop=False),
                     [twb[hi], t_cst], [t_psf[pu]])
                S.op("pe", lambda g: g.matmul(psf[0:nk, pu, 0:N], umat[0:nk, 0:nk], wb[lo][0:nk, 0:N], start=False, stop=True),
                     [twb[lo], t_cst], [t_psf[pu]])
                S.op("pe", lambda g: g.matmul(psf[:, pc, 0:N], ones[0:nk, :], wb[hi][0:nk, 0:N], start=True, stop=False),
                     [twb[hi], t_cst], [t_psf[pc]])
                S.op("pe", lambda g: g.matmul(psf[:, pc, 0:N], ones[0:nk, :], wb[lo][0:nk, 0:N], start=False, stop=True),
                     [twb[lo], t_cst], [t_psf[pc]])
                S.op("dve", lambda g: g.tensor_tensor(out=wf[fs][0:nk, 0:N], in0=psf[0:nk, pu, 0:N], in1=wf[fs][0:nk, 0:N],
                                                      op=ALU.subtract), [t_psf[pu], twf[fs]], [twf[fs]])
                S.op("dve", lambda g: g.tensor_tensor(out=wf[fs][0:nk, 0:N], in0=wf[fs][0:nk, 0:N], in1=cb[0:nk, 0:N],
                                                      op=ALU.add), [twf[fs], t_cb], [twf[fs]])
                if mneg is not None:
                    S.op("dve", lambda g: g.tensor_tensor(out=wf[fs][0:nk, 0:N], in0=wf[fs][0:nk, 0:N], in1=mneg,
                                                          op=ALU.add), [twf[fs], t_bias, t_biass], [twf[fs]])
                ai = nxt("wb", 6)
                S.op("act", lambda g: g.activation(out=wb[ai][0:nk, 0:N], in_=wf[fs][0:nk, 0:N], func=AF.Exp),
                     [twf[fs]], [twb[ai]])
                S.op("dve", lambda g: g.tensor_tensor(out=cb[:, 0:N], in0=psf[:, pc, 0:N], in1=cb[:, 0:N], op=ALU.add),
                     [t_psf[pc], t_cb], [t_cb])
                S.op("pe", lambda g: g.matmul(psf[:, po, 0:N], vap, wb[ai][0:nk, 0:N], start=(bi == 0), stop=(bi == nb - 1)),
                     [twb[ai], t_vh, t_vs, t_vc], [t_psf[po]])
            bi2 = (rr.__setitem__("obb", (rr.get("obb", 0) + 1) % 2) or rr["obb"])
            S.op("act", lambda g: g.activation(out=ob[bi2][:, 0:N], in_=psf[:, po, 0:N], func=AF.Identity),
                 [t_psf[po]], [t_ob[bi2]])
            S.dma("sp", OT[h, :, dst_tok0:dst_tok0 + N], ob[bi2][:, 0:N], t_ob[bi2], False)

        if kind == 2:
            S.dma("sp", bias[:, :, :], cmask_d[:, :, :], t_bias, True)
            S.dma("sp", biass[0:32, 0:2, :], cmasks_d[:, :, :], t_biass, True)

        for h in range(NH):
            hs = slice(h * 128, (h + 1) * 128)
            S.dma("sp", kT[:, :], KT[h, :, :], t_kT, True)
            S.dma("sp", qT[:, :], QT[h, :, :], t_qT, True)
            for vv in range(4):
                S.dma("sp", vh[:, vv * 8:(vv + 1) * 8, :],
                      VB[vv * 1024:(vv + 1) * 1024, hs].rearrange("(b p) d -> p b d", p=128), t_vh, True)
            S.dma("sp", vs[:, :, :], VB[SEQ:NTOK, hs].rearrange("(s p) d -> p s d", p=32), t_vs, True)
            if kind == 0:
                S.dma("sp", bias[:, :, :], biasA_d[j][h], t_bias, True)
                S.dma("sp", biass[:, 0:5, :], biasAs_d[j][h], t_biass, True)
            elif kind == 1:
                S.dma("sp", bias[:, 0:6, :], biasB_d[h], t_bias, True)
                S.dma("sp", biass[:, :, :], biasBs_d[h], t_biass, True)
            for t in range(SEQ // 512):
                qap = qT[:, t * 512:(t + 1) * 512]
                if kind == 0:
                    blocks = []
                    for kb in range(max(0, 4 * t - 4), 4 * t + 4):
                        blocks.append((kT[:, kb * 128:(kb + 1) * 128], vh[:, kb, :], 128, bias[:, kb - (4 * t - 4), :]))
                    oi = softmax_group(qap, 512, blocks, slice(0, 128))
                    finish_A(oi, 512, t * 512, h)
                elif kind == 1:
                    ois = []
                    for half in range(2):
                        blocks = []
                        for kb in range(0, 4 * t + 4):
                            dl = 128 * kb - 512 * t
                            bidx = 0 if dl <= -256 else 1 + (dl + 128) // 128
                            blocks.append((kT[:, kb * 128:(kb + 1) * 128], vh[:, kb, :], 128, bias[:, bidx, :]))
                        ois.append(softmax_group(qap, 512, blocks, slice(64 * half, 64 * half + 64)))
                    finish_B(ois[0], ois[1], 512, t * 512, h)
                else:
                    blocks = []
                    for kb in range(0, 4 * t + 4):
                        dl = 128 * kb - 512 * t
                        if dl >= 0:
                            blocks.append((kT[:, kb * 128:(kb + 1) * 128], vh[:, kb, :], 128,
                                           bias[:, dl // 128, :], bias[:, 4 + dl // 128, :]))
                        else:
                            blocks.append((kT[:, kb * 128:(kb + 1) * 128], vh[:, kb, :], 128, None, None))
                    stick_group(qap, 512, blocks, t * 512, h)
            for sbi in range(2):
                rows = NCB * 128
                for cc2 in range(NCB // 2):
                    S.dma("pool", kcr[:, 2 * cc2:2 * cc2 + 2, :],
                          ck[j, sbi, 256 * cc2:256 * cc2 + 256, hs].rearrange("(b p) d -> p b d", p=128), t_kcr, True)
                    S.dma("pool", vc[:, 2 * cc2:2 * cc2 + 2, :],
                          cv[j, sbi, 256 * cc2:256 * cc2 + 256, hs].rearrange("(b p) d -> p b d", p=128), t_vc, True)
                for b4 in range(NCB // 4):
                    pb = nxt("pst", 2)
                    for jj in range(4):
                        b = b4 * 4 + jj
                        S.op("pe", lambda g, b=b, jj=jj, pb=pb: g.transpose(out=pst[:, pb, jj * 128:(jj + 1) * 128],
                                                                             in_=kcr[:, b, :], identity=ident[:, :]),
                             [t_kcr, t_ident], [t_pst[pb]])
                    copy_op(evac_engine(), kcT[:, b4 * 512:(b4 + 1) * 512], pst[:, pb, 0:512], [t_pst[pb]], [t_kcT])
                q0 = SEQ + 32 * sbi
                qap = qT[:, q0:q0 + 32]
                knew = kT[:, q0:q0 + 32]
                vnew = vs[:, sbi, :]
                if kind == 0 or kind == 1:
                    ois = []
                    for half in range(1 if kind == 0 else 2):
                        blocks = [(kcT[:, b * 128:(b + 1) * 128], vc[:, b, :], 128, biass[:, b, :]) for b in range(NCB)]
                        blocks.append((knew, vnew, 32, biass[0:32, NCB, :]))
                        dsl = slice(0, 128) if kind == 0 else slice(64 * half, 64 * half + 64)
                        ois.append(softmax_group(qap, 32, blocks, dsl))
                    if kind == 0:
                        finish_A(ois[0], 32, q0, h)
                    else:
                        finish_B(ois[0], ois[1], 32, q0, h)
                else:
                    blocks = [(kcT[:, b * 128:(b + 1) * 128], vc[:, b, :], 128, None, None) for b in range(NCB)]
                    blocks.append((knew, vnew, 32, biass[0:32, 0, :], biass[0:32, 1, :]))
                    stick_group(qap, 32, blocks, q0, h)

    if 0 in PHASES:
        phase0()
    S.barrier()
    for li in LAYER_SEL:
        if 1 in PHASES:
            phase1(li)
        S.barrier()
        if 2 in PHASES:
            with contextlib.ExitStack() as st2:
                phase2(li, st2)
                S.barrier()
        S.barrier()
        if 3 in PHASES:
            with contextlib.ExitStack() as st3:
                H["hT"] = st3.enter_context(nc.sbuf_tensor("s_hT_%d" % li, [128, NFC, 512], BF16))
                H["gb"] = st3.enter_context(nc.sbuf_tensor("s_gb_%d" % li, [128, 2, D], F32))
                phase3(li, li == LAYER_SEL[-1])
                S.barrier()
        S.barrier()
    S.final_wait()


def _t5_bucket(rel):
    half, max_exact = 16, 8
    n = np.abs(rel)
    nf = np.maximum(n, 1).astype(np.float32)
    large = max_exact + (np.log(nf / np.float32(max_exact)) / np.float32(math.log(128 / max_exact))
                         * np.float32(half - max_exact)).astype(np.int32)
    return np.where(rel > 0, half, 0) + np.where(n < max_exact, n, np.minimum(large, half - 1))


def _attention_constants(rel_bias_a, t5):
    p = np.arange(128)[:, None]
    out = {}
    neg_row = np.full((1, NH), NEG, np.float32)
    f = np.arange(512)[None, :]
    idxA = np.empty((8, 128, 512), np.int64)
    for i in range(8):
        ko = -512 + 128 * i + p
        rel = ko - f
        kc = np.floor_divide(ko, 64)
        qc = f // 64
        valid = (kc <= qc) & (kc >= qc - 8)
        idxA[i] = np.where(valid, np.clip(rel, -256, 256) + 256, 513)
    fs = np.arange(32)[None, :]
    idxAs = np.empty((5, 128, 32), np.int64)
    for b in range(5):
        kpos = (512 + 128 * b + p) if b < 4 else (1024 + p)
        idxAs[b] = np.clip(kpos - (1024 + fs), -256, 256) + 256
    bA = np.empty((2, NH, 128, 8, 512), np.float32)
    bAs = np.empty((2, NH, 128, 5, 32), np.float32)
    for j in range(2):
        tab = np.concatenate([rel_bias_a[j], neg_row], axis=0)
        bA[j] = np.transpose(tab[idxA], (3, 1, 0, 2))
        bAs[j] = np.transpose(tab[idxAs], (3, 1, 0, 2))
    out["biasA"] = bA
    out["biasAs"] = bAs
    tabB = np.concatenate([t5, neg_row], axis=0)
    idxB = np.empty((6, 128, 512), np.int64)
    idxB[0] = 15
    for i in range(1, 6):
        dl = -128 + 128 * (i - 1)
        ko = dl + p
        rel = ko - f
        valid = np.floor_divide(ko, 64) <= (f // 64)
        idxB[i] = np.where(valid, _t5_bucket(rel), 32)
    out["biasB"] = np.ascontiguousarray(np.transpose(tabB[idxB], (3, 1, 0, 2)))
    idxBs = np.empty((9, 128, 32), np.int64)
    for b in range(9):
        kpos = (128 * b + p) if b < 8 else (1024 + p)
        idxBs[b] = _t5_bucket(kpos - (1024 + fs))
    out["biasBs"] = np.ascontiguousarray(np.transpose(tabB[idxBs], (3, 1, 0, 2)))
    cm = np.empty((128, 8, 512), np.float32)
    for i in range(4):
        v = ((128 * i + p) < f).astype(np.float32)
        cm[:, i, :] = v
        cm[:, 4 + i, :] = (1.0 - v) * NEG
    out["cmask"] = cm
    ps = np.arange(32)[:, None]
    vs_ = (ps < fs).astype(np.float32)
    out["cmasks"] = np.ascontiguousarray(np.stack([vs_, (1.0 - vs_) * NEG], axis=1))
    out["umat"] = (np.arange(128)[:, None] > np.arange(128)[None, :]).astype(np.float32)
    out["ones"] = np.ones((128, 128), np.float32)
    return out

_NC_CACHE = {}


def _get_nc():
    if "nc" not in _NC_CACHE:
        _NC_CACHE["nc"] = build_program()
    return _NC_CACHE["nc"]


def kernel(x_prompt, x_sample, cache_a_k, cache_a_v, cache_b_k, cache_b_v, cache_c_k, cache_c_v,
           w_in_a, w_out_a, rel_bias_a, w_in_b, w_out_b, lambda_q1, lambda_k1, lambda_q2, lambda_k2,
           diff_norm_g, t5_bias, w_in_c, w_out_c, ln1_g, ln1_b, ln2_g, ln2_b, w_gate, w_up, w_down):
    f = lambda a: np.ascontiguousarray(np.asarray(a, dtype=np.float32))
    nc = _get_nc()
    wi_l = [w_in_a[0], w_in_b[0], w_in_c[0], w_in_a[1]]
    wo_l = [w_out_a[0], w_out_b[0], w_out_c[0], w_out_a[1]]
    shared = {
        "ln1_g": f(ln1_g), "ln1_b": f(ln1_b), "ln2_g": f(ln2_g), "ln2_b": f(ln2_b),
        "ident": np.eye(128, dtype=np.float32),
    }
    for l in LAYER_SEL:
        shared["w_in_L%d" % l] = f(wi_l[l])
        shared["w_out_L%d" % l] = f(wo_l[l])
        shared["w_gate_L%d" % l] = f(w_gate[l])
        shared["w_up_L%d" % l] = f(w_up[l])
        shared["w_down_L%d" % l] = f(w_down[l])
    shared.update(_attention_constants(np.asarray(rel_bias_a, dtype=np.float32), np.asarray(t5_bias, dtype=np.float32)))
    shared["lam"] = f(np.stack([np.asarray(lambda_q1)[0], np.asarray(lambda_k1)[0],
                                np.asarray(lambda_q2)[0], np.asarray(lambda_k2)[0]]))
    shared["dng"] = f(np.asarray(diff_norm_g)[0].reshape(128, 1))
    x_prompt = np.asarray(x_prompt)
    x_sample = np.asarray(x_sample)
    in_maps = []
    for c in range(N_CORES):
        m = dict(shared)
        m["xp"] = f(x_prompt[c])
        m["xs"] = f(x_sample[2 * c:2 * c + 2].reshape(NS, D))
        m["ca_k"] = f(np.asarray(cache_a_k)[:, 2 * c:2 * c + 2].reshape(2, 2, 512, D))
        m["ca_v"] = f(np.asarray(cache_a_v)[:, 2 * c:2 * c + 2].reshape(2, 2, 512, D))
        m["cb_k"] = f(np.asarray(cache_b_k)[:, 2 * c:2 * c + 2].reshape(1, 2, PAST, D))
        m["cb_v"] = f(np.asarray(cache_b_v)[:, 2 * c:2 * c + 2].reshape(1, 2, PAST, D))
        m["cc_k"] = f(np.asarray(cache_c_k)[:, 2 * c:2 * c + 2].reshape(1, 2, PAST, D))
        m["cc_v"] = f(np.asarray(cache_c_v)[:, 2 * c:2 * c + 2].reshape(1, 2, PAST, D))
        in_maps.append(m)
    res = run_bass_kernel_spmd(nc, in_maps, core_ids=list(range(N_CORES)))
    R = list(res.results)
    while len(R) < 8:
        R.append(R[0])

    def cat_prompt(name, lead):
        arr = np.stack([R[c][name] for c in range(8)], axis=1)
        return arr.reshape(arr.shape[0], 8, arr.shape[2], NH, HD)

    def cat_sample(name):
        arr = np.stack([R[c][name].reshape(-1, 2, 32, D) for c in range(8)], axis=1)
        return arr.reshape(arr.shape[0], 16, 32, NH, HD)

    y_p = np.stack([R[c]["yp"] for c in range(8)], axis=0)
    y_s = np.concatenate([R[c]["ys"].reshape(2, 32, D) for c in range(8)], axis=0)
    outs = (y_p, y_s,
            cat_prompt("pa_k", 2), cat_prompt("pa_v", 2),
            cat_prompt("pb_k", 1), cat_prompt("pb_v", 1),
            cat_prompt("pc_k", 1), cat_prompt("pc_v", 1),
            cat_sample("sa_k"), cat_sample("sa_v"),
            cat_sample("sb_k"), cat_sample("sb_v"),
            cat_sample("sc_k"), cat_sample("sc_v"))
    return tuple(np.ascontiguousarray(o, dtype=np.float32) for o in outs)
```

```python
import contextlib
import math
import numpy as np
import concourse.bass as bass
import concourse.mybir as mybir
from concourse.bass_utils import run_bass_kernel_spmd

F32 = mybir.dt.float32
BF16 = mybir.dt.bfloat16
AF = mybir.ActivationFunctionType
ALU = mybir.AluOpType

D = 2048
NH = 16
HD = 128
SEQ = 4096
NS = 64
NTOK = SEQ + NS
DFF = 5632
NFC = DFF // 128
DEPTH = 4
ALPHA = (2 * DEPTH) ** 0.25
LN_EPS = 1e-5
RMS_EPS = 1e-5
PAST = 1024
NEG = -30000.0
KINDS = [0, 1, 2, 0]
LJ = [0, 0, 0, 1]

LAYER_SEL = (0, 1, 2, 3)
N_CORES = 8
WCH = 2
DBG_NO_F32OUT = False
TILE_SEL = None
PHASES = (0, 1, 2, 3)


class Tok:
    __slots__ = ("w", "r", "sem", "cnt")

    def __init__(self):
        self.w = None
        self.r = {}
        self.sem = None
        self.cnt = 0


class Sched:
    def __init__(self, nc, stack):
        self.nc = nc
        self.stack = stack
        self.eng = {"pe": nc.tensor, "act": nc.scalar, "dve": nc.vector, "pool": nc.gpsimd, "sp": nc.sync}
        self.semobj = {}
        self.cnt = {}
        self.known = {}
        for e in self.eng:
            self.semobj[e] = stack.enter_context(nc.semaphore("sem_" + e))
            self.cnt[e] = 0
            self.known[e] = {}
        self.ndsem = 0
        self.dma_marks = {}

    def _wait(self, e, deps):
        kn = self.known[e]
        for sid, val in deps.items():
            if kn.get(sid, 0) < val:
                self.eng[e].wait_ge(self.semobj[sid], val)
                kn[sid] = val

    def _deps(self, e, reads, writes):
        deps = {}

        def add(mark, raw):
            sid, val = mark
            if sid == e and (e == "pe" or not raw):
                return
            if deps.get(sid, 0) < val:
                deps[sid] = val
        for t in reads:
            if t.w is not None:
                add(t.w, True)
        for t in writes:
            if t.w is not None:
                add(t.w, False)
            for k, v in t.r.items():
                add((k, v), False)
        return deps

    def op(self, e, fn, reads=(), writes=()):
        self._wait(e, self._deps(e, reads, writes))
        inst = fn(self.eng[e])
        self.cnt[e] += 1
        inst.then_inc(self.semobj[e], 1)
        c = self.cnt[e]
        for t in reads:
            t.r[e] = c
        for t in writes:
            t.w = (e, c)
            t.r = {}
        return inst

    def dma(self, q, out, in_, tok, is_write, extra_reads=(), extra_writes=()):
        if tok.sem is None:
            sid = "d%d" % self.ndsem
            self.ndsem += 1
            assert self.ndsem < 90, "too many dma sems"
            self.semobj[sid] = self.stack.enter_context(self.nc.semaphore("sem_" + sid))
            tok.sem = sid
        reads = list(extra_reads) + ([] if is_write else [tok])
        writes = list(extra_writes) + ([tok] if is_write else [])
        self._wait(q, self._deps(q, reads, writes))
        inst = self.eng[q].dma_start(out=out, in_=in_)
        tok.cnt += 16
        inst.then_inc(self.semobj[tok.sem], 16)
        mark = (tok.sem, tok.cnt)
        self.dma_marks[tok.sem] = tok.cnt
        for t in reads:
            t.r[tok.sem] = tok.cnt
        for t in writes:
            t.w = mark
            t.r = {}
        return inst

    def barrier(self):
        deps = {e: self.cnt[e] for e in self.eng if self.cnt[e] > 0}
        deps.update(self.dma_marks)
        for e in self.eng:
            d = {k: v for k, v in deps.items() if k != e}
            self._wait(e, d)

    def final_wait(self):
        self.barrier()


def build_program():
    nc = bass.Bass("TRN2", target_bir_lowering=False)
    stack = contextlib.ExitStack()
    with stack:
        _emit(nc, stack)
    return nc


def _emit(nc, stack):
    S = Sched(nc, stack)

    def din(name, shape, dt=F32):
        return nc.dram_tensor(name, list(shape), dt, kind="ExternalInput").ap()

    def dout(name, shape, dt=F32):
        return nc.dram_tensor(name, list(shape), dt, kind="ExternalOutput").ap()

    def dscr(name, shape, dt):
        return nc.dram_tensor(name, list(shape), dt, kind="Internal").ap()

    def sb(name, shape, dt):
        return stack.enter_context(nc.sbuf_tensor("s_" + name, list(shape), dt))

    def ps(name, shape, dt):
        return stack.enter_context(nc.psum_tensor("p_" + name, list(shape), dt))

    xp = din("xp", [SEQ, D])
    xs = din("xs", [NS, D])
    ca_k = din("ca_k", [2, 2, 512, D])
    ca_v = din("ca_v", [2, 2, 512, D])
    cb_k = din("cb_k", [1, 2, PAST, D])
    cb_v = din("cb_v", [1, 2, PAST, D])
    cc_k = din("cc_k", [1, 2, PAST, D])
    cc_v = din("cc_v", [1, 2, PAST, D])
    w_in = [din("w_in_L%d" % l, [D, 3 * D]) if l in LAYER_SEL else None for l in range(DEPTH)]
    w_out = [din("w_out_L%d" % l, [D, D]) if l in LAYER_SEL else None for l in range(DEPTH)]
    w_gate = [din("w_gate_L%d" % l, [D, DFF]) if l in LAYER_SEL else None for l in range(DEPTH)]
    w_up = [din("w_up_L%d" % l, [D, DFF]) if l in LAYER_SEL else None for l in range(DEPTH)]
    w_down = [din("w_down_L%d" % l, [DFF, D]) if l in LAYER_SEL else None for l in range(DEPTH)]
    ln1_g = din("ln1_g", [DEPTH, D])
    ln1_b = din("ln1_b", [DEPTH, D])
    ln2_g = din("ln2_g", [DEPTH, D])
    ln2_b = din("ln2_b", [DEPTH, D])
    ident_d = din("ident", [128, 128])
    umat_d = din("umat", [128, 128])
    ones_d = din("ones", [128, 128])
    biasA_d = din("biasA", [2, NH, 128, 8, 512])
    biasAs_d = din("biasAs", [2, NH, 128, 5, 32])
    biasB_d = din("biasB", [NH, 128, 6, 512])
    biasBs_d = din("biasBs", [NH, 128, 9, 32])
    cmask_d = din("cmask", [128, 8, 512])
    cmasks_d = din("cmasks", [32, 2, 32])
    lam_d = din("lam", [4, 64])
    dng_d = din("dng", [128, 1])

    yp = dout("yp", [SEQ, D])
    ys = dout("ys", [NS, D])
    pa_k = dout("pa_k", [2, 512, D])
    pa_v = dout("pa_v", [2, 512, D])
    pb_k = dout("pb_k", [1, SEQ, D])
    pb_v = dout("pb_v", [1, SEQ, D])
    pc_k = dout("pc_k", [1, SEQ, D])
    pc_v = dout("pc_v", [1, SEQ, D])
    sa_k = dout("sa_k", [2, NS, D])
    sa_v = dout("sa_v", [2, NS, D])
    sb_k = dout("sb_k", [1, NS, D])
    sb_v = dout("sb_v", [1, NS, D])
    sc_k = dout("sc_k", [1, NS, D])
    sc_v = dout("sc_v", [1, NS, D])
    pko = [pa_k, pb_k, pc_k]
    pvo = [pa_v, pb_v, pc_v]
    sko = [sa_k, sb_k, sc_k]
    svo = [sa_v, sb_v, sc_v]

    wb_in = [dscr("wb_in_L%d" % l, [D, 3 * D], BF16) if l in LAYER_SEL else None for l in range(DEPTH)]
    wb_out = [dscr("wb_out_L%d" % l, [D, D], BF16) if l in LAYER_SEL else None for l in range(DEPTH)]
    wb_gate = [dscr("wb_gate_L%d" % l, [D, DFF], BF16) if l in LAYER_SEL else None for l in range(DEPTH)]
    wb_up = [dscr("wb_up_L%d" % l, [D, DFF], BF16) if l in LAYER_SEL else None for l in range(DEPTH)]
    wb_down = [dscr("wb_down_L%d" % l, [DFF, D], BF16) if l in LAYER_SEL else None for l in range(DEPTH)]
    XT = dscr("XT", [16, 128, NTOK], BF16)
    QT = dscr("QT", [NH, 128, NTOK], BF16)
    KT = dscr("KT", [NH, 128, NTOK], BF16)
    VB = dscr("VB", [NTOK, D], BF16)
    OT = dscr("OT", [NH, 128, NTOK], BF16)

    ident = sb("ident", [128, 128], BF16)
    t_ident = Tok()
    NW = 3
    wslot = [sb("wslot%d" % i, [128, 16, 512], BF16) for i in range(NW)]
    t_w = [Tok() for _ in range(NW)]
    actT = sb("actT", [128, 16, 512], BF16)
    t_actT = Tok()
    xres = sb("xres", [128, 4, D], F32)
    t_xres = [Tok() for _ in range(4)]
    t_hT = [Tok() for _ in range(NFC)]
    t_gb = Tok()
    H = {}
    NST = 4
    st_f = [sb("stf%d" % i, [128, 512], F32) for i in range(NST)]
    t_stf = [Tok() for _ in range(NST)]
    st_b = [sb("stb%d" % i, [128, 512], BF16) for i in range(NST)]
    t_stb = [Tok() for _ in range(NST)]
    xb16 = sb("xb16", [128, D], BF16)
    t_xb16 = Tok()
    stats = sb("stats", [128, 4, 6], F32)
    mv = sb("mv", [128, 2], F32)
    rstd = sb("rstd", [128, 1], F32)
    t_stats = Tok()

    psf = ps("psf", [128, 6, 512], F32)
    t_psf = [Tok() for _ in range(6)]
    pst = ps("pst", [128, 2, 1024], BF16)
    t_pst = [Tok() for _ in range(2)]

    rr = {"ps": 0, "pst": 0, "stf": 0, "stb": 0, "w": 0, "ev": 0}

    def nxt(key, n):
        v = rr[key]
        rr[key] = (v + 1) % n
        return v

    def evac_engine():
        return "act" if nxt("ev", 2) == 0 else "dve"

    def copy_op(e, out, in_, reads, writes):
        if e == "act":
            S.op("act", lambda g: g.activation(out=out, in_=in_, func=AF.Identity), reads, writes)
        else:
            S.op(e, lambda g: g.tensor_copy(out=out, in_=in_), reads, writes)

    S.dma("pool", ident[:], ident_d[:, :], t_ident, True)

    TILES = [(i * 512, 512) for i in range(SEQ // 512)] + [(SEQ, NS)]
    if TILE_SEL is not None:
        TILES = [TILES[i] for i in TILE_SEL]

    def blocks_of(T):
        return [(b * 128, 128) for b in range(T // 128)] if T >= 128 else [(0, T)]

    def tokmajor_dram(tok0, n, prompt_ap, sample_ap):
        if tok0 < SEQ:
            return prompt_ap[tok0:tok0 + n, :]
        return sample_ap[tok0 - SEQ:tok0 - SEQ + n, :]

    def load_wgroup(src_f32, dst_bf16, nk, first):
        i = nxt("w", NW)
        if first:
            src = src_f32.rearrange("(k p) c -> p k c", p=128)
            k0 = 0
            while k0 < nk:
                k1 = min(nk, k0 + WCH)
                S.dma("pool", wslot[i][:, k0:k1, :], src[:, k0:k1, :], t_w[i], True)
                k0 = k1
            S.dma("sp", dst_bf16.rearrange("(k p) c -> p k c", p=128), wslot[i][:, 0:nk, :], t_w[i], False)
        else:
            S.dma("sp", wslot[i][:, 0:nk, :], dst_bf16.rearrange("(k p) c -> p k c", p=128), t_w[i], True)
        return i

    def transpose_to_actT(src_bf16, t_src, bsz, boff):
        for kc4 in range(4):
            pb = nxt("pst", 2)
            for j in range(4):
                kc = kc4 * 4 + j
                S.op("pe", lambda g, kc=kc, j=j, pb=pb: g.transpose(
                    out=pst[:, pb, j * 128:j * 128 + bsz], in_=src_bf16[0:bsz, kc * 128:(kc + 1) * 128],
                    identity=ident[0:bsz, 0:bsz]),
                    reads=[t_src, t_ident], writes=[t_pst[pb]])
            e = evac_engine()
            copy_op(e, actT[:, kc4 * 4:kc4 * 4 + 4, boff:boff + bsz],
                    pst[:, pb, 0:512].rearrange("p (j t) -> p j t", j=4)[:, :, 0:bsz],
                    [t_pst[pb]], [t_actT])

    def store_actT_to_XT(tok0, T):
        S.dma("sp", XT[:, :, tok0:tok0 + T].rearrange("k p t -> p k t"), actT[:, :, 0:T], t_actT, False)

    def phase0():
        for (tok0, T) in TILES:
            for bi, (boff, bsz) in enumerate(blocks_of(T)):
                src = tokmajor_dram(tok0 + boff, bsz, xp, xs)
                S.dma("sp", xres[0:bsz, bi, :], src, t_xres[bi], True)
                e = evac_engine()
                copy_op(e, xb16[0:bsz, :], xres[0:bsz, bi, :], [t_xres[bi]], [t_xb16])
                transpose_to_actT(xb16, t_xb16, bsz, boff)
            store_actT_to_XT(tok0, T)

    def phase1(li):
        kind, j = KINDS[li], LJ[li]
        win = w_in[li]
        for (tok0, T) in TILES:
            is_sample = tok0 >= SEQ
            S.dma("sp", actT[:, :, 0:T], XT[:, :, tok0:tok0 + T].rearrange("k p t -> p k t"), t_actT, True)
            if is_sample:
                kdst = sko[kind][j]
                vdst = svo[kind][j]
                row0 = 0
                want_k_tm = True
            elif kind == 0:
                want_k_tm = (tok0 == SEQ - 512)
                kdst = pko[0][j]
                vdst = pvo[0][j]
                row0 = 0
            else:
                want_k_tm = True
                kdst = pko[kind][0]
                vdst = pvo[kind][0]
                row0 = tok0
            want_vo = want_k_tm
            for g in range(12):
                wi = load_wgroup(win[:, g * 512:(g + 1) * 512], wb_in[li][:, g * 512:(g + 1) * 512], 16, tok0 == TILES[0][0])
                sect = g // 4
                if sect < 2:
                    dstT = QT if sect == 0 else KT
                    for hh in range(4):
                        head = (g % 4) * 4 + hh
                        pb = nxt("ps", 6)
                        for kc in range(16):
                            S.op("pe", lambda gg, kc=kc, hh=hh, pb=pb, wi=wi: gg.matmul(
                                psf[:, pb, 0:T], wslot[wi][:, kc, hh * 128:(hh + 1) * 128], actT[:, kc, 0:T],
                                start=(kc == 0), stop=(kc == 15)),
                                reads=[t_w[wi], t_actT], writes=[t_psf[pb]])
                        si = nxt("stb", NST)
                        copy_op(evac_engine(), st_b[si][:, 0:T], psf[:, pb, 0:T], [t_psf[pb]], [t_stb[si]])
                        S.dma("sp", dstT[head, :, tok0:tok0 + T], st_b[si][:, 0:T], t_stb[si], False)
                if (sect == 1 and want_k_tm) or sect == 2:
                    for (boff, bsz) in blocks_of(T):
                        pb = nxt("ps", 6)
                        for kc in range(16):
                            S.op("pe", lambda gg, kc=kc, pb=pb, wi=wi, boff=boff, bsz=bsz: gg.matmul(
                                psf[0:bsz, pb, :], actT[:, kc, boff:boff + bsz], wslot[wi][:, kc, :],
                                start=(kc == 0), stop=(kc == 15)),
                                reads=[t_w[wi], t_actT], writes=[t_psf[pb]])
                        c0 = (g % 4) * 512
                        sf = nxt("stf", NST)
                        copy_op(evac_engine(), st_f[sf][0:bsz, :], psf[0:bsz, pb, :], [t_psf[pb]], [t_stf[sf]])
                        if sect == 2:
                            si = nxt("stb", NST)
                            copy_op(evac_engine(), st_b[si][0:bsz, :], st_f[sf][0:bsz, :], [t_stf[sf]], [t_stb[si]])
                            S.dma("sp", VB[tok0 + boff:tok0 + boff + bsz, c0:c0 + 512], st_b[si][0:bsz, :],
                                  t_stb[si], False)
                        if (sect == 1) or want_vo:
                            dst = kdst if sect == 1 else vdst
                            S.dma("sp", dst[row0 + boff:row0 + boff + bsz, c0:c0 + 512], st_f[sf][0:bsz, :],
                                  t_stf[sf], False)

    def layer_norm_block(bi, bsz, gsel):
        xr = xres[0:bsz, bi, :]
        for c in range(4):
            S.op("dve", lambda g, c=c: g.bn_stats(out=stats[0:bsz, c, :], in_=xres[0:bsz, bi, c * 512:(c + 1) * 512]),
                 reads=[t_xres[bi]], writes=[t_stats])
        S.op("dve", lambda g: g.bn_aggr(out=mv[0:bsz, :], in_=stats[0:bsz, :, :].rearrange("p a b -> p (a b)")),
             reads=[t_stats], writes=[t_stats])
        S.op("dve", lambda g: g.tensor_scalar(out=rstd[0:bsz, :], in0=mv[0:bsz, 1:2], scalar1=LN_EPS, scalar2=None,
                                              op0=ALU.add),
             reads=[t_stats], writes=[t_stats])
        S.op("act", lambda g: g.activation(out=rstd[0:bsz, :], in_=rstd[0:bsz, :], func=AF.Sqrt),
             reads=[t_stats], writes=[t_stats])
        S.op("dve", lambda g: g.reciprocal(out=rstd[0:bsz, :], in_=rstd[0:bsz, :]),
             reads=[t_stats], writes=[t_stats])
        S.op("dve", lambda g: g.tensor_scalar(out=xr, in0=xr, scalar1=mv[0:bsz, 0:1], scalar2=rstd[0:bsz, 0:1],
                                              op0=ALU.subtract, op1=ALU.mult),
             reads=[t_xres[bi], t_stats], writes=[t_xres[bi]])
        S.op("dve", lambda g: g.tensor_tensor(out=xr, in0=xr, in1=H['gb'][0:bsz, 0, :], op=ALU.mult),
             reads=[t_xres[bi], t_gb], writes=[t_xres[bi]])
        S.op("dve", lambda g: g.tensor_tensor(out=xr, in0=xr, in1=H['gb'][0:bsz, 1, :], op=ALU.add),
             reads=[t_xres[bi], t_gb], writes=[t_xres[bi]])
        S.op("act", lambda g: g.copy(out=xb16[0:bsz, :], in_=xr), reads=[t_xres[bi]], writes=[t_xb16])

    def load_gb(gvec, bvec):
        S.dma("sp", H["gb"][:, 0, :], gvec.partition_broadcast(128), t_gb, True)
        S.dma("sp", H["gb"][:, 1, :], bvec.partition_broadcast(128), t_gb, True)

    def phase3(li, last):
        kind, j = KINDS[li], LJ[li]
        wo = w_out[li]
        for (tok0, T) in TILES:
            blks = blocks_of(T)
            rsrc_p, rsrc_s = (xp, xs) if li == 0 else (yp, ys)
            for bi, (boff, bsz) in enumerate(blks):
                S.dma("sp", xres[0:bsz, bi, :], tokmajor_dram(tok0 + boff, bsz, rsrc_p, rsrc_s), t_xres[bi], True)
            S.dma("sp", actT[:, :, 0:T], OT[:, :, tok0:tok0 + T].rearrange("k p t -> p k t"), t_actT, True)
            load_gb(ln1_g[li], ln1_b[li])
            for g in range(4):
                wi = load_wgroup(wo[:, g * 512:(g + 1) * 512], wb_out[li][:, g * 512:(g + 1) * 512], 16, tok0 == TILES[0][0])
                for bi, (boff, bsz) in enumerate(blks):
                    pb = nxt("ps", 6)
                    for kc in range(16):
                        S.op("pe", lambda gg, kc=kc, pb=pb, wi=wi, boff=boff, bsz=bsz: gg.matmul(
                            psf[0:bsz, pb, :], actT[:, kc, boff:boff + bsz], wslot[wi][:, kc, :],
                            start=(kc == 0), stop=(kc == 15)),
                            reads=[t_w[wi], t_actT], writes=[t_psf[pb]])
                    xr = xres[0:bsz, bi, g * 512:(g + 1) * 512]
                    S.op("dve", lambda gg, xr=xr, pb=pb, bsz=bsz: gg.scalar_tensor_tensor(
                        out=xr, in0=xr, scalar=ALPHA, in1=psf[0:bsz, pb, :], op0=ALU.mult, op1=ALU.add),
                        reads=[t_psf[pb], t_xres[bi]], writes=[t_xres[bi]])
            for bi, (boff, bsz) in enumerate(blks):
                layer_norm_block(bi, bsz, 0)
                transpose_to_actT(xb16, t_xb16, bsz, boff)
            load_gb(ln2_g[li], ln2_b[li])
            for g in range(DFF // 512):
                wg = load_wgroup(w_gate[li][:, g * 512:(g + 1) * 512], wb_gate[li][:, g * 512:(g + 1) * 512], 16, tok0 == TILES[0][0])
                wu = load_wgroup(w_up[li][:, g * 512:(g + 1) * 512], wb_up[li][:, g * 512:(g + 1) * 512], 16, tok0 == TILES[0][0])
                for cc in range(4):
                    fc = g * 4 + cc
                    pg = nxt("ps", 6)
                    for kc in range(16):
                        S.op("pe", lambda gg, kc=kc, pg=pg, wg=wg, cc=cc: gg.matmul(
                            psf[:, pg, 0:T], wslot[wg][:, kc, cc * 128:(cc + 1) * 128], actT[:, kc, 0:T],
                            start=(kc == 0), stop=(kc == 15)),
                            reads=[t_w[wg], t_actT], writes=[t_psf[pg]])
                    pu = nxt("ps", 6)
                    for kc in range(16):
                        S.op("pe", lambda gg, kc=kc, pu=pu, wu=wu, cc=cc: gg.matmul(
                            psf[:, pu, 0:T], wslot[wu][:, kc, cc * 128:(cc + 1) * 128], actT[:, kc, 0:T],
                            start=(kc == 0), stop=(kc == 15)),
                            reads=[t_w[wu], t_actT], writes=[t_psf[pu]])
                    si = nxt("stf", NST)
                    S.op("act", lambda gg, si=si, pg=pg: gg.activation(out=st_f[si][:, 0:T], in_=psf[:, pg, 0:T],
                                                                      func=AF.Silu),
                         reads=[t_psf[pg]], writes=[t_stf[si]])
                    S.op("dve", lambda gg, si=si, pu=pu, fc=fc: gg.tensor_tensor(
                        out=H['hT'][:, fc, 0:T], in0=st_f[si][:, 0:T], in1=psf[:, pu, 0:T], op=ALU.mult),
                        reads=[t_stf[si], t_psf[pu]], writes=[t_hT[fc]])
            for g in range(4):
                pbs = [nxt("ps", 6) for _ in blks]
                for pc in range(4):
                    wi = load_wgroup(w_down[li][pc * 1408:(pc + 1) * 1408, g * 512:(g + 1) * 512],
                                     wb_down[li][pc * 1408:(pc + 1) * 1408, g * 512:(g + 1) * 512], 11, tok0 == TILES[0][0])
                    for bi, (boff, bsz) in enumerate(blks):
                        pb = pbs[bi]
                        for k in range(11):
                            fc = pc * 11 + k
                            S.op("pe", lambda gg, k=k, fc=fc, pb=pb, wi=wi, boff=boff, bsz=bsz: gg.matmul(
                                psf[0:bsz, pb, :], H['hT'][:, fc, boff:boff + bsz], wslot[wi][:, k, :],
                                start=(fc == 0), stop=(fc == NFC - 1)),
                                reads=[t_w[wi], t_hT[fc]], writes=[t_psf[pb]])
                for bi, (boff, bsz) in enumerate(blks):
                    pb = pbs[bi]
                    xr = xres[0:bsz, bi, g * 512:(g + 1) * 512]
                    S.op("dve", lambda gg, xr=xr, pb=pb, bsz=bsz: gg.scalar_tensor_tensor(
                        out=xr, in0=xr, scalar=ALPHA, in1=psf[0:bsz, pb, :], op0=ALU.mult, op1=ALU.add),
                        reads=[t_psf[pb], t_xres[bi]], writes=[t_xres[bi]])
            for bi, (boff, bsz) in enumerate(blks):
                layer_norm_block(bi, bsz, 1)
                S.dma("sp", tokmajor_dram(tok0 + boff, bsz, yp, ys), xres[0:bsz, bi, :], t_xres[bi], False)
                if not last:
                    transpose_to_actT(xb16, t_xb16, bsz, boff)
            if not last:
                store_actT_to_XT(tok0, T)

    t_kT, t_qT, t_vh, t_vs, t_bias, t_biass = Tok(), Tok(), Tok(), Tok(), Tok(), Tok()
    t_kcr, t_kcT, t_vc, t_cb, t_lam, t_cst = Tok(), Tok(), Tok(), Tok(), Tok(), Tok()
    t_wf = [Tok() for _ in range(6)]
    t_wb = [Tok() for _ in range(8)]
    t_ob = [Tok() for _ in range(2)]
    t_o = [Tok() for _ in range(3)]
    rr.update({"wf": 0, "wb": 0, "pss": 0, "ob": 0})
    LAM_INIT = [0.8 - 0.6 * math.exp(-0.3 * i) for i in range(DEPTH)]

    def phase2(li, st):
        kind, j = KINDS[li], LJ[li]

        def a_sb(name, shape, dt):
            return st.enter_context(nc.sbuf_tensor("s_%s_%d" % (name, li), list(shape), dt))
        kT = a_sb("kT", [128, NTOK], BF16)
        qT = a_sb("qT", [128, NTOK], BF16)
        vh = a_sb("vh", [128, 32, 128], BF16)
        vs = a_sb("vs", [32, 2, 128], BF16)
        bias = a_sb("bias", [128, 8, 512], F32)
        biass = a_sb("biass", [128, 9, 32], F32)
        kcr = a_sb("kcr", [128, 8, 128], BF16)
        kcT = a_sb("kcT", [128, 1024], BF16)
        vc = a_sb("vc", [128, 8, 128], BF16)
        wf = [st_f[i] for i in range(4)]
        wb = [st_b[i] for i in range(4)]
        twf = t_stf
        twb = t_stb
        if kind == 2:
            FS = [st_f[i] for i in range(4)]
            tFS = t_stf
            FL = [a_sb("FL%d" % i, [128, 512], F32) for i in range(2)]
            tFL = t_wf[0:2]
            HI = [st_b[0], st_b[1], a_sb("HI2", [128, 512], BF16)]
            tHI = [t_stb[0], t_stb[1], t_wb[0]]
            LO = [st_b[2], st_b[3], a_sb("LO2", [128, 512], BF16)]
            tLO = [t_stb[2], t_stb[3], t_wb[1]]
            AI = [a_sb("AI%d" % i, [128, 512], BF16) for i in range(2)]
            tAI = t_wb[2:4]
        ob = [a_sb("ob%d" % i, [128, 512], BF16) for i in range(2)]
        osum = [a_sb("osum%d" % i, [128, 512], F32) for i in range(3)]
        umat = a_sb("umat", [128, 128], BF16)
        ones = a_sb("ones", [128, 128], BF16)
        onesf = a_sb("onesf", [128, 128], F32)
        lam = a_sb("lam", [128, 8], F32)
        lvec = a_sb("lvec", [128, 4, 64], F32)
        gcol = a_sb("gcol", [128, 1], F32)
        S.dma("pool", umat[:], umat_d[:, :], t_cst, True)
        S.dma("pool", ones[:], ones_d[:, :], t_cst, True)
        S.dma("sp", onesf[:], ones_d[:, :], t_cst, True)
        NCB = 4 if kind == 0 else 8
        ck, cv = [(ca_k, ca_v), (cb_k, cb_v), (cc_k, cc_v)][kind]
        scale = (HD ** -0.5) if kind != 1 else (64 ** -0.5)

        if kind == 1:
            for i in range(4):
                S.dma("sp", lvec[:, i, :], lam_d[i].partition_broadcast(128), t_lam, True)
            S.dma("sp", gcol[:, :], dng_d[:, :], t_lam, True)
            S.op("dve", lambda g: g.tensor_tensor(out=lvec[:, 0, :], in0=lvec[:, 0, :], in1=lvec[:, 1, :], op=ALU.mult),
                 [t_lam], [t_lam])
            S.op("dve", lambda g: g.tensor_tensor(out=lvec[:, 2, :], in0=lvec[:, 2, :], in1=lvec[:, 3, :], op=ALU.mult),
                 [t_lam], [t_lam])
            S.op("dve", lambda g: g.reduce_sum(out=lam[:, 0:1], in_=lvec[:, 0, :], axis=mybir.AxisListType.X),
                 [t_lam], [t_lam])
            S.op("dve", lambda g: g.reduce_sum(out=lam[:, 1:2], in_=lvec[:, 2, :], axis=mybir.AxisListType.X),
                 [t_lam], [t_lam])
            S.op("act", lambda g: g.activation(out=lam[:, 0:2], in_=lam[:, 0:2], func=AF.Exp), [t_lam], [t_lam])
            S.op("dve", lambda g: g.tensor_tensor(out=lam[:, 2:3], in0=lam[:, 1:2], in1=lam[:, 0:1], op=ALU.subtract),
                 [t_lam], [t_lam])
            S.op("dve", lambda g: g.tensor_scalar(out=lam[:, 2:3], in0=lam[:, 2:3], scalar1=-LAM_INIT[li], scalar2=None,
                                                  op0=ALU.add), [t_lam], [t_lam])
            S.op("dve", lambda g: g.tensor_scalar(out=gcol[:, :], in0=gcol[:, :], scalar1=1.0 - LAM_INIT[li],
                                                  scalar2=None, op0=ALU.mult), [t_lam], [t_lam])

        rtmp = a_sb("rtmp", [128, 512], F32)
        t_rtmp = Tok()

        def fin_sm(po, pr, N):
            oi = nxt("ob", 3)
            S.op("dve", lambda g: g.reciprocal(out=rtmp[:, 0:N], in_=psf[:, pr, 0:N]), [t_psf[pr]], [t_rtmp])
            S.op("dve", lambda g: g.tensor_tensor(out=osum[oi][:, 0:N], in0=psf[:, po, 0:N], in1=rtmp[:, 0:N], op=ALU.mult),
                 [t_psf[po], t_rtmp], [t_o[oi]])
            return oi

        def run_softmax(B):
            nb = len(B)
            for k in range(-2, nb + 1):
                i = k + 2
                if 0 <= i < nb:
                    b = B[i]
                    nk, N, pss = b["nk"], b["N"], i % 2
                    S.op("pe", lambda g: g.matmul(psf[0:nk, pss, 0:N], b["kap"], b["qap"], start=True, stop=True),
                         [t_kT, t_qT, t_kcT], [t_psf[pss]])
                i = k + 1
                if 0 <= i < nb:
                    b = B[i]
                    nk, N, pss, fi = b["nk"], b["N"], i % 2, i % 4
                    S.op("dve", lambda g: g.scalar_tensor_tensor(out=wf[fi][0:nk, 0:N], in0=psf[0:nk, pss, 0:N],
                                                                 scalar=scale, in1=b["bap"], op0=ALU.mult, op1=ALU.add),
                         [t_psf[pss], t_bias, t_biass], [twf[fi]])
                i = k
                if 0 <= i < nb:
                    b = B[i]
                    nk, N, fi, pi = b["nk"], b["N"], i % 4, i % 4
                    S.op("act", lambda g: g.activation(out=wb[pi][0:nk, 0:N], in_=wf[fi][0:nk, 0:N], func=AF.Exp),
                         [twf[fi]], [twb[pi]])
                i = k - 1
                if 0 <= i < nb:
                    b = B[i]
                    nk, N, pi = b["nk"], b["N"], i % 4
                    po, pr = b["banks"]
                    S.op("pe", lambda g: g.matmul(psf[:, po, 0:N], b["vap"], wb[pi][0:nk, 0:N], start=b["first"],
                                                  stop=b["last"]),
                         [twb[pi], t_vh, t_vs, t_vc], [t_psf[po]])
                    S.op("pe", lambda g: g.matmul(psf[:, pr, 0:N], ones[0:nk, :], wb[pi][0:nk, 0:N], start=b["first"],
                                                  stop=b["last"]),
                         [twb[pi], t_cst], [t_psf[pr]])
                    if b["last"]:
                        b["fin"]()

        def sm_group(qap, N, blocks, dsl, banks, fin):
            out = []
            nb = len(blocks)
            for bi, (kap, vap, nk, bap) in enumerate(blocks):
                out.append(dict(kap=kap[dsl, :], qap=qap[dsl, :], vap=vap, nk=nk, N=N, bap=bap,
                                first=(bi == 0), last=(bi == nb - 1), banks=banks, fin=fin))
            return out

        def finish_A(oi, N, dst_tok0, h):
            bi = nxt("ob", 2) if False else (rr.__setitem__("obb", (rr.get("obb", 0) + 1) % 2) or rr["obb"])
            S.op("act", lambda g: g.activation(out=ob[bi][:, 0:N], in_=osum[oi][:, 0:N], func=AF.Identity),
                 [t_o[oi]], [t_ob[bi]])
            S.dma("sp", OT[h, :, dst_tok0:dst_tok0 + N], ob[bi][:, 0:N], t_ob[bi], False)

        def finish_B(o0, o1, N, dst_tok0, h):
            S.op("dve", lambda g: g.scalar_tensor_tensor(out=osum[o0][:, 0:N], in0=osum[o1][:, 0:N], scalar=lam[:, 2:3],
                                                         in1=osum[o0][:, 0:N], op0=ALU.mult, op1=ALU.add),
                 [t_o[o0], t_o[o1], t_lam], [t_o[o0]])
            S.op("dve", lambda g: g.tensor_tensor(out=osum[o1][:, 0:N], in0=osum[o0][:, 0:N], in1=osum[o0][:, 0:N],
                                                  op=ALU.mult), [t_o[o0]], [t_o[o1]])
            S.op("pe", lambda g: g.matmul(psf[:, 5, 0:N], onesf[:, :], osum[o1][:, 0:N], start=True, stop=True),
                 [t_o[o1], t_cst], [t_psf[5]])
            S.op("dve", lambda g: g.tensor_scalar(out=osum[o1][:, 0:N], in0=psf[:, 5, 0:N], scalar1=1.0 / HD,
                                                  scalar2=RMS_EPS, op0=ALU.mult, op1=ALU.add), [t_psf[5]], [t_o[o1]])
            S.op("act", lambda g: g.activation(out=osum[o1][:, 0:N], in_=osum[o1][:, 0:N], func=AF.Sqrt),
                 [t_o[o1]], [t_o[o1]])
            S.op("dve", lambda g: g.reciprocal(out=osum[o1][:, 0:N], in_=osum[o1][:, 0:N]), [t_o[o1]], [t_o[o1]])
            S.op("dve", lambda g: g.tensor_tensor(out=osum[o0][:, 0:N], in0=osum[o0][:, 0:N], in1=osum[o1][:, 0:N],
                                                  op=ALU.mult), [t_o[o0], t_o[o1]], [t_o[o0]])
            bi = (rr.__setitem__("obb", (rr.get("obb", 0) + 1) % 2) or rr["obb"])
            S.op("dve", lambda g: g.tensor_scalar(out=ob[bi][:, 0:N], in0=osum[o0][:, 0:N], scalar1=gcol[:, 0:1],
                                                  scalar2=None, op0=ALU.mult), [t_o[o0], t_lam], [t_ob[bi]])
            S.dma("sp", OT[h, :, dst_tok0:dst_tok0 + N], ob[bi][:, 0:N], t_ob[bi], False)

        def run_stick(B):
            nb = len(B)
            po, pu, pc = 3, 4, 5
            for k in range(-3, nb + 1):
                i = k + 3
                if 0 <= i < nb:
                    b = B[i]
                    nk, N, pss = b["nk"], b["N"], i % 3
                    S.op("pe", lambda g: g.matmul(psf[0:nk, pss, 0:N], b["kap"], b["qap"], start=True, stop=True),
                         [t_kT, t_qT, t_kcT], [t_psf[pss]])
                i = k + 2
                if 0 <= i < nb:
                    b = B[i]
                    nk, N, pss, fs = b["nk"], b["N"], i % 3, i % 4
                    S.op("act", lambda g: g.activation(out=FS[fs][0:nk, 0:N], in_=psf[0:nk, pss, 0:N], func=AF.Exp,
                                                       scale=-scale), [t_psf[pss]], [tFS[fs]])
                    S.op("act", lambda g: g.activation(out=FS[fs][0:nk, 0:N], in_=FS[fs][0:nk, 0:N], func=AF.Ln, bias=1.0),
                         [tFS[fs]], [tFS[fs]])
                i = k + 1
                if 0 <= i < nb:
                    b = B[i]
                    nk, N, pss, fs, fl, hl = b["nk"], b["N"], i % 3, i % 4, i % 2, i % 3
                    S.op("dve", lambda g: g.scalar_tensor_tensor(out=FL[fl][0:nk, 0:N], in0=psf[0:nk, pss, 0:N],
                                                                 scalar=-scale, in1=FS[fs][0:nk, 0:N], op0=ALU.mult,
                                                                 op1=ALU.subtract),
                         [t_psf[pss], tFS[fs]], [tFL[fl]])
                    if b["m01"] is not None:
                        S.op("dve", lambda g: g.tensor_tensor(out=FL[fl][0:nk, 0:N], in0=FL[fl][0:nk, 0:N], in1=b["m01"],
                                                              op=ALU.mult), [tFL[fl], t_bias, t_biass], [tFL[fl]])
                    S.op("pool", lambda g: g.tensor_copy(out=HI[hl][0:nk, 0:N], in_=FL[fl][0:nk, 0:N]), [tFL[fl]], [tHI[hl]])
                    S.op("pool", lambda g: g.tensor_tensor(out=LO[hl][0:nk, 0:N], in0=FL[fl][0:nk, 0:N],
                                                           in1=HI[hl][0:nk, 0:N], op=ALU.subtract),
                         [tFL[fl], tHI[hl]], [tLO[hl]])
                i = k - 1
                if 0 <= i < nb:
                    b = B[i]
                    nk, N, fs = b["nk"], b["N"], i % 4
                    S.op("dve", lambda g: g.tensor_tensor(out=FS[fs][0:nk, 0:N], in0=psf[0:nk, pu, 0:N],
                                                          in1=FS[fs][0:nk, 0:N], op=ALU.subtract),
                         [t_psf[pu], tFS[fs]], [tFS[fs]])
                    if not b["first"]:
                        S.op("dve", lambda g: g.tensor_tensor(out=FS[fs][0:nk, 0:N], in0=FS[fs][0:nk, 0:N],
                                                              in1=psf[0:nk, pc, 0:N], op=ALU.add),
                             [tFS[fs], t_psf[pc]], [tFS[fs]])
                    if b["mneg"] is not None:
                        S.op("dve", lambda g: g.tensor_tensor(out=FS[fs][0:nk, 0:N], in0=FS[fs][0:nk, 0:N], in1=b["mneg"],
                                                              op=ALU.add), [tFS[fs], t_bias, t_biass], [tFS[fs]])
                i = k
                if 0 <= i < nb:
                    b = B[i]
                    nk, N, hl = b["nk"], b["N"], i % 3
                    S.op("pe", lambda g: g.matmul(psf[0:nk, pu, 0:N], umat[0:nk, 0:nk], HI[hl][0:nk, 0:N], start=True,
                                                  stop=False), [tHI[hl], t_cst], [t_psf[pu]])
                    S.op("pe", lambda g: g.matmul(psf[0:nk, pu, 0:N], umat[0:nk, 0:nk], LO[hl][0:nk, 0:N], start=False,
                                                  stop=True), [tLO[hl], t_cst], [t_psf[pu]])
                i = k - 1
                if 0 <= i < nb:
                    b = B[i]
                    nk, N, fs, hl, ai = b["nk"], b["N"], i % 4, i % 3, i % 2
                    S.op("act", lambda g: g.activation(out=AI[ai][0:nk, 0:N], in_=FS[fs][0:nk, 0:N], func=AF.Exp),
                         [tFS[fs]], [tAI[ai]])
                    if not b["last"]:
                        S.op("pe", lambda g: g.matmul(psf[:, pc, 0:N], ones[0:nk, :], HI[hl][0:nk, 0:N], start=b["first"],
                                                      stop=False), [tHI[hl], t_cst], [t_psf[pc]])
                        S.op("pe", lambda g: g.matmul(psf[:, pc, 0:N], ones[0:nk, :], LO[hl][0:nk, 0:N], start=False,
                                                      stop=True), [tLO[hl], t_cst], [t_psf[pc]])
                    S.op("pe", lambda g: g.matmul(psf[:, po, 0:N], b["vap"], AI[ai][0:nk, 0:N], start=b["first"],
                                                  stop=b["last"]),
                         [tAI[ai], t_vh, t_vs, t_vc], [t_psf[po]])
                    if b["last"]:
                        b["fin"]()

        def st_group(qap, N, blocks, fin):
            out = []
            nb = len(blocks)
            for bi, (kap, vap, nk, m01, mneg) in enumerate(reversed(blocks)):
                out.append(dict(kap=kap, qap=qap, vap=vap, nk=nk, N=N, m01=m01, mneg=mneg,
                                first=(bi == 0), last=(bi == nb - 1), fin=fin))
            return out

        def fin_stick(N, dst_tok0, h):
            bi2 = (rr.__setitem__("obb", (rr.get("obb", 0) + 1) % 2) or rr["obb"])
            S.op("act", lambda g: g.activation(out=ob[bi2][:, 0:N], in_=psf[:, 3, 0:N], func=AF.Identity),
                 [t_psf[3]], [t_ob[bi2]])
            S.dma("sp", OT[h, :, dst_tok0:dst_tok0 + N], ob[bi2][:, 0:N], t_ob[bi2], False)

        if kind == 2:
            S.dma("sp", bias[:, :, :], cmask_d[:, :, :], t_bias, True)
            S.dma("sp", biass[0:32, 0:2, :], cmasks_d[:, :, :], t_biass, True)

        for h in range(NH):
            hs = slice(h * 128, (h + 1) * 128)
            S.dma("sp", kT[:, :], KT[h, :, :], t_kT, True)
            S.dma("sp", qT[:, :], QT[h, :, :], t_qT, True)
            for vv in range(4):
                S.dma("sp", vh[:, vv * 8:(vv + 1) * 8, :],
                      VB[vv * 1024:(vv + 1) * 1024, hs].rearrange("(b p) d -> p b d", p=128), t_vh, True)
            S.dma("sp", vs[:, :, :], VB[SEQ:NTOK, hs].rearrange("(s p) d -> p s d", p=32), t_vs, True)
            if kind == 0:
                S.dma("sp", bias[:, :, :], biasA_d[j][h], t_bias, True)
                S.dma("sp", biass[:, 0:5, :], biasAs_d[j][h], t_biass, True)
            elif kind == 1:
                S.dma("sp", bias[:, 0:6, :], biasB_d[h], t_bias, True)
                S.dma("sp", biass[:, :, :], biasBs_d[h], t_biass, True)
            PB = []
            for t in range(SEQ // 512):
                qap = qT[:, t * 512:(t + 1) * 512]
                if kind == 0:
                    blocks = []
                    for kb in range(max(0, 4 * t - 4), 4 * t + 4):
                        blocks.append((kT[:, kb * 128:(kb + 1) * 128], vh[:, kb, :], 128, bias[:, kb - (4 * t - 4), :]))
                    banks = (2, 3) if t % 2 == 0 else (4, 5)
                    PB += sm_group(qap, 512, blocks, slice(0, 128), banks,
                                   lambda banks=banks, t=t: finish_A(fin_sm(banks[0], banks[1], 512), 512, t * 512, h))
                elif kind == 1:
                    ois = {}
                    for half in range(2):
                        blocks = []
                        for kb in range(0, 4 * t + 4):
                            dl = 128 * kb - 512 * t
                            bidx = 0 if dl <= -256 else 1 + (dl + 128) // 128
                            blocks.append((kT[:, kb * 128:(kb + 1) * 128], vh[:, kb, :], 128, bias[:, bidx, :]))
                        banks = (2, 3) if half == 0 else (4, 5)
                        if half == 0:
                            fin = lambda ois=ois: ois.__setitem__(0, fin_sm(2, 3, 512))
                        else:
                            fin = lambda ois=ois, t=t: finish_B(ois[0], fin_sm(4, 5, 512), 512, t * 512, h)
                        PB += sm_group(qap, 512, blocks, slice(64 * half, 64 * half + 64), banks, fin)
                else:
                    blocks = []
                    for kb in range(0, 4 * t + 4):
                        dl = 128 * kb - 512 * t
                        if dl >= 0:
                            blocks.append((kT[:, kb * 128:(kb + 1) * 128], vh[:, kb, :], 128,
                                           bias[:, dl // 128, :], bias[:, 4 + dl // 128, :]))
                        else:
                            blocks.append((kT[:, kb * 128:(kb + 1) * 128], vh[:, kb, :], 128, None, None))
                    PB += st_group(qap, 512, blocks, lambda t=t: fin_stick(512, t * 512, h))
            if kind == 2:
                run_stick(PB)
            else:
                run_softmax(PB)
            for sbi in range(2):
                rows = NCB * 128
                for cc2 in range(NCB // 2):
                    S.dma("pool", kcr[:, 2 * cc2:2 * cc2 + 2, :],
                          ck[j, sbi, 256 * cc2:256 * cc2 + 256, hs].rearrange("(b p) d -> p b d", p=128), t_kcr, True)
                    S.dma("pool", vc[:, 2 * cc2:2 * cc2 + 2, :],
                          cv[j, sbi, 256 * cc2:256 * cc2 + 256, hs].rearrange("(b p) d -> p b d", p=128), t_vc, True)
                for b4 in range(NCB // 4):
                    pb = nxt("pst", 2)
                    for jj in range(4):
                        b = b4 * 4 + jj
                        S.op("pe", lambda g, b=b, jj=jj, pb=pb: g.transpose(out=pst[:, pb, jj * 128:(jj + 1) * 128],
                                                                             in_=kcr[:, b, :], identity=ident[:, :]),
                             [t_kcr, t_ident], [t_pst[pb]])
                    copy_op(evac_engine(), kcT[:, b4 * 512:(b4 + 1) * 512], pst[:, pb, 0:512], [t_pst[pb]], [t_kcT])
                q0 = SEQ + 32 * sbi
                qap = qT[:, q0:q0 + 32]
                knew = kT[:, q0:q0 + 32]
                vnew = vs[:, sbi, :]
                if kind == 0 or kind == 1:
                    SB = []
                    ois = {}
                    for half in range(1 if kind == 0 else 2):
                        blocks = [(kcT[:, b * 128:(b + 1) * 128], vc[:, b, :], 128, biass[:, b, :]) for b in range(NCB)]
                        blocks.append((knew, vnew, 32, biass[0:32, NCB, :]))
                        dsl = slice(0, 128) if kind == 0 else slice(64 * half, 64 * half + 64)
                        banks = (2, 3) if half == 0 else (4, 5)
                        if kind == 0:
                            fin = lambda q0=q0: finish_A(fin_sm(2, 3, 32), 32, q0, h)
                        elif half == 0:
                            fin = lambda ois=ois: ois.__setitem__(0, fin_sm(2, 3, 32))
                        else:
                            fin = lambda ois=ois, q0=q0: finish_B(ois[0], fin_sm(4, 5, 32), 32, q0, h)
                        SB += sm_group(qap, 32, blocks, dsl, banks, fin)
                    run_softmax(SB)
                else:
                    blocks = [(kcT[:, b * 128:(b + 1) * 128], vc[:, b, :], 128, None, None) for b in range(NCB)]
                    blocks.append((knew, vnew, 32, biass[0:32, 0, :], biass[0:32, 1, :]))
                    run_stick(st_group(qap, 32, blocks, lambda q0=q0: fin_stick(32, q0, h)))

    if 0 in PHASES:
        phase0()
    S.barrier()
    for li in LAYER_SEL:
        if 1 in PHASES:
            phase1(li)
        S.barrier()
        if 2 in PHASES:
            with contextlib.ExitStack() as st2:
                phase2(li, st2)
                S.barrier()
        S.barrier()
        if 3 in PHASES:
            with contextlib.ExitStack() as st3:
                H["hT"] = st3.enter_context(nc.sbuf_tensor("s_hT_%d" % li, [128, NFC, 512], BF16))
                H["gb"] = st3.enter_context(nc.sbuf_tensor("s_gb_%d" % li, [128, 2, D], F32))
                phase3(li, li == LAYER_SEL[-1])
                S.barrier()
        S.barrier()
    S.final_wait()


def _t5_bucket(rel):
    half, max_exact = 16, 8
    n = np.abs(rel)
    nf = np.maximum(n, 1).astype(np.float32)
    large = max_exact + (np.log(nf / np.float32(max_exact)) / np.float32(math.log(128 / max_exact))
                         * np.float32(half - max_exact)).astype(np.int32)
    return np.where(rel > 0, half, 0) + np.where(n < max_exact, n, np.minimum(large, half - 1))


def _attention_constants(rel_bias_a, t5):
    p = np.arange(128)[:, None]
    out = {}
    neg_row = np.full((1, NH), NEG, np.float32)
    f = np.arange(512)[None, :]
    idxA = np.empty((8, 128, 512), np.int64)
    for i in range(8):
        ko = -512 + 128 * i + p
        rel = ko - f
        kc = np.floor_divide(ko, 64)
        qc = f // 64
        valid = (kc <= qc) & (kc >= qc - 8)
        idxA[i] = np.where(valid, np.clip(rel, -256, 256) + 256, 513)
    fs = np.arange(32)[None, :]
    idxAs = np.empty((5, 128, 32), np.int64)
    for b in range(5):
        kpos = (512 + 128 * b + p) if b < 4 else (1024 + p)
        idxAs[b] = np.clip(kpos - (1024 + fs), -256, 256) + 256
    bA = np.empty((2, NH, 128, 8, 512), np.float32)
    bAs = np.empty((2, NH, 128, 5, 32), np.float32)
    for j in range(2):
        tab = np.concatenate([rel_bias_a[j], neg_row], axis=0)
        bA[j] = np.transpose(tab[idxA], (3, 1, 0, 2))
        bAs[j] = np.transpose(tab[idxAs], (3, 1, 0, 2))
    out["biasA"] = bA
    out["biasAs"] = bAs
    tabB = np.concatenate([t5, neg_row], axis=0)
    idxB = np.empty((6, 128, 512), np.int64)
    idxB[0] = 15
    for i in range(1, 6):
        dl = -128 + 128 * (i - 1)
        ko = dl + p
        rel = ko - f
        valid = np.floor_divide(ko, 64) <= (f // 64)
        idxB[i] = np.where(valid, _t5_bucket(rel), 32)
    out["biasB"] = np.ascontiguousarray(np.transpose(tabB[idxB], (3, 1, 0, 2)))
    idxBs = np.empty((9, 128, 32), np.int64)
    for b in range(9):
        kpos = (128 * b + p) if b < 8 else (1024 + p)
        idxBs[b] = _t5_bucket(kpos - (1024 + fs))
    out["biasBs"] = np.ascontiguousarray(np.transpose(tabB[idxBs], (3, 1, 0, 2)))
    cm = np.empty((128, 8, 512), np.float32)
    for i in range(4):
        v = ((128 * i + p) < f).astype(np.float32)
        cm[:, i, :] = v
        cm[:, 4 + i, :] = (1.0 - v) * NEG
    out["cmask"] = cm
    ps = np.arange(32)[:, None]
    vs_ = (ps < fs).astype(np.float32)
    out["cmasks"] = np.ascontiguousarray(np.stack([vs_, (1.0 - vs_) * NEG], axis=1))
    out["umat"] = (np.arange(128)[:, None] > np.arange(128)[None, :]).astype(np.float32)
    out["ones"] = np.ones((128, 128), np.float32)
    return out

_NC_CACHE = {}


def _get_nc():
    if "nc" not in _NC_CACHE:
        _NC_CACHE["nc"] = build_program()
    return _NC_CACHE["nc"]


def kernel(x_prompt, x_sample, cache_a_k, cache_a_v, cache_b_k, cache_b_v, cache_c_k, cache_c_v,
           w_in_a, w_out_a, rel_bias_a, w_in_b, w_out_b, lambda_q1, lambda_k1, lambda_q2, lambda_k2,
           diff_norm_g, t5_bias, w_in_c, w_out_c, ln1_g, ln1_b, ln2_g, ln2_b, w_gate, w_up, w_down):
    f = lambda a: np.ascontiguousarray(np.asarray(a, dtype=np.float32))
    nc = _get_nc()
    wi_l = [w_in_a[0], w_in_b[0], w_in_c[0], w_in_a[1]]
    wo_l = [w_out_a[0], w_out_b[0], w_out_c[0], w_out_a[1]]
    shared = {
        "ln1_g": f(ln1_g), "ln1_b": f(ln1_b), "ln2_g": f(ln2_g), "ln2_b": f(ln2_b),
        "ident": np.eye(128, dtype=np.float32),
    }
    for l in LAYER_SEL:
        shared["w_in_L%d" % l] = f(wi_l[l])
        shared["w_out_L%d" % l] = f(wo_l[l])
        shared["w_gate_L%d" % l] = f(w_gate[l])
        shared["w_up_L%d" % l] = f(w_up[l])
        shared["w_down_L%d" % l] = f(w_down[l])
    shared.update(_attention_constants(np.asarray(rel_bias_a, dtype=np.float32), np.asarray(t5_bias, dtype=np.float32)))
    shared["lam"] = f(np.stack([np.asarray(lambda_q1)[0], np.asarray(lambda_k1)[0],
                                np.asarray(lambda_q2)[0], np.asarray(lambda_k2)[0]]))
    shared["dng"] = f(np.asarray(diff_norm_g)[0].reshape(128, 1))
    x_prompt = np.asarray(x_prompt)
    x_sample = np.asarray(x_sample)
    in_maps = []
    for c in range(N_CORES):
        m = dict(shared)
        m["xp"] = f(x_prompt[c])
        m["xs"] = f(x_sample[2 * c:2 * c + 2].reshape(NS, D))
        m["ca_k"] = f(np.asarray(cache_a_k)[:, 2 * c:2 * c + 2].reshape(2, 2, 512, D))
        m["ca_v"] = f(np.asarray(cache_a_v)[:, 2 * c:2 * c + 2].reshape(2, 2, 512, D))
        m["cb_k"] = f(np.asarray(cache_b_k)[:, 2 * c:2 * c + 2].reshape(1, 2, PAST, D))
        m["cb_v"] = f(np.asarray(cache_b_v)[:, 2 * c:2 * c + 2].reshape(1, 2, PAST, D))
        m["cc_k"] = f(np.asarray(cache_c_k)[:, 2 * c:2 * c + 2].reshape(1, 2, PAST, D))
        m["cc_v"] = f(np.asarray(cache_c_v)[:, 2 * c:2 * c + 2].reshape(1, 2, PAST, D))
        in_maps.append(m)
    res = run_bass_kernel_spmd(nc, in_maps, core_ids=list(range(N_CORES)))
    R = list(res.results)
    while len(R) < 8:
        R.append(R[0])

    def cat_prompt(name, lead):
        arr = np.stack([R[c][name] for c in range(8)], axis=1)
        return arr.reshape(arr.shape[0], 8, arr.shape[2], NH, HD)

    def cat_sample(name):
        arr = np.stack([R[c][name].reshape(-1, 2, 32, D) for c in range(8)], axis=1)
        return arr.reshape(arr.shape[0], 16, 32, NH, HD)

    y_p = np.stack([R[c]["yp"] for c in range(8)], axis=0)
    y_s = np.concatenate([R[c]["ys"].reshape(2, 32, D) for c in range(8)], axis=0)
    outs = (y_p, y_s,
            cat_prompt("pa_k", 2), cat_prompt("pa_v", 2),
            cat_prompt("pb_k", 1), cat_prompt("pb_v", 1),
            cat_prompt("pc_k", 1), cat_prompt("pc_v", 1),
            cat_sample("sa_k"), cat_sample("sa_v"),
            cat_sample("sb_k"), cat_sample("sb_v"),
            cat_sample("sc_k"), cat_sample("sc_v"))
    return tuple(np.ascontiguousarray(o, dtype=np.float32) for o in outs)
```

```python
import contextlib
import math
import numpy as np
import concourse.bass as bass
import concourse.mybir as mybir
from concourse.bass_utils import run_bass_kernel_spmd

F32 = mybir.dt.float32
BF16 = mybir.dt.bfloat16
AF = mybir.ActivationFunctionType
ALU = mybir.AluOpType

D = 2048
NH = 16
HD = 128
SEQ = 4096
NS = 64
NTOK = SEQ + NS
DFF = 5632
NFC = DFF // 128
DEPTH = 4
ALPHA = (2 * DEPTH) ** 0.25
LN_EPS = 1e-5
RMS_EPS = 1e-5
PAST = 1024
NEG = -30000.0
KINDS = [0, 1, 2, 0]
LJ = [0, 0, 0, 1]

LAYER_SEL = (0, 1, 2, 3)
N_CORES = 8
WCH = 2
DBG_NO_F32OUT = False
TILE_SEL = None
PHASES = (0, 1, 2, 3)


class Tok:
    __slots__ = ("w", "r", "sem", "cnt")

    def __init__(self):
        self.w = None
        self.r = {}
        self.sem = None
        self.cnt = 0


class Sched:
    def __init__(self, nc, stack):
        self.nc = nc
        self.stack = stack
        self.eng = {"pe": nc.tensor, "act": nc.scalar, "dve": nc.vector, "pool": nc.gpsimd, "sp": nc.sync}
        self.semobj = {}
        self.cnt = {}
        self.known = {}
        for e in self.eng:
            self.semobj[e] = stack.enter_context(nc.semaphore("sem_" + e))
            self.cnt[e] = 0
            self.known[e] = {}
        self.ndsem = 0
        self.dma_marks = {}

    def _wait(self, e, deps):
        kn = self.known[e]
        for sid, val in deps.items():
            if kn.get(sid, 0) < val:
                self.eng[e].wait_ge(self.semobj[sid], val)
                kn[sid] = val

    def _deps(self, e, reads, writes):
        deps = {}

        def add(mark, raw):
            sid, val = mark
            if sid == e and (e == "pe" or not raw):
                return
            if deps.get(sid, 0) < val:
                deps[sid] = val
        for t in reads:
            if t.w is not None:
                add(t.w, True)
        for t in writes:
            if t.w is not None:
                add(t.w, False)
            for k, v in t.r.items():
                add((k, v), False)
        return deps

    def op(self, e, fn, reads=(), writes=()):
        self._wait(e, self._deps(e, reads, writes))
        inst = fn(self.eng[e])
        self.cnt[e] += 1
        inst.then_inc(self.semobj[e], 1)
        c = self.cnt[e]
        for t in reads:
            t.r[e] = c
        for t in writes:
            t.w = (e, c)
            t.r = {}
        return inst

    def dma(self, q, out, in_, tok, is_write, extra_reads=(), extra_writes=()):
        if tok.sem is None:
            sid = "d%d" % self.ndsem
            self.ndsem += 1
            assert self.ndsem < 90, "too many dma sems"
            self.semobj[sid] = self.stack.enter_context(self.nc.semaphore("sem_" + sid))
            tok.sem = sid
        reads = list(extra_reads) + ([] if is_write else [tok])
        writes = list(extra_writes) + ([tok] if is_write else [])
        self._wait(q, self._deps(q, reads, writes))
        inst = self.eng[q].dma_start(out=out, in_=in_)
        tok.cnt += 16
        inst.then_inc(self.semobj[tok.sem], 16)
        mark = (tok.sem, tok.cnt)
        self.dma_marks[tok.sem] = tok.cnt
        for t in reads:
            t.r[tok.sem] = tok.cnt
        for t in writes:
            t.w = mark
            t.r = {}
        return inst

    def barrier(self):
        deps = {e: self.cnt[e] for e in self.eng if self.cnt[e] > 0}
        deps.update(self.dma_marks)
        for e in self.eng:
            d = {k: v for k, v in deps.items() if k != e}
            self._wait(e, d)

    def final_wait(self):
        self.barrier()


def build_program():
    nc = bass.Bass("TRN2", target_bir_lowering=False)
    stack = contextlib.ExitStack()
    with stack:
        _emit(nc, stack)
    return nc


def _emit(nc, stack):
    S = Sched(nc, stack)

    def din(name, shape, dt=F32):
        return nc.dram_tensor(name, list(shape), dt, kind="ExternalInput").ap()

    def dout(name, shape, dt=F32):
        return nc.dram_tensor(name, list(shape), dt, kind="ExternalOutput").ap()

    def dscr(name, shape, dt):
        return nc.dram_tensor(name, list(shape), dt, kind="Internal").ap()

    def sb(name, shape, dt):
        return stack.enter_context(nc.sbuf_tensor("s_" + name, list(shape), dt))

    def ps(name, shape, dt):
        return stack.enter_context(nc.psum_tensor("p_" + name, list(shape), dt))

    xp = din("xp", [SEQ, D])
    xs = din("xs", [NS, D])
    ca_k = din("ca_k", [2, 2, 512, D])
    ca_v = din("ca_v", [2, 2, 512, D])
    cb_k = din("cb_k", [1, 2, PAST, D])
    cb_v = din("cb_v", [1, 2, PAST, D])
    cc_k = din("cc_k", [1, 2, PAST, D])
    cc_v = din("cc_v", [1, 2, PAST, D])
    w_in = [din("w_in_L%d" % l, [D, 3 * D]) if l in LAYER_SEL else None for l in range(DEPTH)]
    w_out = [din("w_out_L%d" % l, [D, D]) if l in LAYER_SEL else None for l in range(DEPTH)]
    w_gate = [din("w_gate_L%d" % l, [D, DFF]) if l in LAYER_SEL else None for l in range(DEPTH)]
    w_up = [din("w_up_L%d" % l, [D, DFF]) if l in LAYER_SEL else None for l in range(DEPTH)]
    w_down = [din("w_down_L%d" % l, [DFF, D]) if l in LAYER_SEL else None for l in range(DEPTH)]
    ln1_g = din("ln1_g", [DEPTH, D])
    ln1_b = din("ln1_b", [DEPTH, D])
    ln2_g = din("ln2_g", [DEPTH, D])
    ln2_b = din("ln2_b", [DEPTH, D])
    ident_d = din("ident", [128, 128])
    umat_d = din("umat", [128, 128])
    ones_d = din("ones", [128, 128])
    biasA_d = din("biasA", [2, NH, 128, 8, 512])
    biasAs_d = din("biasAs", [2, NH, 128, 5, 32])
    biasB_d = din("biasB", [NH, 128, 6, 512])
    biasBs_d = din("biasBs", [NH, 128, 9, 32])
    cmask_d = din("cmask", [128, 8, 512])
    cmasks_d = din("cmasks", [32, 2, 32])
    lam_d = din("lam", [4, 64])
    dng_d = din("dng", [128, 1])

    yp = dout("yp", [SEQ, D])
    ys = dout("ys", [NS, D])
    pa_k = dout("pa_k", [2, 512, D])
    pa_v = dout("pa_v", [2, 512, D])
    pb_k = dout("pb_k", [1, SEQ, D])
    pb_v = dout("pb_v", [1, SEQ, D])
    pc_k = dout("pc_k", [1, SEQ, D])
    pc_v = dout("pc_v", [1, SEQ, D])
    sa_k = dout("sa_k", [2, NS, D])
    sa_v = dout("sa_v", [2, NS, D])
    sb_k = dout("sb_k", [1, NS, D])
    sb_v = dout("sb_v", [1, NS, D])
    sc_k = dout("sc_k", [1, NS, D])
    sc_v = dout("sc_v", [1, NS, D])
    pko = [pa_k, pb_k, pc_k]
    pvo = [pa_v, pb_v, pc_v]
    sko = [sa_k, sb_k, sc_k]
    svo = [sa_v, sb_v, sc_v]

    wb_in = [dscr("wb_in_L%d" % l, [D, 3 * D], BF16) if l in LAYER_SEL else None for l in range(DEPTH)]
    wb_out = [dscr("wb_out_L%d" % l, [D, D], BF16) if l in LAYER_SEL else None for l in range(DEPTH)]
    wb_gate = [dscr("wb_gate_L%d" % l, [D, DFF], BF16) if l in LAYER_SEL else None for l in range(DEPTH)]
    wb_up = [dscr("wb_up_L%d" % l, [D, DFF], BF16) if l in LAYER_SEL else None for l in range(DEPTH)]
    wb_down = [dscr("wb_down_L%d" % l, [DFF, D], BF16) if l in LAYER_SEL else None for l in range(DEPTH)]
    XT = dscr("XT", [16, 128, NTOK], BF16)
    QT = dscr("QT", [NH, 128, NTOK], BF16)
    KT = dscr("KT", [NH, 128, NTOK], BF16)
    VB = dscr("VB", [NTOK, D], BF16)
    OT = dscr("OT", [NH, 128, NTOK], BF16)

    ident = sb("ident", [128, 128], BF16)
    t_ident = Tok()
    NW = 3
    wslot = [sb("wslot%d" % i, [128, 16, 512], BF16) for i in range(NW)]
    t_w = [Tok() for _ in range(NW)]
    actT = sb("actT", [128, 16, 512], BF16)
    t_actT = Tok()
    xres = sb("xres", [128, 4, D], F32)
    t_xres = [Tok() for _ in range(4)]
    t_hT = [Tok() for _ in range(NFC)]
    t_gb = Tok()
    H = {}
    NST = 4
    st_f = [sb("stf%d" % i, [128, 512], F32) for i in range(NST)]
    t_stf = [Tok() for _ in range(NST)]
    st_b = [sb("stb%d" % i, [128, 512], BF16) for i in range(NST)]
    t_stb = [Tok() for _ in range(NST)]
    xb16 = sb("xb16", [128, D], BF16)
    t_xb16 = Tok()
    stats = sb("stats", [128, 4, 6], F32)
    mv = sb("mv", [128, 2], F32)
    rstd = sb("rstd", [128, 1], F32)
    t_stats = Tok()

    psf = ps("psf", [128, 6, 512], F32)
    t_psf = [Tok() for _ in range(6)]
    pst = ps("pst", [128, 2, 1024], BF16)
    t_pst = [Tok() for _ in range(2)]

    rr = {"ps": 0, "pst": 0, "stf": 0, "stb": 0, "w": 0, "ev": 0}

    def nxt(key, n):
        v = rr[key]
        rr[key] = (v + 1) % n
        return v

    def evac_engine():
        return "act" if nxt("ev", 2) == 0 else "dve"

    def copy_op(e, out, in_, reads, writes):
        if e == "act":
            S.op("act", lambda g: g.activation(out=out, in_=in_, func=AF.Identity), reads, writes)
        else:
            S.op(e, lambda g: g.tensor_copy(out=out, in_=in_), reads, writes)

    S.dma("pool", ident[:], ident_d[:, :], t_ident, True)

    TILES = [(i * 512, 512) for i in range(SEQ // 512)] + [(SEQ, NS)]
    if TILE_SEL is not None:
        TILES = [TILES[i] for i in TILE_SEL]

    def blocks_of(T):
        return [(b * 128, 128) for b in range(T // 128)] if T >= 128 else [(0, T)]

    def tokmajor_dram(tok0, n, prompt_ap, sample_ap):
        if tok0 < SEQ:
            return prompt_ap[tok0:tok0 + n, :]
        return sample_ap[tok0 - SEQ:tok0 - SEQ + n, :]

    def load_wgroup(src_f32, dst_bf16, nk, first):
        i = nxt("w", NW)
        if first:
            src = src_f32.rearrange("(k p) c -> p k c", p=128)
            k0 = 0
            while k0 < nk:
                k1 = min(nk, k0 + WCH)
                S.dma("pool", wslot[i][:, k0:k1, :], src[:, k0:k1, :], t_w[i], True)
                k0 = k1
            S.dma("sp", dst_bf16.rearrange("(k p) c -> p k c", p=128), wslot[i][:, 0:nk, :], t_w[i], False)
        else:
            S.dma("sp", wslot[i][:, 0:nk, :], dst_bf16.rearrange("(k p) c -> p k c", p=128), t_w[i], True)
        return i

    def transpose_to_actT(src_bf16, t_src, bsz, boff):
        for kc4 in range(4):
            pb = nxt("pst", 2)
            for j in range(4):
                kc = kc4 * 4 + j
                S.op("pe", lambda g, kc=kc, j=j, pb=pb: g.transpose(
                    out=pst[:, pb, j * 128:j * 128 + bsz], in_=src_bf16[0:bsz, kc * 128:(kc + 1) * 128],
                    identity=ident[0:bsz, 0:bsz]),
                    reads=[t_src, t_ident], writes=[t_pst[pb]])
            e = evac_engine()
            copy_op(e, actT[:, kc4 * 4:kc4 * 4 + 4, boff:boff + bsz],
                    pst[:, pb, 0:512].rearrange("p (j t) -> p j t", j=4)[:, :, 0:bsz],
                    [t_pst[pb]], [t_actT])

    def store_actT_to_XT(tok0, T):
        S.dma("sp", XT[:, :, tok0:tok0 + T].rearrange("k p t -> p k t"), actT[:, :, 0:T], t_actT, False)

    def phase0():
        for (tok0, T) in TILES:
            for bi, (boff, bsz) in enumerate(blocks_of(T)):
                src = tokmajor_dram(tok0 + boff, bsz, xp, xs)
                S.dma("sp", xres[0:bsz, bi, :], src, t_xres[bi], True)
                e = evac_engine()
                copy_op(e, xb16[0:bsz, :], xres[0:bsz, bi, :], [t_xres[bi]], [t_xb16])
                transpose_to_actT(xb16, t_xb16, bsz, boff)
            store_actT_to_XT(tok0, T)

    def phase1(li):
        kind, j = KINDS[li], LJ[li]
        win = w_in[li]
        for (tok0, T) in TILES:
            is_sample = tok0 >= SEQ
            S.dma("sp", actT[:, :, 0:T], XT[:, :, tok0:tok0 + T].rearrange("k p t -> p k t"), t_actT, True)
            if is_sample:
                kdst = sko[kind][j]
                vdst = svo[kind][j]
                row0 = 0
                want_k_tm = True
            elif kind == 0:
                want_k_tm = (tok0 == SEQ - 512)
                kdst = pko[0][j]
                vdst = pvo[0][j]
                row0 = 0
            else:
                want_k_tm = True
                kdst = pko[kind][0]
                vdst = pvo[kind][0]
                row0 = tok0
            want_vo = want_k_tm
            for g in range(12):
                wi = load_wgroup(win[:, g * 512:(g + 1) * 512], wb_in[li][:, g * 512:(g + 1) * 512], 16, tok0 == TILES[0][0])
                sect = g // 4
                if sect < 2:
                    dstT = QT if sect == 0 else KT
                    for hh in range(4):
                        head = (g % 4) * 4 + hh
                        pb = nxt("ps", 6)
                        for kc in range(16):
                            S.op("pe", lambda gg, kc=kc, hh=hh, pb=pb, wi=wi: gg.matmul(
                                psf[:, pb, 0:T], wslot[wi][:, kc, hh * 128:(hh + 1) * 128], actT[:, kc, 0:T],
                                start=(kc == 0), stop=(kc == 15)),
                                reads=[t_w[wi], t_actT], writes=[t_psf[pb]])
                        si = nxt("stb", NST)
                        copy_op(evac_engine(), st_b[si][:, 0:T], psf[:, pb, 0:T], [t_psf[pb]], [t_stb[si]])
                        S.dma("sp", dstT[head, :, tok0:tok0 + T], st_b[si][:, 0:T], t_stb[si], False)
                if (sect == 1 and want_k_tm) or sect == 2:
                    for (boff, bsz) in blocks_of(T):
                        pb = nxt("ps", 6)
                        for kc in range(16):
                            S.op("pe", lambda gg, kc=kc, pb=pb, wi=wi, boff=boff, bsz=bsz: gg.matmul(
                                psf[0:bsz, pb, :], actT[:, kc, boff:boff + bsz], wslot[wi][:, kc, :],
                                start=(kc == 0), stop=(kc == 15)),
                                reads=[t_w[wi], t_actT], writes=[t_psf[pb]])
                        c0 = (g % 4) * 512
                        sf = nxt("stf", NST)
                        copy_op(evac_engine(), st_f[sf][0:bsz, :], psf[0:bsz, pb, :], [t_psf[pb]], [t_stf[sf]])
                        if sect == 2:
                            si = nxt("stb", NST)
                            copy_op(evac_engine(), st_b[si][0:bsz, :], st_f[sf][0:bsz, :], [t_stf[sf]], [t_stb[si]])
                            S.dma("sp", VB[tok0 + boff:tok0 + boff + bsz, c0:c0 + 512], st_b[si][0:bsz, :],
                                  t_stb[si], False)
                        if (sect == 1) or want_vo:
                            dst = kdst if sect == 1 else vdst
                            S.dma("sp", dst[row0 + boff:row0 + boff + bsz, c0:c0 + 512], st_f[sf][0:bsz, :],
                                  t_stf[sf], False)

    def layer_norm_block(bi, bsz, gsel):
        xr = xres[0:bsz, bi, :]
        for c in range(4):
            S.op("dve", lambda g, c=c: g.bn_stats(out=stats[0:bsz, c, :], in_=xres[0:bsz, bi, c * 512:(c + 1) * 512]),
                 reads=[t_xres[bi]], writes=[t_stats])
        S.op("dve", lambda g: g.bn_aggr(out=mv[0:bsz, :], in_=stats[0:bsz, :, :].rearrange("p a b -> p (a b)")),
             reads=[t_stats], writes=[t_stats])
        S.op("dve", lambda g: g.tensor_scalar(out=rstd[0:bsz, :], in0=mv[0:bsz, 1:2], scalar1=LN_EPS, scalar2=None,
                                              op0=ALU.add),
             reads=[t_stats], writes=[t_stats])
        S.op("act", lambda g: g.activation(out=rstd[0:bsz, :], in_=rstd[0:bsz, :], func=AF.Sqrt),
             reads=[t_stats], writes=[t_stats])
        S.op("dve", lambda g: g.reciprocal(out=rstd[0:bsz, :], in_=rstd[0:bsz, :]),
             reads=[t_stats], writes=[t_stats])
        S.op("dve", lambda g: g.tensor_scalar(out=xr, in0=xr, scalar1=mv[0:bsz, 0:1], scalar2=rstd[0:bsz, 0:1],
                                              op0=ALU.subtract, op1=ALU.mult),
             reads=[t_xres[bi], t_stats], writes=[t_xres[bi]])
        S.op("dve", lambda g: g.tensor_tensor(out=xr, in0=xr, in1=H['gb'][0:bsz, 0, :], op=ALU.mult),
             reads=[t_xres[bi], t_gb], writes=[t_xres[bi]])
        S.op("dve", lambda g: g.tensor_tensor(out=xr, in0=xr, in1=H['gb'][0:bsz, 1, :], op=ALU.add),
             reads=[t_xres[bi], t_gb], writes=[t_xres[bi]])
        S.op("act", lambda g: g.copy(out=xb16[0:bsz, :], in_=xr), reads=[t_xres[bi]], writes=[t_xb16])

    def load_gb(gvec, bvec):
        S.dma("sp", H["gb"][:, 0, :], gvec.partition_broadcast(128), t_gb, True)
        S.dma("sp", H["gb"][:, 1, :], bvec.partition_broadcast(128), t_gb, True)

    def phase3(li, last):
        kind, j = KINDS[li], LJ[li]
        wo = w_out[li]
        for (tok0, T) in TILES:
            blks = blocks_of(T)
            rsrc_p, rsrc_s = (xp, xs) if li == 0 else (yp, ys)
            for bi, (boff, bsz) in enumerate(blks):
                S.dma("sp", xres[0:bsz, bi, :], tokmajor_dram(tok0 + boff, bsz, rsrc_p, rsrc_s), t_xres[bi], True)
            S.dma("sp", actT[:, :, 0:T], OT[:, :, tok0:tok0 + T].rearrange("k p t -> p k t"), t_actT, True)
            load_gb(ln1_g[li], ln1_b[li])
            for g in range(4):
                wi = load_wgroup(wo[:, g * 512:(g + 1) * 512], wb_out[li][:, g * 512:(g + 1) * 512], 16, tok0 == TILES[0][0])
                for bi, (boff, bsz) in enumerate(blks):
                    pb = nxt("ps", 6)
                    for kc in range(16):
                        S.op("pe", lambda gg, kc=kc, pb=pb, wi=wi, boff=boff, bsz=bsz: gg.matmul(
                            psf[0:bsz, pb, :], actT[:, kc, boff:boff + bsz], wslot[wi][:, kc, :],
                            start=(kc == 0), stop=(kc == 15)),
                            reads=[t_w[wi], t_actT], writes=[t_psf[pb]])
                    xr = xres[0:bsz, bi, g * 512:(g + 1) * 512]
                    S.op("dve", lambda gg, xr=xr, pb=pb, bsz=bsz: gg.scalar_tensor_tensor(
                        out=xr, in0=xr, scalar=ALPHA, in1=psf[0:bsz, pb, :], op0=ALU.mult, op1=ALU.add),
                        reads=[t_psf[pb], t_xres[bi]], writes=[t_xres[bi]])
            for bi, (boff, bsz) in enumerate(blks):
                layer_norm_block(bi, bsz, 0)
                transpose_to_actT(xb16, t_xb16, bsz, boff)
            load_gb(ln2_g[li], ln2_b[li])
            for g in range(DFF // 512):
                wg = load_wgroup(w_gate[li][:, g * 512:(g + 1) * 512], wb_gate[li][:, g * 512:(g + 1) * 512], 16, tok0 == TILES[0][0])
                wu = load_wgroup(w_up[li][:, g * 512:(g + 1) * 512], wb_up[li][:, g * 512:(g + 1) * 512], 16, tok0 == TILES[0][0])
                for cc in range(4):
                    fc = g * 4 + cc
                    pg = nxt("ps", 6)
                    for kc in range(16):
                        S.op("pe", lambda gg, kc=kc, pg=pg, wg=wg, cc=cc: gg.matmul(
                            psf[:, pg, 0:T], wslot[wg][:, kc, cc * 128:(cc + 1) * 128], actT[:, kc, 0:T],
                            start=(kc == 0), stop=(kc == 15)),
                            reads=[t_w[wg], t_actT], writes=[t_psf[pg]])
                    pu = nxt("ps", 6)
                    for kc in range(16):
                        S.op("pe", lambda gg, kc=kc, pu=pu, wu=wu, cc=cc: gg.matmul(
                            psf[:, pu, 0:T], wslot[wu][:, kc, cc * 128:(cc + 1) * 128], actT[:, kc, 0:T],
                            start=(kc == 0), stop=(kc == 15)),
                            reads=[t_w[wu], t_actT], writes=[t_psf[pu]])
                    si = nxt("stf", NST)
                    S.op("act", lambda gg, si=si, pg=pg: gg.activation(out=st_f[si][:, 0:T], in_=psf[:, pg, 0:T],
                                                                      func=AF.Silu),
                         reads=[t_psf[pg]], writes=[t_stf[si]])
                    S.op("dve", lambda gg, si=si, pu=pu, fc=fc: gg.tensor_tensor(
                        out=H['hT'][:, fc, 0:T], in0=st_f[si][:, 0:T], in1=psf[:, pu, 0:T], op=ALU.mult),
                        reads=[t_stf[si], t_psf[pu]], writes=[t_hT[fc]])
            for g in range(4):
                pbs = [nxt("ps", 6) for _ in blks]
                for pc in range(4):
                    wi = load_wgroup(w_down[li][pc * 1408:(pc + 1) * 1408, g * 512:(g + 1) * 512],
                                     wb_down[li][pc * 1408:(pc + 1) * 1408, g * 512:(g + 1) * 512], 11, tok0 == TILES[0][0])
                    for bi, (boff, bsz) in enumerate(blks):
                        pb = pbs[bi]
                        for k in range(11):
                            fc = pc * 11 + k
                            S.op("pe", lambda gg, k=k, fc=fc, pb=pb, wi=wi, boff=boff, bsz=bsz: gg.matmul(
                                psf[0:bsz, pb, :], H['hT'][:, fc, boff:boff + bsz], wslot[wi][:, k, :],
                                start=(fc == 0), stop=(fc == NFC - 1)),
                                reads=[t_w[wi], t_hT[fc]], writes=[t_psf[pb]])
                for bi, (boff, bsz) in enumerate(blks):
                    pb = pbs[bi]
                    xr = xres[0:bsz, bi, g * 512:(g + 1) * 512]
                    S.op("dve", lambda gg, xr=xr, pb=pb, bsz=bsz: gg.scalar_tensor_tensor(
                        out=xr, in0=xr, scalar=ALPHA, in1=psf[0:bsz, pb, :], op0=ALU.mult, op1=ALU.add),
                        reads=[t_psf[pb], t_xres[bi]], writes=[t_xres[bi]])
            for bi, (boff, bsz) in enumerate(blks):
                layer_norm_block(bi, bsz, 1)
                S.dma("sp", tokmajor_dram(tok0 + boff, bsz, yp, ys), xres[0:bsz, bi, :], t_xres[bi], False)
                if not last:
                    transpose_to_actT(xb16, t_xb16, bsz, boff)
            if not last:
                store_actT_to_XT(tok0, T)

    t_kT, t_qT, t_vh, t_vs, t_bias, t_biass = Tok(), Tok(), Tok(), Tok(), Tok(), Tok()
    t_kcr, t_kcT, t_vc, t_cb, t_lam, t_cst = Tok(), Tok(), Tok(), Tok(), Tok(), Tok()
    t_wf = [Tok() for _ in range(6)]
    t_wb = [Tok() for _ in range(8)]
    t_ob = [Tok() for _ in range(2)]
    t_o = [Tok() for _ in range(3)]
    rr.update({"wf": 0, "wb": 0, "pss": 0, "ob": 0})
    LAM_INIT = [0.8 - 0.6 * math.exp(-0.3 * i) for i in range(DEPTH)]

    def phase2(li, st):
        kind, j = KINDS[li], LJ[li]

        def a_sb(name, shape, dt):
            return st.enter_context(nc.sbuf_tensor("s_%s_%d" % (name, li), list(shape), dt))
        kT = a_sb("kT", [128, NTOK], BF16)
        qT = a_sb("qT", [128, NTOK], BF16)
        vh = a_sb("vh", [128, 32, 128], BF16)
        vs = a_sb("vs", [32, 2, 128], BF16)
        bias = a_sb("bias", [128, 8, 512], F32)
        biass = a_sb("biass", [128, 9, 32], F32)
        kcr = a_sb("kcr", [128, 8, 128], BF16)
        kcT = a_sb("kcT", [128, 1024], BF16)
        vc = a_sb("vc", [128, 8, 128], BF16)
        wf = [st_f[i] for i in range(4)]
        wb = [st_b[i] for i in range(4)]
        twf = t_stf
        twb = t_stb
        if kind == 2:
            FS = [st_f[i] for i in range(4)]
            tFS = t_stf
            FL = [a_sb("FL%d" % i, [128, 512], F32) for i in range(2)]
            tFL = t_wf[0:2]
            HI = [st_b[0], st_b[1], a_sb("HI2", [128, 512], BF16)]
            tHI = [t_stb[0], t_stb[1], t_wb[0]]
            LO = [st_b[2], st_b[3], a_sb("LO2", [128, 512], BF16)]
            tLO = [t_stb[2], t_stb[3], t_wb[1]]
            AI = [a_sb("AI%d" % i, [128, 512], BF16) for i in range(2)]
            tAI = t_wb[2:4]
        ob = [a_sb("ob%d" % i, [128, 512], BF16) for i in range(2)]
        osum = [a_sb("osum%d" % i, [128, 512], F32) for i in range(3)]
        umat = a_sb("umat", [128, 128], BF16)
        ones = a_sb("ones", [128, 128], BF16)
        onesf = a_sb("onesf", [128, 128], F32)
        lam = a_sb("lam", [128, 8], F32)
        lvec = a_sb("lvec", [128, 4, 64], F32)
        gcol = a_sb("gcol", [128, 1], F32)
        S.dma("pool", umat[:], umat_d[:, :], t_cst, True)
        S.dma("pool", ones[:], ones_d[:, :], t_cst, True)
        S.dma("sp", onesf[:], ones_d[:, :], t_cst, True)
        NCB = 4 if kind == 0 else 8
        ck, cv = [(ca_k, ca_v), (cb_k, cb_v), (cc_k, cc_v)][kind]
        scale = (HD ** -0.5) if kind != 1 else (64 ** -0.5)

        if kind == 1:
            for i in range(4):
                S.dma("sp", lvec[:, i, :], lam_d[i].partition_broadcast(128), t_lam, True)
            S.dma("sp", gcol[:, :], dng_d[:, :], t_lam, True)
            S.op("dve", lambda g: g.tensor_tensor(out=lvec[:, 0, :], in0=lvec[:, 0, :], in1=lvec[:, 1, :], op=ALU.mult),
                 [t_lam], [t_lam])
            S.op("dve", lambda g: g.tensor_tensor(out=lvec[:, 2, :], in0=lvec[:, 2, :], in1=lvec[:, 3, :], op=ALU.mult),
                 [t_lam], [t_lam])
            S.op("dve", lambda g: g.reduce_sum(out=lam[:, 0:1], in_=lvec[:, 0, :], axis=mybir.AxisListType.X),
                 [t_lam], [t_lam])
            S.op("dve", lambda g: g.reduce_sum(out=lam[:, 1:2], in_=lvec[:, 2, :], axis=mybir.AxisListType.X),
                 [t_lam], [t_lam])
            S.op("act", lambda g: g.activation(out=lam[:, 0:2], in_=lam[:, 0:2], func=AF.Exp), [t_lam], [t_lam])
            S.op("dve", lambda g: g.tensor_tensor(out=lam[:, 2:3], in0=lam[:, 1:2], in1=lam[:, 0:1], op=ALU.subtract),
                 [t_lam], [t_lam])
            S.op("dve", lambda g: g.tensor_scalar(out=lam[:, 2:3], in0=lam[:, 2:3], scalar1=-LAM_INIT[li], scalar2=None,
                                                  op0=ALU.add), [t_lam], [t_lam])
            S.op("dve", lambda g: g.tensor_scalar(out=gcol[:, :], in0=gcol[:, :], scalar1=1.0 - LAM_INIT[li],
                                                  scalar2=None, op0=ALU.mult), [t_lam], [t_lam])

        rtmp = a_sb("rtmp", [128, 512], F32)
        t_rtmp = Tok()

        def fin_sm(po, pr, N):
            oi = nxt("ob", 3)
            S.op("dve", lambda g: g.reciprocal(out=rtmp[:, 0:N], in_=psf[:, pr, 0:N]), [t_psf[pr]], [t_rtmp])
            S.op("dve", lambda g: g.tensor_tensor(out=osum[oi][:, 0:N], in0=psf[:, po, 0:N], in1=rtmp[:, 0:N], op=ALU.mult),
                 [t_psf[po], t_rtmp], [t_o[oi]])
            return oi

        def run_softmax(B):
            nb = len(B)
            pend = {}
            k = -2
            while k < nb + 1 or pend:
                i = k + 2
                if 0 <= i < nb:
                    b = B[i]
                    nk, N, pss = b["nk"], b["N"], i % 2
                    S.op("pe", lambda g: g.matmul(psf[0:nk, pss, 0:N], b["kap"], b["qap"], start=True, stop=True),
                         [t_kT, t_qT, t_kcT], [t_psf[pss]])
                i = k + 1
                if 0 <= i < nb:
                    b = B[i]
                    nk, N, pss, fi = b["nk"], b["N"], i % 2, i % 4
                    S.op("dve", lambda g: g.scalar_tensor_tensor(out=wf[fi][0:nk, 0:N], in0=psf[0:nk, pss, 0:N],
                                                                 scalar=scale, in1=b["bap"], op0=ALU.mult, op1=ALU.add),
                         [t_psf[pss], t_bias, t_biass], [twf[fi]])
                i = k
                if 0 <= i < nb:
                    b = B[i]
                    nk, N, fi, pi = b["nk"], b["N"], i % 4, i % 4
                    S.op("act", lambda g: g.activation(out=wb[pi][0:nk, 0:N], in_=wf[fi][0:nk, 0:N], func=AF.Exp),
                         [twf[fi]], [twb[pi]])
                for fn in pend.pop(k, []):
                    fn()
                i = k - 1
                if 0 <= i < nb:
                    b = B[i]
                    nk, N, pi = b["nk"], b["N"], i % 4
                    po, pr = b["banks"]
                    S.op("pe", lambda g: g.matmul(psf[:, po, 0:N], b["vap"], wb[pi][0:nk, 0:N], start=b["first"],
                                                  stop=b["last"]),
                         [twb[pi], t_vh, t_vs, t_vc], [t_psf[po]])
                    S.op("pe", lambda g: g.matmul(psf[:, pr, 0:N], ones[0:nk, :], wb[pi][0:nk, 0:N], start=b["first"],
                                                  stop=b["last"]),
                         [twb[pi], t_cst], [t_psf[pr]])
                    if b["last"]:
                        for si, fn in enumerate(b["fin"]):
                            pend.setdefault(k + 2 + si, []).append(fn)
                k += 1

        def sm_group(qap, N, blocks, dsl, banks, fin):
            out = []
            nb = len(blocks)
            for bi, (kap, vap, nk, bap) in enumerate(blocks):
                out.append(dict(kap=kap[dsl, :], qap=qap[dsl, :], vap=vap, nk=nk, N=N, bap=bap,
                                first=(bi == 0), last=(bi == nb - 1), banks=banks, fin=fin))
            return out

        def stages_A(banks, N, dst_tok0, h):
            st = {}

            def s0():
                st["oi"] = fin_sm(banks[0], banks[1], N)

            def s1():
                oi = st["oi"]
                bi = (rr.__setitem__("obb", (rr.get("obb", 0) + 1) % 2) or rr["obb"])
                S.op("act", lambda g: g.activation(out=ob[bi][:, 0:N], in_=osum[oi][:, 0:N], func=AF.Identity),
                     [t_o[oi]], [t_ob[bi]])
                S.dma("sp", OT[h, :, dst_tok0:dst_tok0 + N], ob[bi][:, 0:N], t_ob[bi], False)
            return [s0, s1]

        def stages_B0(st, N):
            def s0():
                st[0] = fin_sm(2, 3, N)
            return [s0]

        def stages_B1(st, N, dst_tok0, h):
            def s0():
                st[1] = fin_sm(4, 5, N)

            def s1():
                o0, o1 = st[0], st[1]
                S.op("dve", lambda g: g.scalar_tensor_tensor(out=osum[o0][:, 0:N], in0=osum[o1][:, 0:N], scalar=lam[:, 2:3],
                                                             in1=osum[o0][:, 0:N], op0=ALU.mult, op1=ALU.add),
                     [t_o[o0], t_o[o1], t_lam], [t_o[o0]])
                S.op("dve", lambda g: g.tensor_tensor(out=osum[o1][:, 0:N], in0=osum[o0][:, 0:N], in1=osum[o0][:, 0:N],
                                                      op=ALU.mult), [t_o[o0]], [t_o[o1]])
                S.op("pe", lambda g: g.matmul(psf[:, 5, 0:N], onesf[:, :], osum[o1][:, 0:N], start=True, stop=True),
                     [t_o[o1], t_cst], [t_psf[5]])

            def s2():
                o0, o1 = st[0], st[1]
                S.op("dve", lambda g: g.tensor_scalar(out=osum[o1][:, 0:N], in0=psf[:, 5, 0:N], scalar1=1.0 / HD,
                                                      scalar2=RMS_EPS, op0=ALU.mult, op1=ALU.add), [t_psf[5]], [t_o[o1]])
                S.op("act", lambda g: g.activation(out=osum[o1][:, 0:N], in_=osum[o1][:, 0:N], func=AF.Sqrt),
                     [t_o[o1]], [t_o[o1]])

            def s3():
                o0, o1 = st[0], st[1]
                S.op("dve", lambda g: g.reciprocal(out=osum[o1][:, 0:N], in_=osum[o1][:, 0:N]), [t_o[o1]], [t_o[o1]])
                S.op("dve", lambda g: g.tensor_tensor(out=osum[o0][:, 0:N], in0=osum[o0][:, 0:N], in1=osum[o1][:, 0:N],
                                                      op=ALU.mult), [t_o[o0], t_o[o1]], [t_o[o0]])
                bi = (rr.__setitem__("obb", (rr.get("obb", 0) + 1) % 2) or rr["obb"])
                S.op("dve", lambda g: g.tensor_scalar(out=ob[bi][:, 0:N], in0=osum[o0][:, 0:N], scalar1=gcol[:, 0:1],
                                                      scalar2=None, op0=ALU.mult), [t_o[o0], t_lam], [t_ob[bi]])
                S.dma("sp", OT[h, :, dst_tok0:dst_tok0 + N], ob[bi][:, 0:N], t_ob[bi], False)
            return [s0, s1, s2, s3]

        def run_stick(B):
            nb = len(B)
            po, pu, pc = 3, 4, 5
            for k in range(-3, nb + 1):
                i = k + 3
                if 0 <= i < nb:
                    b = B[i]
                    nk, N, pss = b["nk"], b["N"], i % 3
                    S.op("pe", lambda g: g.matmul(psf[0:nk, pss, 0:N], b["kap"], b["qap"], start=True, stop=True),
                         [t_kT, t_qT, t_kcT], [t_psf[pss]])
                i = k + 2
                if 0 <= i < nb:
                    b = B[i]
                    nk, N, pss, fs = b["nk"], b["N"], i % 3, i % 4
                    S.op("act", lambda g: g.activation(out=FS[fs][0:nk, 0:N], in_=psf[0:nk, pss, 0:N], func=AF.Exp,
                                                       scale=-scale), [t_psf[pss]], [tFS[fs]])
                    S.op("act", lambda g: g.activation(out=FS[fs][0:nk, 0:N], in_=FS[fs][0:nk, 0:N], func=AF.Ln, bias=1.0),
                         [tFS[fs]], [tFS[fs]])
                i = k + 1
                if 0 <= i < nb:
                    b = B[i]
                    nk, N, pss, fs, fl, hl = b["nk"], b["N"], i % 3, i % 4, i % 2, i % 3
                    S.op("dve", lambda g: g.scalar_tensor_tensor(out=FL[fl][0:nk, 0:N], in0=psf[0:nk, pss, 0:N],
                                                                 scalar=-scale, in1=FS[fs][0:nk, 0:N], op0=ALU.mult,
                                                                 op1=ALU.subtract),
                         [t_psf[pss], tFS[fs]], [tFL[fl]])
                    if b["m01"] is not None:
                        S.op("dve", lambda g: g.tensor_tensor(out=FL[fl][0:nk, 0:N], in0=FL[fl][0:nk, 0:N], in1=b["m01"],
                                                              op=ALU.mult), [tFL[fl], t_bias, t_biass], [tFL[fl]])
                    S.op("act", lambda g: g.activation(out=HI[hl][0:nk, 0:N], in_=FL[fl][0:nk, 0:N], func=AF.Identity),
                         [tFL[fl]], [tHI[hl]])
                    S.op("pool", lambda g: g.tensor_tensor(out=LO[hl][0:nk, 0:N], in0=FL[fl][0:nk, 0:N],
                                                           in1=HI[hl][0:nk, 0:N], op=ALU.subtract),
                         [tFL[fl], tHI[hl]], [tLO[hl]])
                i = k - 1
                if 0 <= i < nb:
                    b = B[i]
                    nk, N, fs = b["nk"], b["N"], i % 4
                    S.op("dve", lambda g: g.tensor_tensor(out=FS[fs][0:nk, 0:N], in0=psf[0:nk, pu, 0:N],
                                                          in1=FS[fs][0:nk, 0:N], op=ALU.subtract),
                         [t_psf[pu], tFS[fs]], [tFS[fs]])
                    if not b["first"]:
                        S.op("dve", lambda g: g.tensor_tensor(out=FS[fs][0:nk, 0:N], in0=FS[fs][0:nk, 0:N],
                                                              in1=psf[0:nk, pc, 0:N], op=ALU.add),
                             [tFS[fs], t_psf[pc]], [tFS[fs]])
                    if b["mneg"] is not None:
                        S.op("dve", lambda g: g.tensor_tensor(out=FS[fs][0:nk, 0:N], in0=FS[fs][0:nk, 0:N], in1=b["mneg"],
                                                              op=ALU.add), [tFS[fs], t_bias, t_biass], [tFS[fs]])
                i = k
                if 0 <= i < nb:
                    b = B[i]
                    nk, N, hl = b["nk"], b["N"], i % 3
                    S.op("pe", lambda g: g.matmul(psf[0:nk, pu, 0:N], umat[0:nk, 0:nk], HI[hl][0:nk, 0:N], start=True,
                                                  stop=False), [tHI[hl], t_cst], [t_psf[pu]])
                    S.op("pe", lambda g: g.matmul(psf[0:nk, pu, 0:N], umat[0:nk, 0:nk], LO[hl][0:nk, 0:N], start=False,
                                                  stop=True), [tLO[hl], t_cst], [t_psf[pu]])
                i = k - 1
                if 0 <= i < nb:
                    b = B[i]
                    nk, N, fs, hl, ai = b["nk"], b["N"], i % 4, i % 3, i % 2
                    S.op("act", lambda g: g.activation(out=AI[ai][0:nk, 0:N], in_=FS[fs][0:nk, 0:N], func=AF.Exp),
                         [tFS[fs]], [tAI[ai]])
                    if not b["last"]:
                        S.op("pe", lambda g: g.matmul(psf[:, pc, 0:N], ones[0:nk, :], HI[hl][0:nk, 0:N], start=b["first"],
                                                      stop=False), [tHI[hl], t_cst], [t_psf[pc]])
                        S.op("pe", lambda g: g.matmul(psf[:, pc, 0:N], ones[0:nk, :], LO[hl][0:nk, 0:N], start=False,
                                                      stop=True), [tLO[hl], t_cst], [t_psf[pc]])
                    S.op("pe", lambda g: g.matmul(psf[:, po, 0:N], b["vap"], AI[ai][0:nk, 0:N], start=b["first"],
                                                  stop=b["last"]),
                         [tAI[ai], t_vh, t_vs, t_vc], [t_psf[po]])
                    if b["last"]:
                        b["fin"]()

        def st_group(qap, N, blocks, fin):
            out = []
            nb = len(blocks)
            for bi, (kap, vap, nk, m01, mneg) in enumerate(reversed(blocks)):
                out.append(dict(kap=kap, qap=qap, vap=vap, nk=nk, N=N, m01=m01, mneg=mneg,
                                first=(bi == 0), last=(bi == nb - 1), fin=fin))
            return out

        def fin_stick(N, dst_tok0, h):
            bi2 = (rr.__setitem__("obb", (rr.get("obb", 0) + 1) % 2) or rr["obb"])
            S.op("act", lambda g: g.activation(out=ob[bi2][:, 0:N], in_=psf[:, 3, 0:N], func=AF.Identity),
                 [t_psf[3]], [t_ob[bi2]])
            S.dma("sp", OT[h, :, dst_tok0:dst_tok0 + N], ob[bi2][:, 0:N], t_ob[bi2], False)

        if kind == 2:
            S.dma("sp", bias[:, :, :], cmask_d[:, :, :], t_bias, True)
            S.dma("sp", biass[0:32, 0:2, :], cmasks_d[:, :, :], t_biass, True)

        for h in range(NH):
            hs = slice(h * 128, (h + 1) * 128)
            S.dma("sp", kT[:, :], KT[h, :, :], t_kT, True)
            S.dma("sp", qT[:, :], QT[h, :, :], t_qT, True)
            for vv in range(4):
                S.dma("sp", vh[:, vv * 8:(vv + 1) * 8, :],
                      VB[vv * 1024:(vv + 1) * 1024, hs].rearrange("(b p) d -> p b d", p=128), t_vh, True)
            S.dma("sp", vs[:, :, :], VB[SEQ:NTOK, hs].rearrange("(s p) d -> p s d", p=32), t_vs, True)
            if kind == 0:
                S.dma("sp", bias[:, :, :], biasA_d[j][h], t_bias, True)
                S.dma("sp", biass[:, 0:5, :], biasAs_d[j][h], t_biass, True)
            elif kind == 1:
                S.dma("sp", bias[:, 0:6, :], biasB_d[h], t_bias, True)
                S.dma("sp", biass[:, :, :], biasBs_d[h], t_biass, True)
            PB = []
            for t in range(SEQ // 512):
                qap = qT[:, t * 512:(t + 1) * 512]
                if kind == 0:
                    blocks = []
                    for kb in range(max(0, 4 * t - 4), 4 * t + 4):
                        blocks.append((kT[:, kb * 128:(kb + 1) * 128], vh[:, kb, :], 128, bias[:, kb - (4 * t - 4), :]))
                    banks = (2, 3) if t % 2 == 0 else (4, 5)
                    PB += sm_group(qap, 512, blocks, slice(0, 128), banks, stages_A(banks, 512, t * 512, h))
                elif kind == 1:
                    ois = {}
                    for half in range(2):
                        blocks = []
                        for kb in range(0, 4 * t + 4):
                            dl = 128 * kb - 512 * t
                            bidx = 0 if dl <= -256 else 1 + (dl + 128) // 128
                            blocks.append((kT[:, kb * 128:(kb + 1) * 128], vh[:, kb, :], 128, bias[:, bidx, :]))
                        banks = (2, 3) if half == 0 else (4, 5)
                        fin = stages_B0(ois, 512) if half == 0 else stages_B1(ois, 512, t * 512, h)
                        PB += sm_group(qap, 512, blocks, slice(64 * half, 64 * half + 64), banks, fin)
                else:
                    blocks = []
                    for kb in range(0, 4 * t + 4):
                        dl = 128 * kb - 512 * t
                        if dl >= 0:
                            blocks.append((kT[:, kb * 128:(kb + 1) * 128], vh[:, kb, :], 128,
                                           bias[:, dl // 128, :], bias[:, 4 + dl // 128, :]))
                        else:
                            blocks.append((kT[:, kb * 128:(kb + 1) * 128], vh[:, kb, :], 128, None, None))
                    PB += st_group(qap, 512, blocks, lambda t=t: fin_stick(512, t * 512, h))
            if kind == 2:
                run_stick(PB)
            else:
                run_softmax(PB)
            for sbi in range(2):
                rows = NCB * 128
                for cc2 in range(NCB // 2):
                    S.dma("pool", kcr[:, 2 * cc2:2 * cc2 + 2, :],
                          ck[j, sbi, 256 * cc2:256 * cc2 + 256, hs].rearrange("(b p) d -> p b d", p=128), t_kcr, True)
                    S.dma("pool", vc[:, 2 * cc2:2 * cc2 + 2, :],
                          cv[j, sbi, 256 * cc2:256 * cc2 + 256, hs].rearrange("(b p) d -> p b d", p=128), t_vc, True)
                for b4 in range(NCB // 4):
                    pb = nxt("pst", 2)
                    for jj in range(4):
                        b = b4 * 4 + jj
                        S.op("pe", lambda g, b=b, jj=jj, pb=pb: g.transpose(out=pst[:, pb, jj * 128:(jj + 1) * 128],
                                                                             in_=kcr[:, b, :], identity=ident[:, :]),
                             [t_kcr, t_ident], [t_pst[pb]])
                    copy_op(evac_engine(), kcT[:, b4 * 512:(b4 + 1) * 512], pst[:, pb, 0:512], [t_pst[pb]], [t_kcT])
                q0 = SEQ + 32 * sbi
                qap = qT[:, q0:q0 + 32]
                knew = kT[:, q0:q0 + 32]
                vnew = vs[:, sbi, :]
                if kind == 0 or kind == 1:
                    SB = []
                    ois = {}
                    for half in range(1 if kind == 0 else 2):
                        blocks = [(kcT[:, b * 128:(b + 1) * 128], vc[:, b, :], 128, biass[:, b, :]) for b in range(NCB)]
                        blocks.append((knew, vnew, 32, biass[0:32, NCB, :]))
                        dsl = slice(0, 128) if kind == 0 else slice(64 * half, 64 * half + 64)
                        banks = (2, 3) if half == 0 else (4, 5)
                        if kind == 0:
                            fin = stages_A((2, 3), 32, q0, h)
                        elif half == 0:
                            fin = stages_B0(ois, 32)
                        else:
                            fin = stages_B1(ois, 32, q0, h)
                        SB += sm_group(qap, 32, blocks, dsl, banks, fin)
                    run_softmax(SB)
                else:
                    blocks = [(kcT[:, b * 128:(b + 1) * 128], vc[:, b, :], 128, None, None) for b in range(NCB)]
                    blocks.append((knew, vnew, 32, biass[0:32, 0, :], biass[0:32, 1, :]))
                    run_stick(st_group(qap, 32, blocks, lambda q0=q0: fin_stick(32, q0, h)))

    if 0 in PHASES:
        phase0()
    S.barrier()
    for li in LAYER_SEL:
        if 1 in PHASES:
            phase1(li)
        S.barrier()
        if 2 in PHASES:
            with contextlib.ExitStack() as st2:
                phase2(li, st2)
                S.barrier()
        S.barrier()
        if 3 in PHASES:
            with contextlib.ExitStack() as st3:
                H["hT"] = st3.enter_context(nc.sbuf_tensor("s_hT_%d" % li, [128, NFC, 512], BF16))
                H["gb"] = st3.enter_context(nc.sbuf_tensor("s_gb_%d" % li, [128, 2, D], F32))
                phase3(li, li == LAYER_SEL[-1])
                S.barrier()
        S.barrier()
    S.final_wait()


def _t5_bucket(rel):
    half, max_exact = 16, 8
    n = np.abs(rel)
    nf = np.maximum(n, 1).astype(np.float32)
    large = max_exact + (np.log(nf / np.float32(max_exact)) / np.float32(math.log(128 / max_exact))
                         * np.float32(half - max_exact)).astype(np.int32)
    return np.where(rel > 0, half, 0) + np.where(n < max_exact, n, np.minimum(large, half - 1))


def _attention_constants(rel_bias_a, t5):
    p = np.arange(128)[:, None]
    out = {}
    neg_row = np.full((1, NH), NEG, np.float32)
    f = np.arange(512)[None, :]
    idxA = np.empty((8, 128, 512), np.int64)
    for i in range(8):
        ko = -512 + 128 * i + p
        rel = ko - f
        kc = np.floor_divide(ko, 64)
        qc = f // 64
        valid = (kc <= qc) & (kc >= qc - 8)
        idxA[i] = np.where(valid, np.clip(rel, -256, 256) + 256, 513)
    fs = np.arange(32)[None, :]
    idxAs = np.empty((5, 128, 32), np.int64)
    for b in range(5):
        kpos = (512 + 128 * b + p) if b < 4 else (1024 + p)
        idxAs[b] = np.clip(kpos - (1024 + fs), -256, 256) + 256
    bA = np.empty((2, NH, 128, 8, 512), np.float32)
    bAs = np.empty((2, NH, 128, 5, 32), np.float32)
    for j in range(2):
        tab = np.concatenate([rel_bias_a[j], neg_row], axis=0)
        bA[j] = np.transpose(tab[idxA], (3, 1, 0, 2))
        bAs[j] = np.transpose(tab[idxAs], (3, 1, 0, 2))
    out["biasA"] = bA
    out["biasAs"] = bAs
    tabB = np.concatenate([t5, neg_row], axis=0)
    idxB = np.empty((6, 128, 512), np.int64)
    idxB[0] = 15
    for i in range(1, 6):
        dl = -128 + 128 * (i - 1)
        ko = dl + p
        rel = ko - f
        valid = np.floor_divide(ko, 64) <= (f // 64)
        idxB[i] = np.where(valid, _t5_bucket(rel), 32)
    out["biasB"] = np.ascontiguousarray(np.transpose(tabB[idxB], (3, 1, 0, 2)))
    idxBs = np.empty((9, 128, 32), np.int64)
    for b in range(9):
        kpos = (128 * b + p) if b < 8 else (1024 + p)
        idxBs[b] = _t5_bucket(kpos - (1024 + fs))
    out["biasBs"] = np.ascontiguousarray(np.transpose(tabB[idxBs], (3, 1, 0, 2)))
    cm = np.empty((128, 8, 512), np.float32)
    for i in range(4):
        v = ((128 * i + p) < f).astype(np.float32)
        cm[:, i, :] = v
        cm[:, 4 + i, :] = (1.0 - v) * NEG
    out["cmask"] = cm
    ps = np.arange(32)[:, None]
    vs_ = (ps < fs).astype(np.float32)
    out["cmasks"] = np.ascontiguousarray(np.stack([vs_, (1.0 - vs_) * NEG], axis=1))
    out["umat"] = (np.arange(128)[:, None] > np.arange(128)[None, :]).astype(np.float32)
    out["ones"] = np.ones((128, 128), np.float32)
    return out

_NC_CACHE = {}


def _get_nc():
    if "nc" not in _NC_CACHE:
        _NC_CACHE["nc"] = build_program()
    return _NC_CACHE["nc"]


def kernel(x_prompt, x_sample, cache_a_k, cache_a_v, cache_b_k, cache_b_v, cache_c_k, cache_c_v,
           w_in_a, w_out_a, rel_bias_a, w_in_b, w_out_b, lambda_q1, lambda_k1, lambda_q2, lambda_k2,
           diff_norm_g, t5_bias, w_in_c, w_out_c, ln1_g, ln1_b, ln2_g, ln2_b, w_gate, w_up, w_down):
    f = lambda a: np.ascontiguousarray(np.asarray(a, dtype=np.float32))
    nc = _get_nc()
    wi_l = [w_in_a[0], w_in_b[0], w_in_c[0], w_in_a[1]]
    wo_l = [w_out_a[0], w_out_b[0], w_out_c[0], w_out_a[1]]
    shared = {
        "ln1_g": f(ln1_g), "ln1_b": f(ln1_b), "ln2_g": f(ln2_g), "ln2_b": f(ln2_b),
        "ident": np.eye(128, dtype=np.float32),
    }
    for l in LAYER_SEL:
        shared["w_in_L%d" % l] = f(wi_l[l])
        shared["w_out_L%d" % l] = f(wo_l[l])
        shared["w_gate_L%d" % l] = f(w_gate[l])
        shared["w_up_L%d" % l] = f(w_up[l])
        shared["w_down_L%d" % l] = f(w_down[l])
    shared.update(_attention_constants(np.asarray(rel_bias_a, dtype=np.float32), np.asarray(t5_bias, dtype=np.float32)))
    shared["lam"] = f(np.stack([np.asarray(lambda_q1)[0], np.asarray(lambda_k1)[0],
                                np.asarray(lambda_q2)[0], np.asarray(lambda_k2)[0]]))
    shared["dng"] = f(np.asarray(diff_norm_g)[0].reshape(128, 1))
    x_prompt = np.asarray(x_prompt)
    x_sample = np.asarray(x_sample)
    in_maps = []
    for c in range(N_CORES):
        m = dict(shared)
        m["xp"] = f(x_prompt[c])
        m["xs"] = f(x_sample[2 * c:2 * c + 2].reshape(NS, D))
        m["ca_k"] = f(np.asarray(cache_a_k)[:, 2 * c:2 * c + 2].reshape(2, 2, 512, D))
        m["ca_v"] = f(np.asarray(cache_a_v)[:, 2 * c:2 * c + 2].reshape(2, 2, 512, D))
        m["cb_k"] = f(np.asarray(cache_b_k)[:, 2 * c:2 * c + 2].reshape(1, 2, PAST, D))
        m["cb_v"] = f(np.asarray(cache_b_v)[:, 2 * c:2 * c + 2].reshape(1, 2, PAST, D))
        m["cc_k"] = f(np.asarray(cache_c_k)[:, 2 * c:2 * c + 2].reshape(1, 2, PAST, D))
        m["cc_v"] = f(np.asarray(cache_c_v)[:, 2 * c:2 * c + 2].reshape(1, 2, PAST, D))
        in_maps.append(m)
    res = run_bass_kernel_spmd(nc, in_maps, core_ids=list(range(N_CORES)))
    R = list(res.results)
    while len(R) < 8:
        R.append(R[0])

    def cat_prompt(name, lead):
        arr = np.stack([R[c][name] for c in range(8)], axis=1)
        return arr.reshape(arr.shape[0], 8, arr.shape[2], NH, HD)

    def cat_sample(name):
        arr = np.stack([R[c][name].reshape(-1, 2, 32, D) for c in range(8)], axis=1)
        return arr.reshape(arr.shape[0], 16, 32, NH, HD)

    y_p = np.stack([R[c]["yp"] for c in range(8)], axis=0)
    y_s = np.concatenate([R[c]["ys"].reshape(2, 32, D) for c in range(8)], axis=0)
    outs = (y_p, y_s,
            cat_prompt("pa_k", 2), cat_prompt("pa_v", 2),
            cat_prompt("pb_k", 1), cat_prompt("pb_v", 1),
            cat_prompt("pc_k", 1), cat_prompt("pc_v", 1),
            cat_sample("sa_k"), cat_sample("sa_v"),
            cat_sample("sb_k"), cat_sample("sb_v"),
            cat_sample("sc_k"), cat_sample("sc_v"))
    return tuple(np.ascontiguousarray(o, dtype=np.float32) for o in outs)
```

```python
import contextlib
import math
import numpy as np
import concourse.bass as bass
import concourse.mybir as mybir
from concourse.bass_utils import run_bass_kernel_spmd

F32 = mybir.dt.float32
BF16 = mybir.dt.bfloat16
AF = mybir.ActivationFunctionType
ALU = mybir.AluOpType

D = 2048
NH = 16
HD = 128
SEQ = 4096
NS = 64
NTOK = SEQ + NS
DFF = 5632
NFC = DFF // 128
DEPTH = 4
ALPHA = (2 * DEPTH) ** 0.25
LN_EPS = 1e-5
RMS_EPS = 1e-5
PAST = 1024
NEG = -30000.0
KINDS = [0, 1, 2, 0]
LJ = [0, 0, 0, 1]

LAYER_SEL = (0, 1, 2, 3)
N_CORES = 8
WCH = 2
DBG_NO_F32OUT = False
TILE_SEL = None
PHASES = (0, 1, 2, 3)


class Tok:
    __slots__ = ("w", "r", "sem", "cnt")

    def __init__(self):
        self.w = None
        self.r = {}
        self.sem = None
        self.cnt = 0


class Sched:
    def __init__(self, nc, stack):
        self.nc = nc
        self.stack = stack
        self.eng = {"pe": nc.tensor, "act": nc.scalar, "dve": nc.vector, "pool": nc.gpsimd, "sp": nc.sync}
        self.semobj = {}
        self.cnt = {}
        self.known = {}
        for e in self.eng:
            self.semobj[e] = stack.enter_context(nc.semaphore("sem_" + e))
            self.cnt[e] = 0
            self.known[e] = {}
        self.ndsem = 0
        self.dma_marks = {}

    def _wait(self, e, deps):
        kn = self.known[e]
        for sid, val in deps.items():
            if kn.get(sid, 0) < val:
                self.eng[e].wait_ge(self.semobj[sid], val)
                kn[sid] = val

    def _deps(self, e, reads, writes):
        deps = {}

        def add(mark, raw):
            sid, val = mark
            if sid == e and (e == "pe" or not raw):
                return
            if deps.get(sid, 0) < val:
                deps[sid] = val
        for t in reads:
            if t.w is not None:
                add(t.w, True)
        for t in writes:
            if t.w is not None:
                add(t.w, False)
            for k, v in t.r.items():
                add((k, v), False)
        return deps

    def op(self, e, fn, reads=(), writes=()):
        self._wait(e, self._deps(e, reads, writes))
        inst = fn(self.eng[e])
        self.cnt[e] += 1
        inst.then_inc(self.semobj[e], 1)
        c = self.cnt[e]
        for t in reads:
            t.r[e] = c
        for t in writes:
            t.w = (e, c)
            t.r = {}
        return inst

    def dma(self, q, out, in_, tok, is_write, extra_reads=(), extra_writes=()):
        if tok.sem is None:
            sid = "d%d" % self.ndsem
            self.ndsem += 1
            assert self.ndsem < 90, "too many dma sems"
            self.semobj[sid] = self.stack.enter_context(self.nc.semaphore("sem_" + sid))
            tok.sem = sid
        reads = list(extra_reads) + ([] if is_write else [tok])
        writes = list(extra_writes) + ([tok] if is_write else [])
        self._wait(q, self._deps(q, reads, writes))
        inst = self.eng[q].dma_start(out=out, in_=in_)
        tok.cnt += 16
        inst.then_inc(self.semobj[tok.sem], 16)
        mark = (tok.sem, tok.cnt)
        self.dma_marks[tok.sem] = tok.cnt
        for t in reads:
            t.r[tok.sem] = tok.cnt
        for t in writes:
            t.w = mark
            t.r = {}
        return inst

    def barrier(self):
        deps = {e: self.cnt[e] for e in self.eng if self.cnt[e] > 0}
        deps.update(self.dma_marks)
        for e in self.eng:
            d = {k: v for k, v in deps.items() if k != e}
            self._wait(e, d)

    def final_wait(self):
        self.barrier()


def build_program():
    nc = bass.Bass("TRN2", target_bir_lowering=False)
    stack = contextlib.ExitStack()
    with stack:
        _emit(nc, stack)
    return nc


def _emit(nc, stack):
    S = Sched(nc, stack)

    def din(name, shape, dt=F32):
        return nc.dram_tensor(name, list(shape), dt, kind="ExternalInput").ap()

    def dout(name, shape, dt=F32):
        return nc.dram_tensor(name, list(shape), dt, kind="ExternalOutput").ap()

    def dscr(name, shape, dt):
        return nc.dram_tensor(name, list(shape), dt, kind="Internal").ap()

    def sb(name, shape, dt):
        return stack.enter_context(nc.sbuf_tensor("s_" + name, list(shape), dt))

    def ps(name, shape, dt):
        return stack.enter_context(nc.psum_tensor("p_" + name, list(shape), dt))

    xp = din("xp", [SEQ, D])
    xs = din("xs", [NS, D])
    ca_k = din("ca_k", [2, 2, 512, D])
    ca_v = din("ca_v", [2, 2, 512, D])
    cb_k = din("cb_k", [1, 2, PAST, D])
    cb_v = din("cb_v", [1, 2, PAST, D])
    cc_k = din("cc_k", [1, 2, PAST, D])
    cc_v = din("cc_v", [1, 2, PAST, D])
    w_in = [din("w_in_L%d" % l, [D, 3 * D]) if l in LAYER_SEL else None for l in range(DEPTH)]
    w_out = [din("w_out_L%d" % l, [D, D]) if l in LAYER_SEL else None for l in range(DEPTH)]
    w_gate = [din("w_gate_L%d" % l, [D, DFF]) if l in LAYER_SEL else None for l in range(DEPTH)]
    w_up = [din("w_up_L%d" % l, [D, DFF]) if l in LAYER_SEL else None for l in range(DEPTH)]
    w_down = [din("w_down_L%d" % l, [DFF, D]) if l in LAYER_SEL else None for l in range(DEPTH)]
    ln1_g = din("ln1_g", [DEPTH, D])
    ln1_b = din("ln1_b", [DEPTH, D])
    ln2_g = din("ln2_g", [DEPTH, D])
    ln2_b = din("ln2_b", [DEPTH, D])
    ident_d = din("ident", [128, 128])
    umat_d = din("umat", [128, 128])
    ones_d = din("ones", [128, 128])
    biasA_d = din("biasA", [2, NH, 128, 8, 512])
    biasAs_d = din("biasAs", [2, NH, 128, 5, 32])
    biasB_d = din("biasB", [NH, 128, 6, 512])
    biasBs_d = din("biasBs", [NH, 128, 9, 32])
    cmask_d = din("cmask", [128, 8, 512])
    cmasks_d = din("cmasks", [32, 2, 32])
    lam_d = din("lam", [4, 64])
    dng_d = din("dng", [128, 1])

    yp = dout("yp", [SEQ, D])
    ys = dout("ys", [NS, D])
    pa_k = dout("pa_k", [2, 512, D])
    pa_v = dout("pa_v", [2, 512, D])
    pb_k = dout("pb_k", [1, SEQ, D])
    pb_v = dout("pb_v", [1, SEQ, D])
    pc_k = dout("pc_k", [1, SEQ, D])
    pc_v = dout("pc_v", [1, SEQ, D])
    sa_k = dout("sa_k", [2, NS, D])
    sa_v = dout("sa_v", [2, NS, D])
    sb_k = dout("sb_k", [1, NS, D])
    sb_v = dout("sb_v", [1, NS, D])
    sc_k = dout("sc_k", [1, NS, D])
    sc_v = dout("sc_v", [1, NS, D])
    pko = [pa_k, pb_k, pc_k]
    pvo = [pa_v, pb_v, pc_v]
    sko = [sa_k, sb_k, sc_k]
    svo = [sa_v, sb_v, sc_v]

    wb_in = [dscr("wb_in_L%d" % l, [D, 3 * D], BF16) if l in LAYER_SEL else None for l in range(DEPTH)]
    wb_out = [dscr("wb_out_L%d" % l, [D, D], BF16) if l in LAYER_SEL else None for l in range(DEPTH)]
    wb_gate = [dscr("wb_gate_L%d" % l, [D, DFF], BF16) if l in LAYER_SEL else None for l in range(DEPTH)]
    wb_up = [dscr("wb_up_L%d" % l, [D, DFF], BF16) if l in LAYER_SEL else None for l in range(DEPTH)]
    wb_down = [dscr("wb_down_L%d" % l, [DFF, D], BF16) if l in LAYER_SEL else None for l in range(DEPTH)]
    XT = dscr("XT", [16, 128, NTOK], BF16)
    QT = dscr("QT", [NH, 128, NTOK], BF16)
    KT = dscr("KT", [NH, 128, NTOK], BF16)
    VB = dscr("VB", [NTOK, D], BF16)
    OT = dscr("OT", [NH, 128, NTOK], BF16)

    ident = sb("ident", [128, 128], BF16)
    t_ident = Tok()
    NW = 3
    wslot = [sb("wslot%d" % i, [128, 16, 512], BF16) for i in range(NW)]
    t_w = [Tok() for _ in range(NW)]
    actT = sb("actT", [128, 16, 512], BF16)
    t_actT = Tok()
    xres = sb("xres", [128, 4, D], F32)
    t_xres = [Tok() for _ in range(4)]
    t_hT = [Tok() for _ in range(NFC)]
    t_gb = Tok()
    H = {}
    NST = 4
    st_f = [sb("stf%d" % i, [128, 512], F32) for i in range(NST)]
    t_stf = [Tok() for _ in range(NST)]
    st_b = [sb("stb%d" % i, [128, 512], BF16) for i in range(NST)]
    t_stb = [Tok() for _ in range(NST)]
    xb16 = sb("xb16", [128, D], BF16)
    t_xb16 = Tok()
    stats = sb("stats", [128, 4, 6], F32)
    mv = sb("mv", [128, 2], F32)
    rstd = sb("rstd", [128, 1], F32)
    t_stats = Tok()

    psf = ps("psf", [128, 6, 512], F32)
    t_psf = [Tok() for _ in range(6)]
    pst = ps("pst", [128, 2, 1024], BF16)
    t_pst = [Tok() for _ in range(2)]

    rr = {"ps": 0, "pst": 0, "stf": 0, "stb": 0, "w": 0, "ev": 0}

    def nxt(key, n):
        v = rr[key]
        rr[key] = (v + 1) % n
        return v

    def evac_engine():
        return "act" if nxt("ev", 2) == 0 else "dve"

    def copy_op(e, out, in_, reads, writes):
        if e == "act":
            S.op("act", lambda g: g.activation(out=out, in_=in_, func=AF.Identity), reads, writes)
        else:
            S.op(e, lambda g: g.tensor_copy(out=out, in_=in_), reads, writes)

    S.dma("pool", ident[:], ident_d[:, :], t_ident, True)

    TILES = [(i * 512, 512) for i in range(SEQ // 512)] + [(SEQ, NS)]
    if TILE_SEL is not None:
        TILES = [TILES[i] for i in TILE_SEL]

    def blocks_of(T):
        return [(b * 128, 128) for b in range(T // 128)] if T >= 128 else [(0, T)]

    def tokmajor_dram(tok0, n, prompt_ap, sample_ap):
        if tok0 < SEQ:
            return prompt_ap[tok0:tok0 + n, :]
        return sample_ap[tok0 - SEQ:tok0 - SEQ + n, :]

    def emit_wload(i, src_f32, dst_bf16, nk, first):
        if first:
            src = src_f32.rearrange("(k p) c -> p k c", p=128)
            k0 = 0
            while k0 < nk:
                k1 = min(nk, k0 + WCH)
                S.dma("pool", wslot[i][:, k0:k1, :], src[:, k0:k1, :], t_w[i], True)
                k0 = k1
        else:
            S.dma("sp", wslot[i][:, 0:nk, :], dst_bf16.rearrange("(k p) c -> p k c", p=128), t_w[i], True)

    class WStream:
        def __init__(self, specs):
            self.specs = specs
            self.issued = 0
            self.pos = 0

        def next(self):
            while self.issued < min(len(self.specs), self.pos + NW):
                emit_wload(self.issued % NW, *self.specs[self.issued])
                self.issued += 1
            i = self.pos % NW
            src_f32, dst_bf16, nk, first = self.specs[self.pos]
            if first:
                S.dma("sp", dst_bf16.rearrange("(k p) c -> p k c", p=128), wslot[i][:, 0:nk, :], t_w[i], False)
            self.pos += 1
            return i

    def transpose_to_actT(src_bf16, t_src, bsz, boff):
        for kc4 in range(4):
            pb = nxt("pst", 2)
            for j in range(4):
                kc = kc4 * 4 + j
                S.op("pe", lambda g, kc=kc, j=j, pb=pb: g.transpose(
                    out=pst[:, pb, j * 128:j * 128 + bsz], in_=src_bf16[0:bsz, kc * 128:(kc + 1) * 128],
                    identity=ident[0:bsz, 0:bsz]),
                    reads=[t_src, t_ident], writes=[t_pst[pb]])
            e = evac_engine()
            copy_op(e, actT[:, kc4 * 4:kc4 * 4 + 4, boff:boff + bsz],
                    pst[:, pb, 0:512].rearrange("p (j t) -> p j t", j=4)[:, :, 0:bsz],
                    [t_pst[pb]], [t_actT])

    def store_actT_to_XT(tok0, T):
        S.dma("sp", XT[:, :, tok0:tok0 + T].rearrange("k p t -> p k t"), actT[:, :, 0:T], t_actT, False)

    def phase0():
        for (tok0, T) in TILES:
            for bi, (boff, bsz) in enumerate(blocks_of(T)):
                src = tokmajor_dram(tok0 + boff, bsz, xp, xs)
                S.dma("sp", xres[0:bsz, bi, :], src, t_xres[bi], True)
                e = evac_engine()
                copy_op(e, xb16[0:bsz, :], xres[0:bsz, bi, :], [t_xres[bi]], [t_xb16])
                transpose_to_actT(xb16, t_xb16, bsz, boff)
            store_actT_to_XT(tok0, T)

    def phase1(li):
        kind, j = KINDS[li], LJ[li]
        win = w_in[li]
        ws = WStream([(win[:, g * 512:(g + 1) * 512], wb_in[li][:, g * 512:(g + 1) * 512], 16, ti == 0)
                      for ti in range(len(TILES)) for g in range(12)])
        for (tok0, T) in TILES:
            is_sample = tok0 >= SEQ
            S.dma("sp", actT[:, :, 0:T], XT[:, :, tok0:tok0 + T].rearrange("k p t -> p k t"), t_actT, True)
            if is_sample:
                kdst = sko[kind][j]
                vdst = svo[kind][j]
                row0 = 0
                want_k_tm = True
            elif kind == 0:
                want_k_tm = (tok0 == SEQ - 512)
                kdst = pko[0][j]
                vdst = pvo[0][j]
                row0 = 0
            else:
                want_k_tm = True
                kdst = pko[kind][0]
                vdst = pvo[kind][0]
                row0 = tok0
            want_vo = want_k_tm
            for g in range(12):
                wi = ws.next()
                sect = g // 4
                if sect < 2:
                    dstT = QT if sect == 0 else KT
                    for hh in range(4):
                        head = (g % 4) * 4 + hh
                        pb = nxt("ps", 6)
                        for kc in range(16):
                            S.op("pe", lambda gg, kc=kc, hh=hh, pb=pb, wi=wi: gg.matmul(
                                psf[:, pb, 0:T], wslot[wi][:, kc, hh * 128:(hh + 1) * 128], actT[:, kc, 0:T],
                                start=(kc == 0), stop=(kc == 15)),
                                reads=[t_w[wi], t_actT], writes=[t_psf[pb]])
                        si = nxt("stb", NST)
                        copy_op(evac_engine(), st_b[si][:, 0:T], psf[:, pb, 0:T], [t_psf[pb]], [t_stb[si]])
                        S.dma("sp", dstT[head, :, tok0:tok0 + T], st_b[si][:, 0:T], t_stb[si], False)
                if (sect == 1 and want_k_tm) or sect == 2:
                    for (boff, bsz) in blocks_of(T):
                        pb = nxt("ps", 6)
                        for kc in range(16):
                            S.op("pe", lambda gg, kc=kc, pb=pb, wi=wi, boff=boff, bsz=bsz: gg.matmul(
                                psf[0:bsz, pb, :], actT[:, kc, boff:boff + bsz], wslot[wi][:, kc, :],
                                start=(kc == 0), stop=(kc == 15)),
                                reads=[t_w[wi], t_actT], writes=[t_psf[pb]])
                        c0 = (g % 4) * 512
                        sf = nxt("stf", NST)
                        copy_op(evac_engine(), st_f[sf][0:bsz, :], psf[0:bsz, pb, :], [t_psf[pb]], [t_stf[sf]])
                        if sect == 2:
                            si = nxt("stb", NST)
                            copy_op(evac_engine(), st_b[si][0:bsz, :], st_f[sf][0:bsz, :], [t_stf[sf]], [t_stb[si]])
                            S.dma("sp", VB[tok0 + boff:tok0 + boff + bsz, c0:c0 + 512], st_b[si][0:bsz, :],
                                  t_stb[si], False)
                        if (sect == 1) or want_vo:
                            dst = kdst if sect == 1 else vdst
                            S.dma("sp", dst[row0 + boff:row0 + boff + bsz, c0:c0 + 512], st_f[sf][0:bsz, :],
                                  t_stf[sf], False)

    def layer_norm_block(bi, bsz, gsel):
        xr = xres[0:bsz, bi, :]
        for c in range(4):
            S.op("dve", lambda g, c=c: g.bn_stats(out=stats[0:bsz, c, :], in_=xres[0:bsz, bi, c * 512:(c + 1) * 512]),
                 reads=[t_xres[bi]], writes=[t_stats])
        S.op("dve", lambda g: g.bn_aggr(out=mv[0:bsz, :], in_=stats[0:bsz, :, :].rearrange("p a b -> p (a b)")),
             reads=[t_stats], writes=[t_stats])
        S.op("dve", lambda g: g.tensor_scalar(out=rstd[0:bsz, :], in0=mv[0:bsz, 1:2], scalar1=LN_EPS, scalar2=None,
                                              op0=ALU.add),
             reads=[t_stats], writes=[t_stats])
        S.op("act", lambda g: g.activation(out=rstd[0:bsz, :], in_=rstd[0:bsz, :], func=AF.Sqrt),
             reads=[t_stats], writes=[t_stats])
        S.op("dve", lambda g: g.reciprocal(out=rstd[0:bsz, :], in_=rstd[0:bsz, :]),
             reads=[t_stats], writes=[t_stats])
        S.op("dve", lambda g: g.tensor_scalar(out=xr, in0=xr, scalar1=mv[0:bsz, 0:1], scalar2=rstd[0:bsz, 0:1],
                                              op0=ALU.subtract, op1=ALU.mult),
             reads=[t_xres[bi], t_stats], writes=[t_xres[bi]])
        S.op("dve", lambda g: g.tensor_tensor(out=xr, in0=xr, in1=H['gb'][0:bsz, 0, :], op=ALU.mult),
             reads=[t_xres[bi], t_gb], writes=[t_xres[bi]])
        S.op("dve", lambda g: g.tensor_tensor(out=xr, in0=xr, in1=H['gb'][0:bsz, 1, :], op=ALU.add),
             reads=[t_xres[bi], t_gb], writes=[t_xres[bi]])
        S.op("act", lambda g: g.copy(out=xb16[0:bsz, :], in_=xr), reads=[t_xres[bi]], writes=[t_xb16])

    def load_gb(gvec, bvec):
        S.dma("sp", H["gb"][:, 0, :], gvec.partition_broadcast(128), t_gb, True)
        S.dma("sp", H["gb"][:, 1, :], bvec.partition_broadcast(128), t_gb, True)

    def phase3(li, last):
        kind, j = KINDS[li], LJ[li]
        wo = w_out[li]
        specs = []
        for ti in range(len(TILES)):
            f = (ti == 0)
            for g in range(4):
                specs.append((wo[:, g * 512:(g + 1) * 512], wb_out[li][:, g * 512:(g + 1) * 512], 16, f))
            for g in range(DFF // 512):
                specs.append((w_gate[li][:, g * 512:(g + 1) * 512], wb_gate[li][:, g * 512:(g + 1) * 512], 16, f))
                specs.append((w_up[li][:, g * 512:(g + 1) * 512], wb_up[li][:, g * 512:(g + 1) * 512], 16, f))
            for g in range(4):
                for pc in range(4):
                    specs.append((w_down[li][pc * 1408:(pc + 1) * 1408, g * 512:(g + 1) * 512],
                                  wb_down[li][pc * 1408:(pc + 1) * 1408, g * 512:(g + 1) * 512], 11, f))
        ws = WStream(specs)
        for (tok0, T) in TILES:
            blks = blocks_of(T)
            rsrc_p, rsrc_s = (xp, xs) if li == 0 else (yp, ys)
            for bi, (boff, bsz) in enumerate(blks):
                S.dma("sp", xres[0:bsz, bi, :], tokmajor_dram(tok0 + boff, bsz, rsrc_p, rsrc_s), t_xres[bi], True)
            S.dma("sp", actT[:, :, 0:T], OT[:, :, tok0:tok0 + T].rearrange("k p t -> p k t"), t_actT, True)
            load_gb(ln1_g[li], ln1_b[li])
            for g in range(4):
                wi = ws.next()
                for bi, (boff, bsz) in enumerate(blks):
                    pb = nxt("ps", 6)
                    for kc in range(16):
                        S.op("pe", lambda gg, kc=kc, pb=pb, wi=wi, boff=boff, bsz=bsz: gg.matmul(
                            psf[0:bsz, pb, :], actT[:, kc, boff:boff + bsz], wslot[wi][:, kc, :],
                            start=(kc == 0), stop=(kc == 15)),
                            reads=[t_w[wi], t_actT], writes=[t_psf[pb]])
                    xr = xres[0:bsz, bi, g * 512:(g + 1) * 512]
                    S.op("dve", lambda gg, xr=xr, pb=pb, bsz=bsz: gg.scalar_tensor_tensor(
                        out=xr, in0=xr, scalar=ALPHA, in1=psf[0:bsz, pb, :], op0=ALU.mult, op1=ALU.add),
                        reads=[t_psf[pb], t_xres[bi]], writes=[t_xres[bi]])
            for bi, (boff, bsz) in enumerate(blks):
                layer_norm_block(bi, bsz, 0)
                transpose_to_actT(xb16, t_xb16, bsz, boff)
            load_gb(ln2_g[li], ln2_b[li])
            for g in range(DFF // 512):
                wg = ws.next()
                sis = []
                for cc in range(4):
                    pg = nxt("ps", 6)
                    for kc in range(16):
                        S.op("pe", lambda gg, kc=kc, pg=pg, wg=wg, cc=cc: gg.matmul(
                            psf[:, pg, 0:T], wslot[wg][:, kc, cc * 128:(cc + 1) * 128], actT[:, kc, 0:T],
                            start=(kc == 0), stop=(kc == 15)),
                            reads=[t_w[wg], t_actT], writes=[t_psf[pg]])
                    si = nxt("stf", NST)
                    sis.append(si)
                    S.op("act", lambda gg, si=si, pg=pg: gg.activation(out=st_f[si][:, 0:T], in_=psf[:, pg, 0:T],
                                                                      func=AF.Silu),
                         reads=[t_psf[pg]], writes=[t_stf[si]])
                wu = ws.next()
                for cc in range(4):
                    fc = g * 4 + cc
                    si = sis[cc]
                    pu = nxt("ps", 6)
                    for kc in range(16):
                        S.op("pe", lambda gg, kc=kc, pu=pu, wu=wu, cc=cc: gg.matmul(
                            psf[:, pu, 0:T], wslot[wu][:, kc, cc * 128:(cc + 1) * 128], actT[:, kc, 0:T],
                            start=(kc == 0), stop=(kc == 15)),
                            reads=[t_w[wu], t_actT], writes=[t_psf[pu]])
                    S.op("dve", lambda gg, si=si, pu=pu, fc=fc: gg.tensor_tensor(
                        out=H['hT'][:, fc, 0:T], in0=st_f[si][:, 0:T], in1=psf[:, pu, 0:T], op=ALU.mult),
                        reads=[t_stf[si], t_psf[pu]], writes=[t_hT[fc]])
            for g in range(4):
                pbs = [nxt("ps", 6) for _ in blks]
                for pc in range(4):
                    wi = ws.next()
                    for bi, (boff, bsz) in enumerate(blks):
                        pb = pbs[bi]
                        for k in range(11):
                            fc = pc * 11 + k
                            S.op("pe", lambda gg, k=k, fc=fc, pb=pb, wi=wi, boff=boff, bsz=bsz: gg.matmul(
                                psf[0:bsz, pb, :], H['hT'][:, fc, boff:boff + bsz], wslot[wi][:, k, :],
                                start=(fc == 0), stop=(fc == NFC - 1)),
                                reads=[t_w[wi], t_hT[fc]], writes=[t_psf[pb]])
                for bi, (boff, bsz) in enumerate(blks):
                    pb = pbs[bi]
                    xr = xres[0:bsz, bi, g * 512:(g + 1) * 512]
                    S.op("dve", lambda gg, xr=xr, pb=pb, bsz=bsz: gg.scalar_tensor_tensor(
                        out=xr, in0=xr, scalar=ALPHA, in1=psf[0:bsz, pb, :], op0=ALU.mult, op1=ALU.add),
                        reads=[t_psf[pb], t_xres[bi]], writes=[t_xres[bi]])
            for bi, (boff, bsz) in enumerate(blks):
                layer_norm_block(bi, bsz, 1)
                S.dma("sp", tokmajor_dram(tok0 + boff, bsz, yp, ys), xres[0:bsz, bi, :], t_xres[bi], False)
                if not last:
                    transpose_to_actT(xb16, t_xb16, bsz, boff)
            if not last:
                store_actT_to_XT(tok0, T)

    t_kT, t_qT, t_vh, t_vs, t_bias, t_biass = Tok(), Tok(), Tok(), Tok(), Tok(), Tok()
    t_kcr, t_kcT, t_vc, t_cb, t_lam, t_cst = Tok(), Tok(), Tok(), Tok(), Tok(), Tok()
    t_wf = [Tok() for _ in range(6)]
    t_wb = [Tok() for _ in range(8)]
    t_ob = [Tok() for _ in range(2)]
    t_o = [Tok() for _ in range(3)]
    rr.update({"wf": 0, "wb": 0, "pss": 0, "ob": 0})
    LAM_INIT = [0.8 - 0.6 * math.exp(-0.3 * i) for i in range(DEPTH)]

    def phase2(li, st):
        kind, j = KINDS[li], LJ[li]

        def a_sb(name, shape, dt):
            return st.enter_context(nc.sbuf_tensor("s_%s_%d" % (name, li), list(shape), dt))
        kT = a_sb("kT", [128, NTOK], BF16)
        qT = a_sb("qT", [128, NTOK], BF16)
        vh = a_sb("vh", [128, 32, 128], BF16)
        vs = a_sb("vs", [32, 2, 128], BF16)
        bias = a_sb("bias", [128, 8, 512], F32)
        biass = a_sb("biass", [128, 9, 32], F32)
        kcr = a_sb("kcr", [128, 8, 128], BF16)
        kcT = a_sb("kcT", [128, 1024], BF16)
        vc = a_sb("vc", [128, 8, 128], BF16)
        wf = [st_f[i] for i in range(4)]
        wb = [st_b[i] for i in range(4)]
        twf = t_stf
        twb = t_stb
        if kind == 2:
            FS = [st_f[i] for i in range(4)]
            tFS = t_stf
            FL = [a_sb("FL%d" % i, [128, 512], F32) for i in range(2)]
            tFL = t_wf[0:2]
            HI = [st_b[0], st_b[1], a_sb("HI2", [128, 512], BF16)]
            tHI = [t_stb[0], t_stb[1], t_wb[0]]
            LO = [st_b[2], st_b[3], a_sb("LO2", [128, 512], BF16)]
            tLO = [t_stb[2], t_stb[3], t_wb[1]]
            AI = [a_sb("AI%d" % i, [128, 512], BF16) for i in range(2)]
            tAI = t_wb[2:4]
        ob = [a_sb("ob%d" % i, [128, 512], BF16) for i in range(2)]
        osum = [a_sb("osum%d" % i, [128, 512], F32) for i in range(3)]
        umat = a_sb("umat", [128, 128], BF16)
        ones = a_sb("ones", [128, 128], BF16)
        onesf = a_sb("onesf", [128, 128], F32)
        lam = a_sb("lam", [128, 8], F32)
        lvec = a_sb("lvec", [128, 4, 64], F32)
        gcol = a_sb("gcol", [128, 1], F32)
        S.dma("pool", umat[:], umat_d[:, :], t_cst, True)
        S.dma("pool", ones[:], ones_d[:, :], t_cst, True)
        S.dma("sp", onesf[:], ones_d[:, :], t_cst, True)
        NCB = 4 if kind == 0 else 8
        ck, cv = [(ca_k, ca_v), (cb_k, cb_v), (cc_k, cc_v)][kind]
        scale = (HD ** -0.5) if kind != 1 else (64 ** -0.5)

        if kind == 1:
            for i in range(4):
                S.dma("sp", lvec[:, i, :], lam_d[i].partition_broadcast(128), t_lam, True)
            S.dma("sp", gcol[:, :], dng_d[:, :], t_lam, True)
            S.op("dve", lambda g: g.tensor_tensor(out=lvec[:, 0, :], in0=lvec[:, 0, :], in1=lvec[:, 1, :], op=ALU.mult),
                 [t_lam], [t_lam])
            S.op("dve", lambda g: g.tensor_tensor(out=lvec[:, 2, :], in0=lvec[:, 2, :], in1=lvec[:, 3, :], op=ALU.mult),
                 [t_lam], [t_lam])
            S.op("dve", lambda g: g.reduce_sum(out=lam[:, 0:1], in_=lvec[:, 0, :], axis=mybir.AxisListType.X),
                 [t_lam], [t_lam])
            S.op("dve", lambda g: g.reduce_sum(out=lam[:, 1:2], in_=lvec[:, 2, :], axis=mybir.AxisListType.X),
                 [t_lam], [t_lam])
            S.op("act", lambda g: g.activation(out=lam[:, 0:2], in_=lam[:, 0:2], func=AF.Exp), [t_lam], [t_lam])
            S.op("dve", lambda g: g.tensor_tensor(out=lam[:, 2:3], in0=lam[:, 1:2], in1=lam[:, 0:1], op=ALU.subtract),
                 [t_lam], [t_lam])
            S.op("dve", lambda g: g.tensor_scalar(out=lam[:, 2:3], in0=lam[:, 2:3], scalar1=-LAM_INIT[li], scalar2=None,
                                                  op0=ALU.add), [t_lam], [t_lam])
            S.op("dve", lambda g: g.tensor_scalar(out=gcol[:, :], in0=gcol[:, :], scalar1=1.0 - LAM_INIT[li],
                                                  scalar2=None, op0=ALU.mult), [t_lam], [t_lam])

        rtmp = a_sb("rtmp", [128, 512], F32)
        t_rtmp = Tok()

        def fin_sm(po, pr, N):
            oi = nxt("ob", 3)
            S.op("dve", lambda g: g.reciprocal(out=rtmp[:, 0:N], in_=psf[:, pr, 0:N]), [t_psf[pr]], [t_rtmp])
            S.op("dve", lambda g: g.tensor_tensor(out=osum[oi][:, 0:N], in0=psf[:, po, 0:N], in1=rtmp[:, 0:N], op=ALU.mult),
                 [t_psf[po], t_rtmp], [t_o[oi]])
            return oi

        def run_softmax(B):
            nb = len(B)
            pend = {}
            k = -2
            while k < nb + 1 or pend:
                i = k + 2
                if 0 <= i < nb:
                    b = B[i]
                    nk, N, pss = b["nk"], b["N"], i % 2
                    S.op("pe", lambda g: g.matmul(psf[0:nk, pss, 0:N], b["kap"], b["qap"], start=True, stop=True),
                         [t_kT, t_qT, t_kcT], [t_psf[pss]])
                i = k + 1
                if 0 <= i < nb:
                    b = B[i]
                    nk, N, pss, fi = b["nk"], b["N"], i % 2, i % 4
                    S.op("dve", lambda g: g.scalar_tensor_tensor(out=wf[fi][0:nk, 0:N], in0=psf[0:nk, pss, 0:N],
                                                                 scalar=scale, in1=b["bap"], op0=ALU.mult, op1=ALU.add),
                         [t_psf[pss], t_bias, t_biass], [twf[fi]])
                i = k
                if 0 <= i < nb:
                    b = B[i]
                    nk, N, fi, pi = b["nk"], b["N"], i % 4, i % 4
                    S.op("act", lambda g: g.activation(out=wb[pi][0:nk, 0:N], in_=wf[fi][0:nk, 0:N], func=AF.Exp),
                         [twf[fi]], [twb[pi]])
                for fn in pend.pop(k, []):
                    fn()
                i = k - 1
                if 0 <= i < nb:
                    b = B[i]
                    nk, N, pi = b["nk"], b["N"], i % 4
                    po, pr = b["banks"]
                    S.op("pe", lambda g: g.matmul(psf[:, po, 0:N], b["vap"], wb[pi][0:nk, 0:N], start=b["first"],
                                                  stop=b["last"]),
                         [twb[pi], t_vh, t_vs, t_vc], [t_psf[po]])
                    S.op("pe", lambda g: g.matmul(psf[:, pr, 0:N], ones[0:nk, :], wb[pi][0:nk, 0:N], start=b["first"],
                                                  stop=b["last"]),
                         [twb[pi], t_cst], [t_psf[pr]])
                    if b["last"]:
                        for si, fn in enumerate(b["fin"]):
                            pend.setdefault(k + 2 + si, []).append(fn)
                k += 1

        def sm_group(qap, N, blocks, dsl, banks, fin):
            out = []
            nb = len(blocks)
            for bi, (kap, vap, nk, bap) in enumerate(blocks):
                out.append(dict(kap=kap[dsl, :], qap=qap[dsl, :], vap=vap, nk=nk, N=N, bap=bap,
                                first=(bi == 0), last=(bi == nb - 1), banks=banks, fin=fin))
            return out

        def stages_A(banks, N, dst_tok0, h):
            st = {}

            def s0():
                st["oi"] = fin_sm(banks[0], banks[1], N)

            def s1():
                oi = st["oi"]
                bi = (rr.__setitem__("obb", (rr.get("obb", 0) + 1) % 2) or rr["obb"])
                S.op("act", lambda g: g.activation(out=ob[bi][:, 0:N], in_=osum[oi][:, 0:N], func=AF.Identity),
                     [t_o[oi]], [t_ob[bi]])
                S.dma("sp", OT[h, :, dst_tok0:dst_tok0 + N], ob[bi][:, 0:N], t_ob[bi], False)
            return [s0, s1]

        def stages_B0(st, N):
            def s0():
                st[0] = fin_sm(2, 3, N)
            return [s0]

        def stages_B1(st, N, dst_tok0, h):
            def s0():
                st[1] = fin_sm(4, 5, N)

            def s1():
                o0, o1 = st[0], st[1]
                S.op("dve", lambda g: g.scalar_tensor_tensor(out=osum[o0][:, 0:N], in0=osum[o1][:, 0:N], scalar=lam[:, 2:3],
                                                             in1=osum[o0][:, 0:N], op0=ALU.mult, op1=ALU.add),
                     [t_o[o0], t_o[o1], t_lam], [t_o[o0]])
                S.op("dve", lambda g: g.tensor_tensor(out=osum[o1][:, 0:N], in0=osum[o0][:, 0:N], in1=osum[o0][:, 0:N],
                                                      op=ALU.mult), [t_o[o0]], [t_o[o1]])
                S.op("pe", lambda g: g.matmul(psf[:, 5, 0:N], onesf[:, :], osum[o1][:, 0:N], start=True, stop=True),
                     [t_o[o1], t_cst], [t_psf[5]])

            def s2():
                o0, o1 = st[0], st[1]
                S.op("dve", lambda g: g.tensor_scalar(out=osum[o1][:, 0:N], in0=psf[:, 5, 0:N], scalar1=1.0 / HD,
                                                      scalar2=RMS_EPS, op0=ALU.mult, op1=ALU.add), [t_psf[5]], [t_o[o1]])
                S.op("act", lambda g: g.activation(out=osum[o1][:, 0:N], in_=osum[o1][:, 0:N], func=AF.Sqrt),
                     [t_o[o1]], [t_o[o1]])

            def s3():
                o0, o1 = st[0], st[1]
                S.op("dve", lambda g: g.reciprocal(out=osum[o1][:, 0:N], in_=osum[o1][:, 0:N]), [t_o[o1]], [t_o[o1]])
                S.op("dve", lambda g: g.tensor_tensor(out=osum[o0][:, 0:N], in0=osum[o0][:, 0:N], in1=osum[o1][:, 0:N],
                                                      op=ALU.mult), [t_o[o0], t_o[o1]], [t_o[o0]])
                bi = (rr.__setitem__("obb", (rr.get("obb", 0) + 1) % 2) or rr["obb"])
                S.op("dve", lambda g: g.tensor_scalar(out=ob[bi][:, 0:N], in0=osum[o0][:, 0:N], scalar1=gcol[:, 0:1],
                                                      scalar2=None, op0=ALU.mult), [t_o[o0], t_lam], [t_ob[bi]])
                S.dma("sp", OT[h, :, dst_tok0:dst_tok0 + N], ob[bi][:, 0:N], t_ob[bi], False)
            return [s0, s1, s2, s3]

        def run_stick(B):
            nb = len(B)
            po, pu, pc = 3, 4, 5
            for k in range(-3, nb + 1):
                i = k + 3
                if 0 <= i < nb:
                    b = B[i]
                    nk, N, pss = b["nk"], b["N"], i % 3
                    S.op("pe", lambda g: g.matmul(psf[0:nk, pss, 0:N], b["kap"], b["qap"], start=True, stop=True),
                         [t_kT, t_qT, t_kcT], [t_psf[pss]])
                i = k + 2
                if 0 <= i < nb:
                    b = B[i]
                    nk, N, pss, fs = b["nk"], b["N"], i % 3, i % 4
                    S.op("act", lambda g: g.activation(out=FS[fs][0:nk, 0:N], in_=psf[0:nk, pss, 0:N], func=AF.Exp,
                                                       scale=-scale), [t_psf[pss]], [tFS[fs]])
                    S.op("act", lambda g: g.activation(out=FS[fs][0:nk, 0:N], in_=FS[fs][0:nk, 0:N], func=AF.Ln, bias=1.0),
                         [tFS[fs]], [tFS[fs]])
                i = k + 1
                if 0 <= i < nb:
                    b = B[i]
                    nk, N, pss, fs, fl, hl = b["nk"], b["N"], i % 3, i % 4, i % 2, i % 3
                    S.op("dve", lambda g: g.scalar_tensor_tensor(out=FL[fl][0:nk, 0:N], in0=psf[0:nk, pss, 0:N],
                                                                 scalar=-scale, in1=FS[fs][0:nk, 0:N], op0=ALU.mult,
                                                                 op1=ALU.subtract),
                         [t_psf[pss], tFS[fs]], [tFL[fl]])
                    if b["m01"] is not None:
                        S.op("dve", lambda g: g.tensor_tensor(out=FL[fl][0:nk, 0:N], in0=FL[fl][0:nk, 0:N], in1=b["m01"],
                                                              op=ALU.mult), [tFL[fl], t_bias, t_biass], [tFL[fl]])
                    S.op("act", lambda g: g.activation(out=HI[hl][0:nk, 0:N], in_=FL[fl][0:nk, 0:N], func=AF.Identity),
                         [tFL[fl]], [tHI[hl]])
                    S.op("pool", lambda g: g.tensor_tensor(out=LO[hl][0:nk, 0:N], in0=FL[fl][0:nk, 0:N],
                                                           in1=HI[hl][0:nk, 0:N], op=ALU.subtract),
                         [tFL[fl], tHI[hl]], [tLO[hl]])
                i = k - 1
                if 0 <= i < nb:
                    b = B[i]
                    nk, N, fs = b["nk"], b["N"], i % 4
                    S.op("dve", lambda g: g.tensor_tensor(out=FS[fs][0:nk, 0:N], in0=psf[0:nk, pu, 0:N],
                                                          in1=FS[fs][0:nk, 0:N], op=ALU.subtract),
                         [t_psf[pu], tFS[fs]], [tFS[fs]])
                    if not b["first"]:
                        S.op("dve", lambda g: g.tensor_tensor(out=FS[fs][0:nk, 0:N], in0=FS[fs][0:nk, 0:N],
                                                              in1=psf[0:nk, pc, 0:N], op=ALU.add),
                             [tFS[fs], t_psf[pc]], [tFS[fs]])
                    if b["mneg"] is not None:
                        S.op("dve", lambda g: g.tensor_tensor(out=FS[fs][0:nk, 0:N], in0=FS[fs][0:nk, 0:N], in1=b["mneg"],
                                                              op=ALU.add), [tFS[fs], t_bias, t_biass], [tFS[fs]])
                i = k
                if 0 <= i < nb:
                    b = B[i]
                    nk, N, hl = b["nk"], b["N"], i % 3
                    S.op("pe", lambda g: g.matmul(psf[0:nk, pu, 0:N], umat[0:nk, 0:nk], HI[hl][0:nk, 0:N], start=True,
                                                  stop=False), [tHI[hl], t_cst], [t_psf[pu]])
                    S.op("pe", lambda g: g.matmul(psf[0:nk, pu, 0:N], umat[0:nk, 0:nk], LO[hl][0:nk, 0:N], start=False,
                                                  stop=True), [tLO[hl], t_cst], [t_psf[pu]])
                i = k - 1
                if 0 <= i < nb:
                    b = B[i]
                    nk, N, fs, hl, ai = b["nk"], b["N"], i % 4, i % 3, i % 2
                    S.op("act", lambda g: g.activation(out=AI[ai][0:nk, 0:N], in_=FS[fs][0:nk, 0:N], func=AF.Exp),
                         [tFS[fs]], [tAI[ai]])
                    if not b["last"]:
                        S.op("pe", lambda g: g.matmul(psf[:, pc, 0:N], ones[0:nk, :], HI[hl][0:nk, 0:N], start=b["first"],
                                                      stop=False), [tHI[hl], t_cst], [t_psf[pc]])
                        S.op("pe", lambda g: g.matmul(psf[:, pc, 0:N], ones[0:nk, :], LO[hl][0:nk, 0:N], start=False,
                                                      stop=True), [tLO[hl], t_cst], [t_psf[pc]])
                    S.op("pe", lambda g: g.matmul(psf[:, po, 0:N], b["vap"], AI[ai][0:nk, 0:N], start=b["first"],
                                                  stop=b["last"]),
                         [tAI[ai], t_vh, t_vs, t_vc], [t_psf[po]])
                    if b["last"]:
                        b["fin"]()

        def st_group(qap, N, blocks, fin):
            out = []
            nb = len(blocks)
            for bi, (kap, vap, nk, m01, mneg) in enumerate(reversed(blocks)):
                out.append(dict(kap=kap, qap=qap, vap=vap, nk=nk, N=N, m01=m01, mneg=mneg,
                                first=(bi == 0), last=(bi == nb - 1), fin=fin))
            return out

        def fin_stick(N, dst_tok0, h):
            bi2 = (rr.__setitem__("obb", (rr.get("obb", 0) + 1) % 2) or rr["obb"])
            S.op("act", lambda g: g.activation(out=ob[bi2][:, 0:N], in_=psf[:, 3, 0:N], func=AF.Identity),
                 [t_psf[3]], [t_ob[bi2]])
            S.dma("sp", OT[h, :, dst_tok0:dst_tok0 + N], ob[bi2][:, 0:N], t_ob[bi2], False)

        if kind == 2:
            S.dma("sp", bias[:, :, :], cmask_d[:, :, :], t_bias, True)
            S.dma("sp", biass[0:32, 0:2, :], cmasks_d[:, :, :], t_biass, True)

        for h in range(NH):
            hs = slice(h * 128, (h + 1) * 128)
            S.dma("sp", kT[:, :], KT[h, :, :], t_kT, True)
            S.dma("sp", qT[:, :], QT[h, :, :], t_qT, True)
            for vv in range(4):
                S.dma("sp", vh[:, vv * 8:(vv + 1) * 8, :],
                      VB[vv * 1024:(vv + 1) * 1024, hs].rearrange("(b p) d -> p b d", p=128), t_vh, True)
            S.dma("sp", vs[:, :, :], VB[SEQ:NTOK, hs].rearrange("(s p) d -> p s d", p=32), t_vs, True)
            if kind == 0:
                S.dma("sp", bias[:, :, :], biasA_d[j][h], t_bias, True)
                S.dma("sp", biass[:, 0:5, :], biasAs_d[j][h], t_biass, True)
            elif kind == 1:
                S.dma("sp", bias[:, 0:6, :], biasB_d[h], t_bias, True)
                S.dma("sp", biass[:, :, :], biasBs_d[h], t_biass, True)
            PB = []
            for t in range(SEQ // 512):
                qap = qT[:, t * 512:(t + 1) * 512]
                if kind == 0:
                    blocks = []
                    for kb in range(max(0, 4 * t - 4), 4 * t + 4):
                        blocks.append((kT[:, kb * 128:(kb + 1) * 128], vh[:, kb, :], 128, bias[:, kb - (4 * t - 4), :]))
                    banks = (2, 3) if t % 2 == 0 else (4, 5)
                    PB += sm_group(qap, 512, blocks, slice(0, 128), banks, stages_A(banks, 512, t * 512, h))
                elif kind == 1:
                    ois = {}
                    for half in range(2):
                        blocks = []
                        for kb in range(0, 4 * t + 4):
                            dl = 128 * kb - 512 * t
                            bidx = 0 if dl <= -256 else 1 + (dl + 128) // 128
                            blocks.append((kT[:, kb * 128:(kb + 1) * 128], vh[:, kb, :], 128, bias[:, bidx, :]))
                        banks = (2, 3) if half == 0 else (4, 5)
                        fin = stages_B0(ois, 512) if half == 0 else stages_B1(ois, 512, t * 512, h)
                        PB += sm_group(qap, 512, blocks, slice(64 * half, 64 * half + 64), banks, fin)
                else:
                    blocks = []
                    for kb in range(0, 4 * t + 4):
                        dl = 128 * kb - 512 * t
                        if dl >= 0:
                            blocks.append((kT[:, kb * 128:(kb + 1) * 128], vh[:, kb, :], 128,
                                           bias[:, dl // 128, :], bias[:, 4 + dl // 128, :]))
                        else:
                            blocks.append((kT[:, kb * 128:(kb + 1) * 128], vh[:, kb, :], 128, None, None))
                    PB += st_group(qap, 512, blocks, lambda t=t: fin_stick(512, t * 512, h))
            if kind == 2:
                run_stick(PB)
            else:
                run_softmax(PB)
            for sbi in range(2):
                rows = NCB * 128
                for cc2 in range(NCB // 2):
                    S.dma("pool", kcr[:, 2 * cc2:2 * cc2 + 2, :],
                          ck[j, sbi, 256 * cc2:256 * cc2 + 256, hs].rearrange("(b p) d -> p b d", p=128), t_kcr, True)
                    S.dma("pool", vc[:, 2 * cc2:2 * cc2 + 2, :],
                          cv[j, sbi, 256 * cc2:256 * cc2 + 256, hs].rearrange("(b p) d -> p b d", p=128), t_vc, True)
                for b4 in range(NCB // 4):
                    pb = nxt("pst", 2)
                    for jj in range(4):
                        b = b4 * 4 + jj
                        S.op("pe", lambda g, b=b, jj=jj, pb=pb: g.transpose(out=pst[:, pb, jj * 128:(jj + 1) * 128],
                                                                             in_=kcr[:, b, :], identity=ident[:, :]),
                             [t_kcr, t_ident], [t_pst[pb]])
                    copy_op(evac_engine(), kcT[:, b4 * 512:(b4 + 1) * 512], pst[:, pb, 0:512], [t_pst[pb]], [t_kcT])
                q0 = SEQ + 32 * sbi
                qap = qT[:, q0:q0 + 32]
                knew = kT[:, q0:q0 + 32]
                vnew = vs[:, sbi, :]
                if kind == 0 or kind == 1:
                    SB = []
                    ois = {}
                    for half in range(1 if kind == 0 else 2):
                        blocks = [(kcT[:, b * 128:(b + 1) * 128], vc[:, b, :], 128, biass[:, b, :]) for b in range(NCB)]
                        blocks.append((knew, vnew, 32, biass[0:32, NCB, :]))
                        dsl = slice(0, 128) if kind == 0 else slice(64 * half, 64 * half + 64)
                        banks = (2, 3) if half == 0 else (4, 5)
                        if kind == 0:
                            fin = stages_A((2, 3), 32, q0, h)
                        elif half == 0:
                            fin = stages_B0(ois, 32)
                        else:
                            fin = stages_B1(ois, 32, q0, h)
                        SB += sm_group(qap, 32, blocks, dsl, banks, fin)
                    run_softmax(SB)
                else:
                    blocks = [(kcT[:, b * 128:(b + 1) * 128], vc[:, b, :], 128, None, None) for b in range(NCB)]
                    blocks.append((knew, vnew, 32, biass[0:32, 0, :], biass[0:32, 1, :]))
                    run_stick(st_group(qap, 32, blocks, lambda q0=q0: fin_stick(32, q0, h)))

    if 0 in PHASES:
        phase0()
    S.barrier()
    for li in LAYER_SEL:
        if 1 in PHASES:
            phase1(li)
        S.barrier()
        if 2 in PHASES:
            with contextlib.ExitStack() as st2:
                phase2(li, st2)
                S.barrier()
        S.barrier()
        if 3 in PHASES:
            with contextlib.ExitStack() as st3:
                H["hT"] = st3.enter_context(nc.sbuf_tensor("s_hT_%d" % li, [128, NFC, 512], BF16))
                H["gb"] = st3.enter_context(nc.sbuf_tensor("s_gb_%d" % li, [128, 2, D], F32))
                phase3(li, li == LAYER_SEL[-1])
                S.barrier()
        S.barrier()
    S.final_wait()


def _t5_bucket(rel):
    half, max_exact = 16, 8
    n = np.abs(rel)
    nf = np.maximum(n, 1).astype(np.float32)
    large = max_exact + (np.log(nf / np.float32(max_exact)) / np.float32(math.log(128 / max_exact))
                         * np.float32(half - max_exact)).astype(np.int32)
    return np.where(rel > 0, half, 0) + np.where(n < max_exact, n, np.minimum(large, half - 1))


def _attention_constants(rel_bias_a, t5):
    p = np.arange(128)[:, None]
    out = {}
    neg_row = np.full((1, NH), NEG, np.float32)
    f = np.arange(512)[None, :]
    idxA = np.empty((8, 128, 512), np.int64)
    for i in range(8):
        ko = -512 + 128 * i + p
        rel = ko - f
        kc = np.floor_divide(ko, 64)
        qc = f // 64
        valid = (kc <= qc) & (kc >= qc - 8)
        idxA[i] = np.where(valid, np.clip(rel, -256, 256) + 256, 513)
    fs = np.arange(32)[None, :]
    idxAs = np.empty((5, 128, 32), np.int64)
    for b in range(5):
        kpos = (512 + 128 * b + p) if b < 4 else (1024 + p)
        idxAs[b] = np.clip(kpos - (1024 + fs), -256, 256) + 256
    bA = np.empty((2, NH, 128, 8, 512), np.float32)
    bAs = np.empty((2, NH, 128, 5, 32), np.float32)
    for j in range(2):
        tab = np.concatenate([rel_bias_a[j], neg_row], axis=0)
        bA[j] = np.transpose(tab[idxA], (3, 1, 0, 2))
        bAs[j] = np.transpose(tab[idxAs], (3, 1, 0, 2))
    out["biasA"] = bA
    out["biasAs"] = bAs
    tabB = np.concatenate([t5, neg_row], axis=0)
    idxB = np.empty((6, 128, 512), np.int64)
    idxB[0] = 15
    for i in range(1, 6):
        dl = -128 + 128 * (i - 1)
        ko = dl + p
        rel = ko - f
        valid = np.floor_divide(ko, 64) <= (f // 64)
        idxB[i] = np.where(valid, _t5_bucket(rel), 32)
    out["biasB"] = np.ascontiguousarray(np.transpose(tabB[idxB], (3, 1, 0, 2)))
    idxBs = np.empty((9, 128, 32), np.int64)
    for b in range(9):
        kpos = (128 * b + p) if b < 8 else (1024 + p)
        idxBs[b] = _t5_bucket(kpos - (1024 + fs))
    out["biasBs"] = np.ascontiguousarray(np.transpose(tabB[idxBs], (3, 1, 0, 2)))
    cm = np.empty((128, 8, 512), np.float32)
    for i in range(4):
        v = ((128 * i + p) < f).astype(np.float32)
        cm[:, i, :] = v
        cm[:, 4 + i, :] = (1.0 - v) * NEG
    out["cmask"] = cm
    ps = np.arange(32)[:, None]
    vs_ = (ps < fs).astype(np.float32)
    out["cmasks"] = np.ascontiguousarray(np.stack([vs_, (1.0 - vs_) * NEG], axis=1))
    out["umat"] = (np.arange(128)[:, None] > np.arange(128)[None, :]).astype(np.float32)
    out["ones"] = np.ones((128, 128), np.float32)
    return out

_NC_CACHE = {}


def _get_nc():
    if "nc" not in _NC_CACHE:
        _NC_CACHE["nc"] = build_program()
    return _NC_CACHE["nc"]


def kernel(x_prompt, x_sample, cache_a_k, cache_a_v, cache_b_k, cache_b_v, cache_c_k, cache_c_v,
           w_in_a, w_out_a, rel_bias_a, w_in_b, w_out_b, lambda_q1, lambda_k1, lambda_q2, lambda_k2,
           diff_norm_g, t5_bias, w_in_c, w_out_c, ln1_g, ln1_b, ln2_g, ln2_b, w_gate, w_up, w_down):
    f = lambda a: np.ascontiguousarray(np.asarray(a, dtype=np.float32))
    nc = _get_nc()
    wi_l = [w_in_a[0], w_in_b[0], w_in_c[0], w_in_a[1]]
    wo_l = [w_out_a[0], w_out_b[0], w_out_c[0], w_out_a[1]]
    shared = {
        "ln1_g": f(ln1_g), "ln1_b": f(ln1_b), "ln2_g": f(ln2_g), "ln2_b": f(ln2_b),
        "ident": np.eye(128, dtype=np.float32),
    }
    for l in LAYER_SEL:
        shared["w_in_L%d" % l] = f(wi_l[l])
        shared["w_out_L%d" % l] = f(wo_l[l])
        shared["w_gate_L%d" % l] = f(w_gate[l])
        shared["w_up_L%d" % l] = f(w_up[l])
        shared["w_down_L%d" % l] = f(w_down[l])
    shared.update(_attention_constants(np.asarray(rel_bias_a, dtype=np.float32), np.asarray(t5_bias, dtype=np.float32)))
    shared["lam"] = f(np.stack([np.asarray(lambda_q1)[0], np.asarray(lambda_k1)[0],
                                np.asarray(lambda_q2)[0], np.asarray(lambda_k2)[0]]))
    shared["dng"] = f(np.asarray(diff_norm_g)[0].reshape(128, 1))
    x_prompt = np.asarray(x_prompt)
    x_sample = np.asarray(x_sample)
    in_maps = []
    for c in range(N_CORES):
        m = dict(shared)
        m["xp"] = f(x_prompt[c])
        m["xs"] = f(x_sample[2 * c:2 * c + 2].reshape(NS, D))
        m["ca_k"] = f(np.asarray(cache_a_k)[:, 2 * c:2 * c + 2].reshape(2, 2, 512, D))
        m["ca_v"] = f(np.asarray(cache_a_v)[:, 2 * c:2 * c + 2].reshape(2, 2, 512, D))
        m["cb_k"] = f(np.asarray(cache_b_k)[:, 2 * c:2 * c + 2].reshape(1, 2, PAST, D))
        m["cb_v"] = f(np.asarray(cache_b_v)[:, 2 * c:2 * c + 2].reshape(1, 2, PAST, D))
        m["cc_k"] = f(np.asarray(cache_c_k)[:, 2 * c:2 * c + 2].reshape(1, 2, PAST, D))
        m["cc_v"] = f(np.asarray(cache_c_v)[:, 2 * c:2 * c + 2].reshape(1, 2, PAST, D))
        in_maps.append(m)
    res = run_bass_kernel_spmd(nc, in_maps, core_ids=list(range(N_CORES)))
    R = list(res.results)
    while len(R) < 8:
        R.append(R[0])

    def cat_prompt(name, lead):
        arr = np.stack([R[c][name] for c in range(8)], axis=1)
        return arr.reshape(arr.shape[0], 8, arr.shape[2], NH, HD)

    def cat_sample(name):
        arr = np.stack([R[c][name].reshape(-1, 2, 32, D) for c in range(8)], axis=1)
        return arr.reshape(arr.shape[0], 16, 32, NH, HD)

    y_p = np.stack([R[c]["yp"] for c in range(8)], axis=0)
    y_s = np.concatenate([R[c]["ys"].reshape(2, 32, D) for c in range(8)], axis=0)
    outs = (y_p, y_s,
            cat_prompt("pa_k", 2), cat_prompt("pa_v", 2),
            cat_prompt("pb_k", 1), cat_prompt("pb_v", 1),
            cat_prompt("pc_k", 1), cat_prompt("pc_v", 1),
            cat_sample("sa_k"), cat_sample("sa_v"),
            cat_sample("sb_k"), cat_sample("sb_v"),
            cat_sample("sc_k"), cat_sample("sc_v"))
    return tuple(np.ascontiguousarray(o, dtype=np.float32) for o in outs)
```
